# Optimizing a Trainium2 kernel written in Bass

```python
import jax, jax.numpy as jnp
from jax import lax
import numpy as np

D_MODEL = 1024
BATCH = 8
SEQ = 2048
DEPTH = 2
DEC_BATCH = 128
DEC_SEQ = 8
PAST_LEN = 16384
PAGE_SIZE = 128

RNN_WIDTH = D_MODEL
RNN_HEADS = 16
RNN_HEAD_DIM = RNN_WIDTH // RNN_HEADS
RNN_CONV = 4
RG_C = 8.0
S5_WIDTH = D_MODEL // 2
S5_GROUP = 16
S5_GROUPS = S5_WIDTH // S5_GROUP
S5_STATE = 64
D_FF = 3 * D_MODEL
FFN_CONV = 3
IN_WIDTH = 2 * RNN_WIDTH + S5_WIDTH + 2 * D_MODEL
SPLITS = [RNN_WIDTH, 2 * RNN_WIDTH, 2 * RNN_WIDTH + S5_WIDTH, 2 * RNN_WIDTH + S5_WIDTH + D_MODEL]
EPS = 1e-6
F32 = jnp.float32

kernel_name = "hawk_s5_gated_hybrid_step"


def rmsnorm(x, g):
    xf = x.astype(F32)
    y = xf * lax.rsqrt(jnp.mean(xf * xf, axis=-1, keepdims=True) + EPS)
    return (y * g.astype(F32)).astype(x.dtype)


def modulate(x, g, shift, scale):
    return rmsnorm(x, g) * (1 + scale[:, None]) + shift[:, None]


def causal_dwconv(buf, x, w, b):
    k = w.shape[0]
    t = x.shape[1]
    xp = jnp.concatenate([buf.astype(x.dtype), x], axis=1)
    y = b
    for j in range(k):
        y = y + xp[:, j:j + t] * w[j]
    return y, xp[:, xp.shape[1] - (k - 1):]


def _lin_combine(e1, e2):
    a1, b1 = e1
    a2, b2 = e2
    return a1 * a2, a2 * b1 + b2


def _cplx_combine(e1, e2):
    ar1, ai1, br1, bi1 = e1
    ar2, ai2, br2, bi2 = e2
    return (ar1 * ar2 - ai1 * ai2, ar1 * ai2 + ai1 * ar2,
            ar2 * br1 - ai2 * bi1 + br2, ar2 * bi1 + ai2 * br1 + bi2)


def rglru_branch(rx, ry, conv_buf, h0, conv_w, conv_b, wa, ba, wx, bx, lam):
    n, t, _ = rx.shape
    xc, new_buf = causal_dwconv(conv_buf, rx, conv_w, conv_b)
    xh = xc.reshape(n, t, RNN_HEADS, RNN_HEAD_DIM)
    r = jax.nn.sigmoid((jnp.einsum('nthi,hij->nthj', xh, wa).reshape(n, t, RNN_WIDTH) + ba).astype(F32))
    i = jax.nn.sigmoid((jnp.einsum('nthi,hij->nthj', xh, wx).reshape(n, t, RNN_WIDTH) + bx).astype(F32))
    log_a = RG_C * r * jax.nn.log_sigmoid(lam.astype(F32))
    a = jnp.exp(log_a)
    b = jnp.sqrt(-jnp.expm1(2.0 * log_a)) * i * xc.astype(F32)
    b = b.at[:, 0].add(a[:, 0] * h0.astype(F32))
    _, h = lax.associative_scan(_lin_combine, (a, b), axis=1)
    out = h.astype(rx.dtype) * jax.nn.gelu(ry)
    return out, h[:, -1], new_buf


def s5_branch(sx, re0, im0, lam_re, lam_im, b_re, b_im, c_re, c_im, d, log_dt):
    n, t, _ = sx.shape
    dt = jnp.exp(log_dt.astype(F32))[:, None]
    lr = lam_re.astype(F32)
    li = lam_im.astype(F32)
    mag = jnp.exp(lr * dt)
    ar = mag * jnp.cos(li * dt)
    ai = mag * jnp.sin(li * dt)
    den = lr * lr + li * li
    zr = ((ar - 1.0) * lr + ai * li) / den
    zi = (ai * lr - (ar - 1.0) * li) / den
    br = b_re.astype(F32)
    bi = b_im.astype(F32)
    bbr = zr[..., None] * br - zi[..., None] * bi
    bbi = zr[..., None] * bi + zi[..., None] * br
    u = sx.astype(F32).reshape(n, t, S5_GROUPS, S5_GROUP)
    bur = jnp.einsum('ntgc,gpc->ntgp', u, bbr)
    bui = jnp.einsum('ntgc,gpc->ntgp', u, bbi)
    re0 = re0.astype(F32)
    im0 = im0.astype(F32)
    bur = bur.at[:, 0].add(ar * re0 - ai * im0)
    bui = bui.at[:, 0].add(ar * im0 + ai * re0)
    arf = jnp.broadcast_to(ar, bur.shape)
    aif = jnp.broadcast_to(ai, bur.shape)
    _, _, sr, si = lax.associative_scan(_cplx_combine, (arf, aif, bur, bui), axis=1)
    y = (jnp.einsum('ntgp,gcp->ntgc', sr, c_re.astype(F32))
         - jnp.einsum('ntgp,gcp->ntgc', si, c_im.astype(F32)))
    y = y.reshape(n, t, S5_WIDTH) + d.astype(F32) * sx.astype(F32)
    return jax.nn.gelu(y).astype(sx.dtype), sr[:, -1], si[:, -1]


def layer(x, c, rg_h, rg_conv, s5_re, s5_im, ffn_conv, p):
    mod = jax.nn.silu(c) @ p['w_ada'] + p['b_ada']
    sh1, sc1, gt1, sh2, sc2, gt2 = jnp.split(mod, 6, axis=-1)
    h = modulate(x, p['g_norm1'], sh1, sc1)
    u = h @ p['w_in'] + p['b_in']
    rx, ry, sx, ga, gb = jnp.split(u, SPLITS, axis=-1)
    ra, new_rg_h, new_rg_conv = rglru_branch(rx, ry, rg_conv, rg_h, p['rg_conv_w'], p['rg_conv_b'],
                                             p['rg_wa'], p['rg_ba'], p['rg_wx'], p['rg_bx'], p['rg_lambda'])
    sb, new_s5_re, new_s5_im = s5_branch(sx, s5_re, s5_im, p['s5_lam_re'], p['s5_lam_im'], p['s5_b_re'],
                                         p['s5_b_im'], p['s5_c_re'], p['s5_c_im'], p['s5_d'], p['s5_log_dt'])
    branch_a = ra @ p['w_rg_proj']
    glu_a, glu_b = jnp.split(sb @ p['w_glu'], 2, axis=-1)
    branch_b = glu_a * jax.nn.sigmoid(glu_b)
    merged = jax.nn.sigmoid(ga) * branch_a + jax.nn.sigmoid(gb) * branch_b
    x = x + gt1[:, None] * (merged @ p['w_out'])
    h = modulate(x, p['g_norm2'], sh2, sc2)
    up = h @ p['w_up']
    uc, new_ffn_conv = causal_dwconv(ffn_conv, up, p['ffn_conv_w'], p['ffn_conv_b'])
    ua, ub = jnp.split(uc, 2, axis=-1)
    x = x + gt2[:, None] * ((jax.nn.gelu(ua) * ub) @ p['w_down'])
    return x, (new_rg_h, new_rg_conv, new_s5_re, new_s5_im, new_ffn_conv)


def trunk(x, c, rg_h, rg_conv, s5_re, s5_im, ffn_conv, params, g_final):
    outs = [[], [], [], [], []]
    for l in range(DEPTH):
        p = {k: v[l] for k, v in params.items()}
        x, st = layer(x, c, rg_h[:, l], rg_conv[:, l], s5_re[:, l], s5_im[:, l], ffn_conv[:, l], p)
        for o, s in zip(outs, st):
            o.append(s)
    y = rmsnorm(x, g_final)
    return (y, jnp.stack(outs[0], axis=1), jnp.stack(outs[1], axis=1), jnp.stack(outs[2], axis=1),
            jnp.stack(outs[3], axis=1), jnp.stack(outs[4], axis=1))


def setup_inputs(seed: int = 0) -> dict:
    key = jax.random.key(seed)
    ks = iter(jax.random.split(key, 48))

    def nrm(shape, s):
        return s * jax.random.normal(next(ks), shape, F32)

    L, D = DEPTH, D_MODEL
    u = jax.random.uniform(next(ks), (L, RNN_WIDTH), F32, 0.9, 0.999)
    s = u ** (1.0 / RG_C)
    rg_lambda = jnp.log(s) - jnp.log1p(-s)
    lam_im = jnp.pi * jnp.broadcast_to(jnp.arange(S5_STATE, dtype=F32), (L, S5_GROUPS, S5_STATE))
    return {
        'x_prompt': nrm((BATCH, SEQ, D), 1.0),
        'x_sample': nrm((DEC_BATCH, DEC_SEQ, D), 1.0),
        'state_rg_h': nrm((DEC_BATCH, L, RNN_WIDTH), 0.5),
        'state_rg_conv': nrm((DEC_BATCH, L, RNN_CONV - 1, RNN_WIDTH), 1.0),
        'state_s5_re': nrm((DEC_BATCH, L, S5_GROUPS, S5_STATE), 0.5),
        'state_s5_im': nrm((DEC_BATCH, L, S5_GROUPS, S5_STATE), 0.5),
        'state_ffn_conv': nrm((DEC_BATCH, L, FFN_CONV - 1, 2 * D_FF), 1.0),
        'c_prompt': nrm((BATCH, D), 1.0),
        'c_sample': nrm((DEC_BATCH, D), 1.0),
        'w_ada': nrm((L, D, 6 * D), 0.5 * D ** -0.5),
        'b_ada': nrm((L, 6 * D), 0.02),
        'g_norm1': 1.0 + nrm((L, D), 0.02),
        'g_norm2': 1.0 + nrm((L, D), 0.02),
        'w_in': nrm((L, D, IN_WIDTH), D ** -0.5),
        'b_in': nrm((L, IN_WIDTH), 0.02),
        'rg_conv_w': nrm((L, RNN_CONV, RNN_WIDTH), RNN_CONV ** -0.5),
        'rg_conv_b': nrm((L, RNN_WIDTH), 0.02),
        'rg_wa': nrm((L, RNN_HEADS, RNN_HEAD_DIM, RNN_HEAD_DIM), RNN_HEAD_DIM ** -0.5),
        'rg_ba': nrm((L, RNN_WIDTH), 0.02),
        'rg_wx': nrm((L, RNN_HEADS, RNN_HEAD_DIM, RNN_HEAD_DIM), RNN_HEAD_DIM ** -0.5),
        'rg_bx': nrm((L, RNN_WIDTH), 0.02),
        'rg_lambda': rg_lambda,
        's5_lam_re': -0.5 + nrm((L, S5_GROUPS, S5_STATE), 0.01),
        's5_lam_im': lam_im + nrm((L, S5_GROUPS, S5_STATE), 0.01),
        's5_b_re': nrm((L, S5_GROUPS, S5_STATE, S5_GROUP), (2.0 * S5_GROUP) ** -0.5),
        's5_b_im': nrm((L, S5_GROUPS, S5_STATE, S5_GROUP), (2.0 * S5_GROUP) ** -0.5),
        's5_c_re': nrm((L, S5_GROUPS, S5_GROUP, S5_STATE), (2.0 * S5_STATE) ** -0.5),
        's5_c_im': nrm((L, S5_GROUPS, S5_GROUP, S5_STATE), (2.0 * S5_STATE) ** -0.5),
        's5_d': nrm((L, S5_WIDTH), 0.5),
        's5_log_dt': jax.random.uniform(next(ks), (L, S5_GROUPS), F32, float(np.log(0.001)), float(np.log(0.1))),
        'w_rg_proj': nrm((L, RNN_WIDTH, D), RNN_WIDTH ** -0.5),
        'w_glu': nrm((L, S5_WIDTH, 2 * D), S5_WIDTH ** -0.5),
        'w_out': nrm((L, D, D), D ** -0.5),
        'w_up': nrm((L, D, 2 * D_FF), D ** -0.5),
        'ffn_conv_w': nrm((L, FFN_CONV, 2 * D_FF), FFN_CONV ** -0.5),
        'ffn_conv_b': nrm((L, 2 * D_FF), 0.02),
        'w_down': nrm((L, D_FF, D), D_FF ** -0.5),
        'g_final': 1.0 + nrm((D,), 0.02),
    }


def reference(x_prompt, x_sample, state_rg_h, state_rg_conv, state_s5_re, state_s5_im, state_ffn_conv,
              c_prompt, c_sample, w_ada, b_ada, g_norm1, g_norm2, w_in, b_in, rg_conv_w, rg_conv_b,
              rg_wa, rg_ba, rg_wx, rg_bx, rg_lambda, s5_lam_re, s5_lam_im, s5_b_re, s5_b_im,
              s5_c_re, s5_c_im, s5_d, s5_log_dt, w_rg_proj, w_glu, w_out, w_up, ffn_conv_w,
              ffn_conv_b, w_down, g_final):
    params = dict(w_ada=w_ada, b_ada=b_ada, g_norm1=g_norm1, g_norm2=g_norm2, w_in=w_in, b_in=b_in,
                  rg_conv_w=rg_conv_w, rg_conv_b=rg_conv_b, rg_wa=rg_wa, rg_ba=rg_ba, rg_wx=rg_wx,
                  rg_bx=rg_bx, rg_lambda=rg_lambda, s5_lam_re=s5_lam_re, s5_lam_im=s5_lam_im,
                  s5_b_re=s5_b_re, s5_b_im=s5_b_im, s5_c_re=s5_c_re, s5_c_im=s5_c_im, s5_d=s5_d,
                  s5_log_dt=s5_log_dt, w_rg_proj=w_rg_proj, w_glu=w_glu, w_out=w_out, w_up=w_up,
                  ffn_conv_w=ffn_conv_w, ffn_conv_b=ffn_conv_b, w_down=w_down)
    nb = x_prompt.shape[0]
    z_rg_h = jnp.zeros((nb, DEPTH, RNN_WIDTH), F32)
    z_rg_conv = jnp.zeros((nb, DEPTH, RNN_CONV - 1, RNN_WIDTH), x_prompt.dtype)
    z_s5 = jnp.zeros((nb, DEPTH, S5_GROUPS, S5_STATE), F32)
    z_ffn_conv = jnp.zeros((nb, DEPTH, FFN_CONV - 1, 2 * D_FF), x_prompt.dtype)
    y_prompt, p_rg_h, p_rg_conv, p_s5_re, p_s5_im, p_ffn_conv = trunk(
        x_prompt, c_prompt, z_rg_h, z_rg_conv, z_s5, z_s5, z_ffn_conv, params, g_final)
    y_sample, s_rg_h, s_rg_conv, s_s5_re, s_s5_im, s_ffn_conv = trunk(
        x_sample, c_sample, state_rg_h, state_rg_conv, state_s5_re, state_s5_im, state_ffn_conv,
        params, g_final)
    return (y_prompt, y_sample, p_rg_h, p_rg_conv, p_s5_re, p_s5_im, p_ffn_conv,
            s_rg_h, s_rg_conv, s_s5_re, s_s5_im, s_ffn_conv)
```

```python
import contextlib
import math
import numpy as np
import concourse.bass as bass
import concourse.mybir as mybir
from concourse.bass_utils import run_bass_kernel_spmd

F32 = mybir.dt.float32
F32R = mybir.dt.float32r
AF = mybir.ActivationFunctionType
ALU = mybir.AluOpType

D = 1024
SEQ = 2048
NS = 16
ST = 8
NTOK = SEQ + NS * ST
L = 2
INW = 4608
DFF = 3072
LS = 64
ENGS = ("pe", "act", "dve", "pool", "sp")
N_DMA_SEMS = 24
SELF_SYNC = True
WB = 2048
NWBUF = 4
NTMP = 12
NTMPR = 8

CO = {}
_o = 0
for _n, _w in (("g1", 8), ("g2", 8), ("bin", 36), ("rcw", 32), ("rcb", 8), ("ba", 8), ("bx", 8), ("lam", 8),
               ("s5d", 4), ("fcw", 144), ("fcb", 48), ("bada", 48)):
    CO[_n] = _o
    _o += _w
NCL = _o
NCOL = 2 * NCL + 8


class Prog:
    def __init__(self, nc, stack):
        self.nc = nc
        self.stack = stack
        self.ops = {e: [] for e in ENGS}
        self.sem = {e: stack.enter_context(nc.semaphore("s_" + e)) for e in ENGS}
        self.cnt = {e: 0 for e in ENGS}
        self.dsem = [stack.enter_context(nc.semaphore("d%d" % i)) for i in range(3 * N_DMA_SEMS)]
        self.dcnt = [0] * (3 * N_DMA_SEMS)
        self.dnext = {"sp": 0, "pool": 0, "act": 0}
        self.seen = {e: {} for e in ENGS}
        self.tiles = {}
        self.nbuf = 0

    def sb(self, shape, dtype=F32, name=None):
        self.nbuf += 1
        return self.stack.enter_context(self.nc.sbuf_tensor("S_" + (name or ("t%d" % self.nbuf)), list(shape), dtype))

    def ps(self, shape, dtype=F32, name=None):
        self.nbuf += 1
        return self.stack.enter_context(self.nc.psum_tensor("P_" + (name or ("p%d" % self.nbuf)), list(shape), dtype))

    def _semobj(self, key):
        return self.sem[key] if isinstance(key, str) else self.dsem[key]

    def _deps(self, eng, reads, writes):
        need = {}

        def req(ev):
            if ev is None:
                return
            k, v = ev
            if k == eng and (eng == "pe" or not SELF_SYNC):
                return
            if need.get(k, 0) < v:
                need[k] = v

        for t in reads:
            st = self.tiles.get(t)
            if st:
                req(st["w"])
        for t in writes:
            st = self.tiles.get(t)
            if st:
                req(st["w"])
                for ev in st["r"]:
                    req(ev)
        waits = []
        for k, v in need.items():
            if self.seen[eng].get(k, 0) < v:
                self.seen[eng][k] = v
                waits.append((k, v))
        return waits

    def _commit(self, ev, reads, writes):
        for t in reads:
            st = self.tiles.setdefault(t, {"w": None, "r": []})
            st["r"].append(ev)
            if len(st["r"]) > 16:
                best = {}
                for k, v in st["r"]:
                    if best.get(k, 0) < v:
                        best[k] = v
                st["r"] = list(best.items())
        for t in writes:
            self.tiles[t] = {"w": ev, "r": []}

    def op(self, eng, fn, reads=(), writes=()):
        waits = self._deps(eng, reads, writes)
        self.cnt[eng] += 1
        ev = (eng, self.cnt[eng])
        sem = self.sem[eng]

        def emit(e, fn=fn, waits=waits, sem=sem):
            for k, v in waits:
                e.wait_ge(self._semobj(k), v)
            fn(e).then_inc(sem, 1)

        self.ops[eng].append(emit)
        self._commit(ev, reads, writes)

    def dma(self, out, in_, reads=(), writes=(), eng="sp"):
        k = self.dnext[eng] + {"sp": 0, "pool": N_DMA_SEMS, "act": 2 * N_DMA_SEMS}[eng]
        self.dnext[eng] = (self.dnext[eng] + 1) % N_DMA_SEMS
        waits = self._deps(eng, reads, writes)
        prev = self.dcnt[k]
        if prev and self.seen[eng].get(k, 0) < prev:
            self.seen[eng][k] = prev
            waits.append((k, prev))
        self.dcnt[k] += 16
        ev = (k, self.dcnt[k])
        dsem = self.dsem[k]

        def emit(e, waits=waits, out=out, in_=in_, dsem=dsem):
            for kk, v in waits:
                e.wait_ge(self._semobj(kk), v)
            e.dma_start(out=out, in_=in_).then_inc(dsem, 16)

        self.ops[eng].append(emit)
        self._commit(ev, reads, writes)

    def finish(self):
        fin = [(k, self.dcnt[k]) for k in range(3 * N_DMA_SEMS) if self.dcnt[k]]

        def emit_fin(e):
            for k, v in fin:
                e.wait_ge(self.dsem[k], v)

        self.ops["sp"].append(emit_fin)
        nc = self.nc
        with nc.Block() as block:
            @block.tensor
            def _(e):
                for f in self.ops["pe"]:
                    f(e)

            @block.scalar
            def _(e):
                for f in self.ops["act"]:
                    f(e)

            @block.vector
            def _(e):
                for f in self.ops["dve"]:
                    f(e)

            @block.gpsimd
            def _(e):
                for f in self.ops["pool"]:
                    f(e)

            @block.sync
            def _(e):
                for f in self.ops["sp"]:
                    f(e)


def stream_tiles():
    t = []
    for c in range(8):
        t.append(("rg%d" % c, 2048))
        t.append(("rgd%d" % c, 256))
    for q in range(4):
        t.append(("sx%d" % q, 1024))
        t.append(("s5c%d" % q, 1024))
    for m in range(8):
        t.append(("mga%d" % m, 2048))
        t.append(("mgb%d" % m, 2048))
    for o in range(4):
        t.append(("wo%d" % o, 2048))
    for j in range(24):
        t.append(("up%d" % j, 2048))
    for o in range(4):
        for i in range(3):
            t.append(("dn%d_%d" % (o, i), 2048))
    return t


STILES = stream_tiles()
SOFF = {}
_o = 0
for _n, _s in STILES:
    SOFF[_n] = (_o, _s)
    _o += _s
SLEN = _o
ADA_TILES = 24


def _pack(W, kc, cols, k0=0):
    cols = np.asarray(cols)
    blk = W[k0 * 128:(k0 + kc) * 128][:, cols].reshape(kc, 128, len(cols))
    return np.ascontiguousarray(blk.transpose(1, 0, 2)).reshape(128, kc * len(cols))


def _ar(a, n=128):
    return np.arange(a, a + n)


def host_stream(inp, l):
    w_in = inp["w_in"][l]
    out = np.zeros((128, SLEN), np.float32)

    def put(name, arr):
        o, s = SOFF[name]
        assert arr.shape == (128, s), (name, arr.shape, s)
        out[:, o:o + s] = arr

    g = np.zeros((128, 8, 256), np.float32)
    for c in range(8):
        for hh in range(2):
            h = 2 * c + hh
            g[hh * 64:(hh + 1) * 64, c, hh * 64:(hh + 1) * 64] = inp["rg_wa"][l, h]
            g[hh * 64:(hh + 1) * 64, c, 128 + hh * 64:128 + (hh + 1) * 64] = inp["rg_wx"][l, h]
    for c in range(8):
        put("rgd%d" % c, np.ascontiguousarray(g[:, c, :]))
    for c in range(8):
        put("rg%d" % c, _pack(w_in, 8, np.concatenate([_ar(c * 128), _ar(1024 + c * 128)])))
    cre = inp["s5_c_re"][l]
    cim = inp["s5_c_im"][l]
    for q in range(4):
        put("sx%d" % q, _pack(w_in, 8, _ar(2048 + q * 128)))
        cc = np.zeros((128, 2, 4, 128), np.float32)
        for pp in range(4):
            for j in range(2):
                gidx = 2 * (4 * q + pp) + j
                sl = 16 * (gidx % 8)
                cc[64 * j:64 * j + 64, 0, pp, sl:sl + 16] = cre[gidx].T
                cc[64 * j:64 * j + 64, 1, pp, sl:sl + 16] = cim[gidx].T
        put("s5c%d" % q, cc.reshape(128, 1024))
    for m in range(8):
        put("mga%d" % m, _pack(w_in, 8, np.concatenate([_ar(2560 + m * 128), _ar(3584 + m * 128)])))
        a = _pack(inp["w_rg_proj"][l], 8, _ar(m * 128))
        b = _pack(inp["w_glu"][l], 4, np.concatenate([_ar(m * 128), _ar(1024 + m * 128)]))
        put("mgb%d" % m, np.concatenate([a, b], axis=1))
    for o in range(4):
        put("wo%d" % o, _pack(inp["w_out"][l], 8, _ar(o * 256, 256)))
    for j in range(24):
        put("up%d" % j, _pack(inp["w_up"][l], 8, np.concatenate([_ar(j * 128), _ar(DFF + j * 128)])))
    for o in range(4):
        for i in range(3):
            put("dn%d_%d" % (o, i), _pack(inp["w_down"][l], 8, _ar(o * 256, 256), k0=8 * i))
    return out


def host_ada(inp):
    out = np.zeros((128, L * ADA_TILES * 2048), np.float32)
    for l in range(L):
        for t in range(ADA_TILES):
            out[:, (l * ADA_TILES + t) * 2048:(l * ADA_TILES + t + 1) * 2048] = _pack(inp["w_ada"][l], 8, _ar(t * 256, 256))
    return out


def host_cols(inp):
    c = np.zeros((128, NCOL), np.float32)

    def colv(v):
        return np.asarray(v).reshape(-1, 128).T

    for l in range(L):
        b = l * NCL
        c[:, b + CO["g1"]:b + CO["g1"] + 8] = colv(inp["g_norm1"][l])
        c[:, b + CO["g2"]:b + CO["g2"] + 8] = colv(inp["g_norm2"][l])
        c[:, b + CO["bin"]:b + CO["bin"] + 36] = colv(inp["b_in"][l])
        for j in range(4):
            c[:, b + CO["rcw"] + j * 8:b + CO["rcw"] + j * 8 + 8] = colv(inp["rg_conv_w"][l, j])
        c[:, b + CO["rcb"]:b + CO["rcb"] + 8] = colv(inp["rg_conv_b"][l])
        c[:, b + CO["ba"]:b + CO["ba"] + 8] = colv(inp["rg_ba"][l])
        c[:, b + CO["bx"]:b + CO["bx"] + 8] = colv(inp["rg_bx"][l])
        c[:, b + CO["lam"]:b + CO["lam"] + 8] = colv(inp["rg_lambda"][l])
        c[:, b + CO["s5d"]:b + CO["s5d"] + 4] = colv(inp["s5_d"][l])
        for j in range(3):
            c[:, b + CO["fcw"] + j * 48:b + CO["fcw"] + j * 48 + 48] = colv(inp["ffn_conv_w"][l, j])
        c[:, b + CO["fcb"]:b + CO["fcb"] + 48] = colv(inp["ffn_conv_b"][l])
        c[:, b + CO["bada"]:b + CO["bada"] + 48] = colv(inp["b_ada"][l])
    c[:, 2 * NCL:2 * NCL + 8] = colv(inp["g_final"])
    return c


def host_s5(inp):
    col = np.zeros((L, 128, 3, 16), np.float32)
    bp = np.zeros((L, 128, 2, 16, 128), np.float32)
    for l in range(L):
        for p in range(16):
            for j in range(2):
                g = 2 * p + j
                col[l, 64 * j:64 * j + 64, 0, p] = inp["s5_lam_re"][l, g]
                col[l, 64 * j:64 * j + 64, 1, p] = inp["s5_lam_im"][l, g]
                col[l, 64 * j:64 * j + 64, 2, p] = inp["s5_log_dt"][l, g]
                sl = 16 * (g % 8)
                bp[l, sl:sl + 16, 0, p, 64 * j:64 * j + 64] = inp["s5_b_re"][l, g].T
                bp[l, sl:sl + 16, 1, p, 64 * j:64 * j + 64] = inp["s5_b_im"][l, g].T
    return col, bp.reshape(L, 128, 2, 2048)


def build():
    nc = bass.Bass("TRN2", target_bir_lowering=False)
    nc.dge_precook = False

    def din(name, shape, dt=F32):
        return nc.dram_tensor(name, list(shape), dt, kind="ExternalInput").ap()

    def dout(name, shape, dt=F32):
        return nc.dram_tensor(name, list(shape), dt, kind="ExternalOutput").ap()

    xT = din("xT", [D, NTOK])
    cT = din("cT", [D, 18])
    cols_d = din("cols", [128, NCOL])
    st_rgh = din("st_rgh", [L, 128, 8, NS])
    st_rgc = din("st_rgc", [L, 128, 8, NS, 3])
    st_s5 = din("st_s5", [L, 2, 128, 16, NS])
    st_ffn = din("st_ffn", [L, 128, 48, NS, 2])
    s5col_d = din("s5col", [L, 128, 3, 16])
    s5bp_d = din("s5bp", [L, 128, 2, 2048])
    wst = [din("wst%d" % l, [128, SLEN], F32R) for l in range(L)]
    wada = din("wada", [128, L * ADA_TILES * 2048], F32R)
    yT = dout("yT", [D, NTOK])
    o_rgh = dout("o_rgh", [L, 128, 8, 17])
    o_rgc = dout("o_rgc", [L, 128, 8, 17, 3])
    o_s5 = dout("o_s5", [L, 2, 128, 16, 17])
    o_ffn = dout("o_ffn", [L, 128, 48, 17, 2])
    s5b_d = nc.dram_tensor("s5b_scratch", [L, 128, 4, 1024], F32R, kind="Internal").ap()
    tab_d = nc.dram_tensor("tab_scratch", [L, 16, 128, 2, 512], F32, kind="Internal").ap()

    with contextlib.ExitStack() as stack:
        P = Prog(nc, stack)
        x = P.sb([128, 8, 512], F32, "x")
        h = P.sb([128, 8, 512], F32R, "h")
        R1 = P.sb([128, 24, 512], F32R, "R1")
        wbuf = [P.sb([128, WB], F32R, "wbuf%d" % i) for i in range(NWBUF)]
        tmps = [P.sb([128, 520], F32, "tmp%d" % i) for i in range(NTMP)]
        tmpr = [P.sb([128, 520], F32R, "tmpr%d" % i) for i in range(NTMPR)]
        pss = [P.ps([128, 512], F32, "ps%d" % i) for i in range(8)]
        cols = P.sb([128, NCOL], F32, "cols")
        dcol = P.sb([128, L, 64], F32, "dcol")
        mod = P.sb([128, L, 48, 18], F32, "mod")
        dmod = P.sb([128, L, 3, 8, 18], F32, "dmod")
        ones = P.sb([128, 128], F32R, "ones")
        ident = P.sb([128, 128], F32, "ident")
        cst = P.sb([128, 8], F32, "cst")
        rgh = P.sb([128, L, 8, 17], F32, "rgh")
        rgc = P.sb([128, L, 8, 17, 3], F32, "rgc")
        s5s = P.sb([128, L, 2, 16, 17], F32, "s5s")
        ffc = P.sb([128, L, 48, 17, 2], F32, "ffc")
        tbuf = [P.sb([128, 1024], F32, "tbuf%d" % i) for i in range(2)]
        dgb = [P.sb([128, 6, 128], F32R, "dgb%d" % i) for i in range(2)]
        cC = P.sb([128, 16, 8], F32, "cC")
        cS = P.sb([128, 16, 8], F32, "cS")
        rhoM = P.sb([128, L, 16, 8], F32, "rhoM")
        identR = P.sb([128, 128], F32R, "identR")
        nidentR = P.sb([128, 128], F32R, "nidentR")
        s5p = P.sb([128, L, 8, 16], F32, "s5p")
        scT = P.sb([128, 8, 18], F32R, "scT")

        free_t = list(range(NTMP))
        free_r = list(range(NTMPR))
        free_p = list(range(8))

        def talloc():
            i = free_t.pop(0)
            return i

        def tfree(*ids):
            for i in ids:
                free_t.append(i)

        def T(i):
            return tmps[i]

        def TK(i):
            return "tmp%d" % i

        def ralloc():
            return free_r.pop(0)

        def rfree(*ids):
            for i in ids:
                free_r.append(i)

        def psalloc():
            return free_p.pop(0)

        def psfree(*ids):
            for i in ids:
                free_p.append(i)

        def PK(i):
            return "ps%d" % i


        def tt(out, in0, in1, op, reads, writes, eng="dve"):
            P.op(eng, lambda e: e.tensor_tensor(out=out, in0=in0, in1=in1, op=op), reads, writes)

        def ts(out, in0, s1, s2, op0, op1, reads, writes, eng="dve"):
            if s2 is None:
                P.op(eng, lambda e: e.tensor_scalar(out=out, in0=in0, scalar1=s1, scalar2=None, op0=op0), reads, writes)
            else:
                P.op(eng, lambda e: e.tensor_scalar(out=out, in0=in0, scalar1=s1, scalar2=s2, op0=op0, op1=op1), reads, writes)

        def stt(out, in0, sc, in1, op0, op1, reads, writes):
            P.op("dve", lambda e: e.scalar_tensor_tensor(out=out, in0=in0, scalar=sc, in1=in1, op0=op0, op1=op1), reads, writes)

        def act(out, in_, func, reads, writes, scale=1.0, bias=None):
            if bias is None:
                P.op("act", lambda e: e.activation(out=out, in_=in_, func=func, scale=scale), reads, writes)
            else:
                P.op("act", lambda e: e.activation(out=out, in_=in_, func=func, scale=scale, bias=bias), reads, writes)

        def mm(out, lhsT, rhs, start, stop, reads, writes):
            P.op("pe", lambda e: e.matmul(out, lhsT, rhs, start=start, stop=stop), reads, writes)

        def cp(out, in_, reads, writes, eng="dve"):
            P.op(eng, lambda e: e.tensor_copy(out=out, in_=in_), reads, writes)

        def scan(out, d0, d1, init, reads, writes):
            P.op("dve", lambda e: e.tensor_tensor_scan(out=out, data0=d0, data1=d1, initial=init, op0=ALU.mult, op1=ALU.add),
                 reads, writes)

        def mset(ap, v, reads, writes, eng="dve"):
            P.op(eng, lambda e: e.memset(ap, v), reads, writes)

        MUL, ADD, SUB = ALU.mult, ALU.add, ALU.subtract

        wstate = {"n": 0, "nrot": NWBUF}

        def wload(src_ap, size, reads=(), eng="sp"):
            i = wstate["n"] % wstate["nrot"]
            wstate["n"] += 1
            key = "wbuf%d" % i
            wkeys = [key] + (["wbuf3a", "wbuf3b"] if i == 3 else [])
            P.dma(wbuf[i][:, 0:size], src_ap, reads=list(reads), writes=wkeys, eng=eng)
            return wbuf[i], key

        def wtile(l, name):
            o, s = SOFF[name]
            return wload(wst[l][:, o:o + s], s)

        P.dma(cols[:], cols_d, writes=["cols"])
        ti_c = talloc()
        P.dma(T(ti_c)[:, 0:144].rearrange("p (k s) -> p k s", s=18), cT.rearrange("(k p) s -> p k s", p=128), writes=[TK(ti_c)])
        mset(ident[:], 0.0, [], ["ident"], eng="pool")
        P.op("pool", lambda e: e.affine_select(out=ident[:], in_=ident[:], pattern=[[-1, 128]],
                                               compare_op=ALU.not_equal, fill=1.0, base=0, channel_multiplier=1),
             reads=["ident"], writes=["ident"])
        ts(ones[:], ident[:], 0.0, 1.0, MUL, ADD, ["ident"], ["ones"])
        ts(identR[:], ident[:], 1.0, 0.0, MUL, ADD, ["ident"], ["identR"])
        ts(nidentR[:], ident[:], -1.0, 0.0, MUL, ADD, ["ident"], ["identR"])
        mset(cst[:, 0:1], 1e-6, [], ["cst"])
        mset(cst[:, 1:2], math.pi / 2, ["cst"], ["cst"])
        mset(cst[:, 2:3], 1.0, ["cst"], ["cst"])
        mset(cst[:, 3:4], 0.0, ["cst"], ["cst"])
        mset(rgh[:], 0.0, [], ["rgh"], eng="pool")
        mset(rgc[:], 0.0, [], ["rgc"], eng="pool")
        mset(s5s[:], 0.0, [], ["s5s"], eng="pool")
        mset(ffc[:], 0.0, [], ["ffc"], eng="pool")
        for l in range(L):
            P.dma(rgh[:, l, :, 1:17], st_rgh[l], writes=["rgh"])
            P.dma(rgc[:, l, :, 1:17, :], st_rgc[l], writes=["rgc"])
            for ri in range(2):
                P.dma(s5s[:, l, ri, :, 1:17], st_s5[l, ri], writes=["s5s"])
            for c0 in range(0, 48, 8):
                P.dma(ffc[:, l, c0:c0 + 8, 1:17, :], st_ffn[l, :, c0:c0 + 8], writes=["ffc"])

        def setup_A1(l):
            a0 = talloc()
            a1 = talloc()
            K0, K1 = TK(a0), TK(a1)
            prm = T(a0)[:, 0:48].rearrange("p (a b) -> p a b", b=16)
            P.dma(prm, s5col_d[l], writes=[K0], eng="pool")
            w_ = T(a1)

            def S(i, w_=w_):
                return w_[:, i * 16:(i + 1) * 16]
            lr, li, ldt = prm[:, 0, :], prm[:, 1, :], prm[:, 2, :]
            rho, ar, ai, zr, zi = (s5p[:, l, i, :] for i in range(5))
            act(S(0), ldt, AF.Exp, [K0], [K1])
            tt(S(1), lr, S(0), MUL, [K0, K1], [K1])
            act(rho, S(1), AF.Exp, [K1], ["s5p"])
            tt(S(2), li, S(0), MUL, [K0, K1], [K1])
            act(S(3), S(2), AF.Sin, [K1], [K1], scale=1.0 / 16)
            act(S(4), S(2), AF.Sin, [K1, "cst"], [K1], scale=1.0 / 16, bias=cst[:, 1:2])
            for _ in range(4):
                tt(S(5), S(4), S(4), MUL, [K1], [K1])
                tt(S(6), S(3), S(3), MUL, [K1], [K1])
                tt(S(7), S(4), S(3), MUL, [K1], [K1])
                tt(S(4), S(5), S(6), SUB, [K1], [K1])
                ts(S(3), S(7), 2.0, None, MUL, None, [K1], [K1])
            cp(s5p[:, l, 5, :], S(4), [K1, "s5p"], ["s5p"])
            cp(s5p[:, l, 6, :], S(3), [K1, "s5p"], ["s5p"])
            tt(ar, rho, S(4), MUL, [K1, "s5p"], ["s5p"])
            tt(ai, rho, S(3), MUL, [K1, "s5p"], ["s5p"])
            tt(S(5), lr, lr, MUL, [K0], [K1])
            tt(S(6), li, li, MUL, [K0], [K1])
            tt(S(5), S(5), S(6), ADD, [K1], [K1])
            P.op("dve", lambda e, o=S(5): e.reciprocal(out=o, in_=o), [K1], [K1])
            ts(S(6), ar, -1.0, None, ADD, None, ["s5p"], [K1])
            tt(S(7), S(6), lr, MUL, [K1, K0], [K1])
            tt(S(8), ai, li, MUL, ["s5p", K0], [K1])
            tt(S(7), S(7), S(8), ADD, [K1], [K1])
            tt(zr, S(7), S(5), MUL, [K1], ["s5p"])
            tt(S(7), ai, lr, MUL, ["s5p", K0], [K1])
            tt(S(8), S(6), li, MUL, [K1, K0], [K1])
            tt(S(7), S(7), S(8), SUB, [K1], [K1])
            tt(zi, S(7), S(5), MUL, [K1], ["s5p"])
            tfree(a0, a1)

        def setup_A2(l):
            for q in range(4):
                zr_ps = psalloc()
                zi_ps = psalloc()
                for pp in range(4):
                    p = 4 * q + pp
                    for (zi_, dst) in ((3, zr_ps), (4, zi_ps)):
                        r = ralloc()
                        ts(tmpr[r][:, 0:128], ident[:], s5p[:, l, zi_, p:p + 1], None, MUL, None, ["ident", "s5p"], ["tmpr%d" % r])
                        mm(pss[dst][:, pp * 128:(pp + 1) * 128], ones[:], tmpr[r][:, 0:128], True, True,
                           ["ones", "tmpr%d" % r], [PK(dst)])
                        rfree(r)
                br = talloc()
                bi = talloc()
                P.dma(T(br)[:, 0:512], s5bp_d[l, :, 0, q * 512:(q + 1) * 512], writes=[TK(br)])
                P.dma(T(bi)[:, 0:512], s5bp_d[l, :, 1, q * 512:(q + 1) * 512], writes=[TK(bi)])
                t1 = talloc()
                t2 = talloc()
                r = ralloc()
                r2 = ralloc()
                RK = "tmpr%d" % r
                RK2 = "tmpr%d" % r2
                A, B_, U1, U2 = T(br)[:, 0:512], T(bi)[:, 0:512], T(t1)[:, 0:512], T(t2)[:, 0:512]
                tt(U1, A, pss[zr_ps][:], MUL, [TK(br), PK(zr_ps)], [TK(t1)])
                tt(U2, B_, pss[zi_ps][:], MUL, [TK(bi), PK(zi_ps)], [TK(t2)])
                tt(tmpr[r][:, 0:512], U1, U2, SUB, [TK(t1), TK(t2)], [RK])
                tt(U1, A, pss[zi_ps][:], MUL, [TK(br), PK(zi_ps)], [TK(t1)])
                tt(U2, B_, pss[zr_ps][:], MUL, [TK(bi), PK(zr_ps)], [TK(t2)])
                tt(tmpr[r2][:, 0:512], U1, U2, ADD, [TK(t1), TK(t2)], [RK2])
                P.dma(s5b_d[l, :, q, 0:512], tmpr[r][:, 0:512], reads=[RK], writes=["s5b_d"])
                P.dma(s5b_d[l, :, q, 512:1024], tmpr[r2][:, 0:512], reads=[RK2, "s5b_d"], writes=["s5b_d"])
                rfree(r, r2)
                tfree(br, bi, t1, t2)
                psfree(zr_ps, zi_ps)

        act(scT[:].rearrange("p k s -> p (k s)"), T(ti_c)[:, 0:144], AF.Silu, [TK(ti_c)], ["scT"])
        tfree(ti_c)
        def ada_tile(l, t):
            if True:
                wb, wk = wload(wada[:, (l * ADA_TILES + t) * 2048:(l * ADA_TILES + t + 1) * 2048], 2048,
                               eng=("act" if l == 0 else "sp"))
                wv = wb[:, 0:2048].rearrange("p (k n) -> p k n", n=256)
                for half in range(2):
                    m = 2 * t + half
                    pi = psalloc()
                    for k in range(8):
                        mm(pss[pi][:, 0:18], wv[:, k, half * 128:(half + 1) * 128], scT[:, k, :], k == 0, k == 7,
                           [wk, "scT"], [PK(pi)])
                    bcol = l * NCL + CO["bada"] + m
                    act(mod[:, l, m, :], pss[pi][:, 0:18], AF.Identity, [PK(pi), "cols"], ["mod%d" % l], bias=cols[:, bcol:bcol + 1])
                    psfree(pi)

        def setup_B(l):
            baseC = tbuf[0][:, 0:1024].rearrange("p (a b) -> p a b", b=64)
            baseS = tbuf[1][:, 0:1024].rearrange("p (a b) -> p a b", b=64)
            BK = ["tbuf0", "tbuf1"]
            cp(baseC[:, :, 0], s5p[:, l, 5, :], ["s5p"] + BK, ["tbuf0"])
            cp(baseS[:, :, 0], s5p[:, l, 6, :], ["s5p"] + BK, ["tbuf1"])
            b1 = talloc()
            b2 = talloc()
            k = 1
            while k < 64:
                ec = baseC[:, :, k - 1:k].to_broadcast([128, 16, k])
                es = baseS[:, :, k - 1:k].to_broadcast([128, 16, k])
                u1 = T(b1)[:, 0:16 * k].rearrange("p (a b) -> p a b", b=k)
                u2 = T(b2)[:, 0:16 * k].rearrange("p (a b) -> p a b", b=k)
                c0 = baseC[:, :, 0:k]
                s0 = baseS[:, :, 0:k]
                tt(u1, c0, ec, MUL, BK, [TK(b1)])
                tt(u2, s0, es, MUL, BK, [TK(b2)])
                tt(baseC[:, :, k:2 * k], u1, u2, SUB, [TK(b1), TK(b2)] + BK, ["tbuf0"])
                tt(u1, s0, ec, MUL, BK, [TK(b1)])
                tt(u2, c0, es, MUL, BK, [TK(b2)])
                tt(baseS[:, :, k:2 * k], u1, u2, ADD, [TK(b1), TK(b2)] + BK, ["tbuf1"])
                k *= 2
            CK = ["cCS"]
            mset(cC[:, :, 0], 1.0, CK, CK)
            mset(cS[:, :, 0], 0.0, CK, CK)
            cp(cC[:, :, 1], baseC[:, :, 63], BK + CK, CK)
            cp(cS[:, :, 1], baseS[:, :, 63], BK + CK, CK)
            v1 = T(b1)[:, 0:16]
            v2 = T(b2)[:, 0:16]
            for a in range(2, 8):
                tt(v1, cC[:, :, a - 1], cC[:, :, 1], MUL, CK, [TK(b1)])
                tt(v2, cS[:, :, a - 1], cS[:, :, 1], MUL, CK, [TK(b2)])
                tt(cC[:, :, a], v1, v2, SUB, [TK(b1), TK(b2)] + CK, CK)
                tt(v1, cS[:, :, a - 1], cC[:, :, 1], MUL, CK, [TK(b1)])
                tt(v2, cC[:, :, a - 1], cS[:, :, 1], MUL, CK, [TK(b2)])
                tt(cS[:, :, a], v1, v2, ADD, [TK(b1), TK(b2)] + CK, CK)
            tfree(b1, b2)
            mset(rhoM[:, l, :, 0:1], 0.0, ["rhoM"], ["rhoM"])
            cp(rhoM[:, l, :, 1:8], s5p[:, l, 0, :].unsqueeze(2).to_broadcast([128, 16, 7]), ["s5p", "rhoM"], ["rhoM"])
            for p in range(16):
                ccb = cC[:, p, :].unsqueeze(2).to_broadcast([128, 8, 64])
                scb = cS[:, p, :].unsqueeze(2).to_broadcast([128, 8, 64])
                cbb = baseC[:, p, :].unsqueeze(1).to_broadcast([128, 8, 64])
                sbb = baseS[:, p, :].unsqueeze(1).to_broadcast([128, 8, 64])
                m1, m2, m3, m4, fc, fs = (talloc() for _ in range(6))

                def W3(i):
                    return T(i)[:, 0:512].rearrange("p (a b) -> p a b", b=64)
                tt(W3(m1), ccb, cbb, MUL, BK + CK, [TK(m1)])
                tt(W3(m2), scb, sbb, MUL, BK + CK, [TK(m2)])
                tt(T(fc)[:, 0:512], T(m1)[:, 0:512], T(m2)[:, 0:512], SUB, [TK(m1), TK(m2)], [TK(fc)])
                tt(W3(m3), scb, cbb, MUL, BK + CK, [TK(m3)])
                tt(W3(m4), ccb, sbb, MUL, BK + CK, [TK(m4)], eng="pool")
                tt(T(fs)[:, 0:512], T(m3)[:, 0:512], T(m4)[:, 0:512], ADD, [TK(m3), TK(m4)], [TK(fs)], eng="pool")
                P.dma(tab_d[l, p, :, 0, :], T(fc)[:, 0:512], reads=[TK(fc)], writes=["tab_d"])
                P.dma(tab_d[l, p, :, 1, :], T(fs)[:, 0:512], reads=[TK(fs), "tab_d"], writes=["tab_d"])
                tfree(m1, m2, m3, m4, fc, fs)

        def dmod_l(l):
            for c in range(8):
                g1c = l * NCL + CO["g1"] + c
                g2c = l * NCL + CO["g2"] + c
                ts(dmod[:, l, 0, c, :], mod[:, l, 8 + c, :], 1.0, cols[:, g1c:g1c + 1], ADD, MUL, ["mod%d" % l, "cols"], ["dmod%d" % l])
                ts(dmod[:, l, 1, c, :], mod[:, l, 16 + c, :], 0.5, None, MUL, None, ["mod%d" % l], ["dmod%d" % l])
                ts(dmod[:, l, 2, c, :], mod[:, l, 32 + c, :], 1.0, cols[:, g2c:g2c + 1], ADD, MUL, ["mod%d" % l, "cols"], ["dmod%d" % l])


        for l in range(L):
            setup_A1(l)
        for l in range(L):
            b = l * NCL
            for (dst, src) in ((0, CO["ba"]), (8, CO["bx"]), (32, CO["bin"] + 20), (40, CO["bin"] + 28)):
                ts(dcol[:, l, dst:dst + 8], cols[:, b + src:b + src + 8], 0.5, None, MUL, None, ["cols"], ["dcol"])
            act(dcol[:, l, 48:56], cols[:, b + CO["lam"]:b + CO["lam"] + 8], AF.Exp, ["cols"], ["dcol"], scale=-1.0)
            act(dcol[:, l, 56:64], dcol[:, l, 48:56], AF.Ln, ["dcol", "cst"], ["dcol"], bias=cst[:, 2:3])
            ts(dcol[:, l, 16:24], dcol[:, l, 56:64], -8.0, None, MUL, None, ["dcol"], ["dcol"])
            ts(dcol[:, l, 24:32], dcol[:, l, 56:64], -4.0, None, MUL, None, ["dcol"], ["dcol"])

        for l in range(L):
            setup_A2(l)
        for t in range(ADA_TILES):
            ada_tile(0, t)
        for l in range(L):
            setup_B(l)
        dmod_l(0)
        ada_pending = [(1, t) for t in range(ADA_TILES)]

        def ada_step():
            if ada_pending:
                ada_tile(*ada_pending.pop(0))
                if not ada_pending:
                    dmod_l(1)

        tstate = {"n": 0}

        def tabload(l, p, small):
            i = tstate["n"] % 2
            tstate["n"] += 1
            key = "tbuf%d" % i
            if small:
                P.dma(tbuf[i][:, 0:16].rearrange("p (a n) -> p a n", a=2), tab_d[l, p, :, :, 0:8], reads=["tab_d"], writes=[key])
            else:
                P.dma(tbuf[i][:, 0:1024].rearrange("p (a n) -> p a n", a=2), tab_d[l, p], reads=["tab_d"], writes=[key])
            return tbuf[i], key

        dstate = {"n": 0}

        def mkdiag(wcols):
            i = dstate["n"] % 2
            dstate["n"] += 1
            key = "dgb%d" % i
            for t, wc in enumerate(wcols):
                ts(dgb[i][:, t, :], ident[:], wc, 0.0, MUL, ADD, ["ident", "cols"], [key], eng="pool")
            return dgb[i], key

        def run_tile(col0, N, nseq, slen, seq0):
            prompt = nseq == 1

            def V3(ap2):
                return ap2.rearrange("p (s t) -> p s t", t=slen)

            def DV(ap2):
                return ap2 if prompt else V3(ap2)

            def bc_seq(ap_pn):
                return ap_pn.unsqueeze(2).to_broadcast([128, nseq, slen])

            for c in range(8):
                P.dma(x[:, c, 0:N], xT[c * 128:(c + 1) * 128, col0:col0 + N], writes=["x%d" % c], eng="pool")

            def rmsnorm_rstd():
                pi = psalloc()
                for c in range(8):
                    r = ralloc()
                    act(tmpr[r][:, 0:N], x[:, c, 0:N], AF.Square, ["x%d" % c], ["tmpr%d" % r])
                    mm(pss[pi][:, 0:N], ones[:], tmpr[r][:, 0:N], c == 0, c == 7, ["ones", "tmpr%d" % r], [PK(pi)])
                    rfree(r)
                t1 = talloc()
                rs = talloc()
                act(T(t1)[:, 0:N], pss[pi][:, 0:N], AF.Ln, [PK(pi), "cst"], [TK(t1)], scale=1.0 / D, bias=cst[:, 0:1])
                act(T(rs)[:, 0:N], T(t1)[:, 0:N], AF.Exp, [TK(t1)], [TK(rs)], scale=-0.5)
                tfree(t1)
                psfree(pi)
                return rs

            def modulate(l, ai_, shift_chunk0):
                rs = rmsnorm_rstd()
                for c in range(8):
                    t1 = talloc()
                    eng = "pool" if c in (2, 5, 7) else "dve"
                    tt(T(t1)[:, 0:N], x[:, c, 0:N], T(rs)[:, 0:N], MUL, ["x%d" % c, TK(rs)], [TK(t1)], eng=eng)
                    if prompt:
                        act(h[:, c, 0:N], T(t1)[:, 0:N], AF.Identity, [TK(t1), "dmod%d" % l, "mod%d" % l], ["h%d" % c],
                            scale=dmod[:, l, ai_, c, 0:1], bias=mod[:, l, shift_chunk0 + c, 0:1])
                    else:
                        tt(V3(T(t1)[:, 0:N]), V3(T(t1)[:, 0:N]), bc_seq(dmod[:, l, ai_, c, seq0:seq0 + nseq]), MUL,
                           [TK(t1), "dmod%d" % l], [TK(t1)])
                        tt(V3(h[:, c, 0:N]), V3(T(t1)[:, 0:N]), bc_seq(mod[:, l, shift_chunk0 + c, seq0:seq0 + nseq]), ADD,
                           [TK(t1), "mod%d" % l], ["h%d" % c])
                    tfree(t1)
                tfree(rs)

            def resid_add(l, o, pi, gate18):
                if prompt:
                    stt(x[:, o, 0:N], pss[pi][:, 0:N], gate18[:, 0:1], x[:, o, 0:N], MUL, ADD,
                        [PK(pi), "x%d" % o, "mod%d" % l, "dmod%d" % l], ["x%d" % o])
                else:
                    t1 = talloc()
                    tt(V3(T(t1)[:, 0:N]), V3(pss[pi][:, 0:N]), bc_seq(gate18[:, seq0:seq0 + nseq]), MUL,
                       [PK(pi), "mod%d" % l, "dmod%d" % l], [TK(t1)])
                    tt(x[:, o, 0:N], x[:, o, 0:N], T(t1)[:, 0:N], ADD, [TK(t1), "x%d" % o], ["x%d" % o])
                    tfree(t1)

            for l in range(L):
                cb = l * NCL

                def col(name, i, cb=cb):
                    return cols[:, cb + CO[name] + i:cb + CO[name] + i + 1]

                modulate(l, 0, 0)

                rgst = {}
                rgst2 = {}

                def rg_A(c):
                    wb, wk = wtile(l, "rg%d" % c)
                    wv = wb[:, 0:2048].rearrange("p (k n) -> p k n", n=256)
                    p_rx = psalloc()
                    p_ry = psalloc()
                    for k in range(8):
                        mm(pss[p_rx][:, 0:N], wv[:, k, 0:128], h[:, k, 0:N], k == 0, k == 7, [wk, "h%d" % k], [PK(p_rx)])
                    for k in range(8):
                        mm(pss[p_ry][:, 0:N], wv[:, k, 128:256], h[:, k, 0:N], k == 0, k == 7, [wk, "h%d" % k], [PK(p_ry)])
                    ex = ralloc()
                    EK = "tmpr%d" % ex
                    exv = tmpr[ex][:, 0:nseq * (slen + 4)].rearrange("p (s t) -> p s t", t=slen + 4)
                    act(exv[:, :, 0:3], rgc[:, l, c, seq0:seq0 + nseq, :], AF.Identity, ["rgc", EK], [EK])
                    act(exv[:, :, 3:3 + slen], V3(pss[p_rx][:, 0:N]), AF.Identity, [PK(p_rx), "cols", EK], [EK], bias=col("bin", c))
                    act(rgc[:, l, c, seq0:seq0 + nseq, :], V3(pss[p_rx][:, 0:N])[:, :, slen - 3:slen], AF.Identity,
                        [PK(p_rx), "cols", "rgc"], ["rgc"], bias=col("bin", c))
                    gy = talloc()
                    act(T(gy)[:, 0:N], pss[p_ry][:, 0:N], AF.Gelu_apprx_tanh, [PK(p_ry), "cols"], [TK(gy)], bias=col("bin", 8 + c))
                    psfree(p_rx, p_ry)
                    dg, dk = mkdiag([col("rcw", j * 8 + c) for j in range(4)])
                    rgst[c] = (ex, gy, dg, dk)

                def rg_B(c):
                    ex, gy, dg, dk = rgst.pop(c)
                    EK = "tmpr%d" % ex
                    exv = tmpr[ex][:, 0:nseq * (slen + 4)].rearrange("p (s t) -> p s t", t=slen + 4)
                    gw, gk = wtile(l, "rgd%d" % c)
                    p_c = psalloc()
                    for j in range(4):
                        mm(V3(pss[p_c][:, 0:N]), dg[:, j, :], exv[:, :, j:j + slen], j == 0, j == 3, [dk, EK], [PK(p_c)])
                    xc = ralloc()
                    XK = "tmpr%d" % xc
                    act(tmpr[xc][:, 0:N], pss[p_c][:, 0:N], AF.Identity, [PK(p_c), "cols"], [XK], bias=col("rcb", c))
                    rfree(ex)
                    psfree(p_c)
                    xcf = tmpr[xc][:, 0:N].bitcast(F32)
                    p_a = psalloc()
                    p_x = psalloc()
                    mm(pss[p_a][:, 0:N], gw[:, 0:128], tmpr[xc][:, 0:N], True, True, [gk, XK], [PK(p_a)])
                    mm(pss[p_x][:, 0:N], gw[:, 128:256], tmpr[xc][:, 0:N], True, True, [gk, XK], [PK(p_x)])
                    tr_ = talloc()
                    ti_ = talloc()
                    TR, TI, GY = T(tr_)[:, 0:N], T(ti_)[:, 0:N], T(gy)[:, 0:N]
                    act(TR, pss[p_a][:, 0:N], AF.Tanh, [PK(p_a), "dcol"], [TK(tr_)], scale=0.5, bias=dcol[:, l, c:c + 1])
                    act(TI, pss[p_x][:, 0:N], AF.Tanh, [PK(p_x), "dcol"], [TK(ti_)], scale=0.5, bias=dcol[:, l, 8 + c:9 + c])
                    psfree(p_a, p_x)
                    a_ = talloc()
                    a2 = talloc()
                    A_, A2 = T(a_)[:, 0:N], T(a2)[:, 0:N]
                    act(A_, TR, AF.Exp, [TK(tr_), "dcol"], [TK(a_)], scale=dcol[:, l, 24 + c:25 + c], bias=dcol[:, l, 24 + c:25 + c])
                    act(A2, TR, AF.Exp, [TK(tr_), "dcol"], [TK(a2)], scale=dcol[:, l, 16 + c:17 + c], bias=dcol[:, l, 16 + c:17 + c])
                    act(A2, A2, AF.Ln, [TK(a2), "cst"], [TK(a2)], scale=-1.0, bias=cst[:, 2:3])
                    act(A2, A2, AF.Exp, [TK(a2)], [TK(a2)], scale=0.5)
                    rgst2[c] = (xc, tr_, ti_, gy, a_, a2)

                def rg_B2(c):
                    xc, tr_, ti_, gy, a_, a2 = rgst2.pop(c)
                    XK = "tmpr%d" % xc
                    xcf = tmpr[xc][:, 0:N].bitcast(F32)
                    TR, TI, GY = T(tr_)[:, 0:N], T(ti_)[:, 0:N], T(gy)[:, 0:N]
                    A_, A2 = T(a_)[:, 0:N], T(a2)[:, 0:N]
                    stt(TI, TI, 1.0, xcf, ADD, MUL, [TK(ti_), XK], [TK(ti_)])
                    stt(TI, TI, 0.5, A2, MUL, MUL, [TK(ti_), TK(a2)], [TK(ti_)])
                    rfree(xc)
                    if prompt:
                        scan(TR, A_, TI, rgh[:, l, c, seq0:seq0 + 1], [TK(a_), TK(ti_), "rgh", TK(tr_)], [TK(tr_)])
                    else:
                        a3, b3 = V3(A_), V3(TI)
                        t0 = talloc()
                        t0v = T(t0)[:, 0:nseq]
                        tt(t0v, a3[:, :, 0], rgh[:, l, c, seq0:seq0 + nseq], MUL, [TK(a_), "rgh"], [TK(t0)], eng="pool")
                        tt(b3[:, :, 0], b3[:, :, 0], t0v, ADD, [TK(ti_), TK(t0)], [TK(ti_)], eng="pool")
                        mset(a3[:, :, 0], 0.0, [TK(a_), TK(t0)], [TK(a_)], eng="pool")
                        tfree(t0)
                        scan(TR, A_, TI, 0.0, [TK(a_), TK(ti_), TK(tr_)], [TK(tr_)])
                    cp(rgh[:, l, c, seq0:seq0 + nseq], V3(TR)[:, :, slen - 1], [TK(tr_), "rgh"], ["rgh"], eng="pool")
                    tt(R1[:, c, 0:N], TR, GY, MUL, [TK(tr_), TK(gy)], ["R%d" % c])
                    tfree(tr_, ti_, gy, a_, a2)

                s5st = {}
                qst = {}

                def s5_A(p):
                    q, pp = divmod(p, 4)
                    if pp == 0:
                        wb, wk = wtile(l, "sx%d" % q)
                        wv = wb[:, 0:1024].rearrange("p (k n) -> p k n", n=128)
                        p_sx = psalloc()
                        for k in range(8):
                            mm(pss[p_sx][:, 0:N], wv[:, k, :], h[:, k, 0:N], k == 0, k == 7, [wk, "h%d" % k], [PK(p_sx)])
                        sx = ralloc()
                        SXK = "tmpr%d" % sx
                        act(tmpr[sx][:, 0:N], pss[p_sx][:, 0:N], AF.Identity, [PK(p_sx), "cols"], [SXK], bias=col("bin", 16 + q))
                        psfree(p_sx)
                        bwb, bwk = wbuf[3], "wbuf3a"
                        P.dma(bwb[:, 0:1024], s5b_d[l, :, q, :], reads=["s5b_d"], writes=[bwk, "wbuf3"])
                        qst[q] = dict(sx=sx, bwb=bwb, bwk=bwk, p_y=psalloc())
                    Q = qst[q]
                    sx = Q["sx"]
                    SXK = "tmpr%d" % sx
                    bwv = Q["bwb"][:, 0:1024].rearrange("p (a b n) -> p a b n", a=2, n=128)
                    p_vr = psalloc()
                    p_vi = psalloc()
                    mm(pss[p_vr][:, 0:N], bwv[:, 0, pp, :], tmpr[sx][:, 0:N], True, True, [Q["bwk"], SXK], [PK(p_vr)])
                    mm(pss[p_vi][:, 0:N], bwv[:, 1, pp, :], tmpr[sx][:, 0:N], True, True, [Q["bwk"], SXK], [PK(p_vi)])
                    tb, tk = tabload(l, p, not prompt)
                    s5st[p] = (p_vr, p_vi, tb, tk)

                def s5_B(p):
                    q, pp = divmod(p, 4)
                    Q = qst[q]
                    p_vr, p_vi, tb, tk = s5st.pop(p)
                    if pp == 0:
                        cwb, cwk = wbuf[3][:, 1024:2048], "wbuf3b"
                        o_, s_ = SOFF["s5c%d" % q]
                        P.dma(cwb, wst[l][:, o_:o_ + s_], writes=[cwk, "wbuf3"])
                        act(cwb[:, 512:1024], cwb[:, 512:1024].bitcast(F32), AF.Identity, [cwk], [cwk], scale=-1.0)
                        Q["cwb"], Q["cwk"] = cwb, cwk
                    if prompt:
                        Cv, Sv = tb[:, 0:N], tb[:, 512:512 + N]
                        cL, sL = tb[:, N - 1:N], tb[:, 512 + N - 1:512 + N]
                    else:
                        Cv = tb[:, 0:8].unsqueeze(1).to_broadcast([128, nseq, 8])
                        Sv = tb[:, 8:16].unsqueeze(1).to_broadcast([128, nseq, 8])
                        cL, sL = tb[:, 7:8], tb[:, 15:16]
                    t1, t2, wr, wi = talloc(), talloc(), talloc(), talloc()
                    K1_, K2_, KR, KI = TK(t1), TK(t2), TK(wr), TK(wi)
                    T1, T2, WR, WI = T(t1)[:, 0:N], T(t2)[:, 0:N], T(wr)[:, 0:N], T(wi)[:, 0:N]
                    vr, vi = DV(pss[p_vr][:, 0:N]), DV(pss[p_vi][:, 0:N])
                    tt(DV(T1), vr, Cv, MUL, [PK(p_vr), tk], [K1_])
                    tt(DV(T2), vi, Sv, MUL, [PK(p_vi), tk], [K2_])
                    tt(WR, T1, T2, ADD, [K1_, K2_], [KR])
                    tt(DV(T1), vi, Cv, MUL, [PK(p_vi), tk], [K1_])
                    tt(DV(T2), vr, Sv, MUL, [PK(p_vr), tk], [K2_])
                    tt(WI, T1, T2, SUB, [K1_, K2_], [KI])
                    psfree(p_vr, p_vi)
                    if prompt:
                        rho_b = s5p[:, l, 0, p:p + 1].to_broadcast([128, N])
                        scan(T1, rho_b, WR, s5s[:, l, 0, p, seq0:seq0 + 1], [KR, "s5s", "s5p", K1_], [K1_])
                        scan(T2, rho_b, WI, s5s[:, l, 1, p, seq0:seq0 + 1], [KI, "s5s", "s5p", K2_], [K2_])
                        zr_e, zi_e = T1[:, N - 1:N], T2[:, N - 1:N]
                        nst = 1
                    else:
                        rm = talloc()
                        RM = T(rm)[:, 0:N]
                        cp(V3(RM), rhoM[:, l, p, :].unsqueeze(1).to_broadcast([128, nseq, 8]), ["rhoM"], [TK(rm)], eng="pool")
                        stt(V3(WR)[:, :, 0], s5s[:, l, 0, p, seq0:seq0 + nseq], s5p[:, l, 0, p:p + 1], V3(WR)[:, :, 0], MUL, ADD,
                            [KR, "s5s", "s5p"], [KR])
                        stt(V3(WI)[:, :, 0], s5s[:, l, 1, p, seq0:seq0 + nseq], s5p[:, l, 0, p:p + 1], V3(WI)[:, :, 0], MUL, ADD,
                            [KI, "s5s", "s5p"], [KI])
                        scan(T1, RM, WR, 0.0, [KR, TK(rm), K1_], [K1_])
                        scan(T2, RM, WI, 0.0, [KI, TK(rm), K2_], [K2_])
                        tfree(rm)
                        zr_e, zi_e = V3(T1)[:, :, 7], V3(T2)[:, :, 7]
                        nst = nseq
                    tfree(wr, wi)
                    us = [ralloc() for _ in range(4)]
                    UK = ["tmpr%d" % u for u in us]
                    U = [tmpr[u][:, 0:N] for u in us]
                    tt(DV(U[0]), DV(T1), Cv, MUL, [K1_, tk], [UK[0]])
                    stt(DV(U[1]), DV(T2), -1.0, Sv, MUL, MUL, [K2_, tk], [UK[1]])
                    tt(DV(U[2]), DV(T2), Cv, MUL, [K2_, tk], [UK[2]], eng="pool")
                    tt(DV(U[3]), DV(T1), Sv, MUL, [K1_, tk], [UK[3]], eng="pool")
                    e1 = talloc()
                    E = T(e1)
                    EK1 = TK(e1)
                    ts(E[:, 0:nst], zr_e, cL, 0.0, MUL, ADD, [K1_, tk], [EK1], eng="pool")
                    ts(E[:, 32:32 + nst], zi_e, sL, 0.0, MUL, ADD, [K2_, tk, EK1], [EK1], eng="pool")
                    tt(s5s[:, l, 0, p, seq0:seq0 + nst], E[:, 0:nst], E[:, 32:32 + nst], SUB, [EK1, "s5s"], ["s5s"], eng="pool")
                    ts(E[:, 64:64 + nst], zi_e, cL, 0.0, MUL, ADD, [K2_, tk, EK1], [EK1], eng="pool")
                    ts(E[:, 96:96 + nst], zr_e, sL, 0.0, MUL, ADD, [K1_, tk, EK1], [EK1], eng="pool")
                    tt(s5s[:, l, 1, p, seq0:seq0 + nst], E[:, 64:64 + nst], E[:, 96:96 + nst], ADD, [EK1, "s5s"], ["s5s"], eng="pool")
                    tfree(e1)
                    tfree(t1, t2)
                    cwv = Q["cwb"].rearrange("p (a b n) -> p a b n", a=2, n=128)
                    p_y = Q["p_y"]
                    mm(pss[p_y][:, 0:N], cwv[:, 0, pp, :], U[0], pp == 0, False, [Q["cwk"], UK[0]], [PK(p_y)])
                    mm(pss[p_y][:, 0:N], cwv[:, 0, pp, :], U[1], False, False, [Q["cwk"], UK[1]], [PK(p_y)])
                    mm(pss[p_y][:, 0:N], cwv[:, 1, pp, :], U[2], False, False, [Q["cwk"], UK[2]], [PK(p_y)])
                    mm(pss[p_y][:, 0:N], cwv[:, 1, pp, :], U[3], False, pp == 3, [Q["cwk"], UK[3]], [PK(p_y)])
                    rfree(*us)
                    if pp == 3:
                        sx = Q["sx"]
                        ty = talloc()
                        stt(T(ty)[:, 0:N], tmpr[sx][:, 0:N].bitcast(F32), col("s5d", q), pss[p_y][:, 0:N], MUL, ADD,
                            ["tmpr%d" % sx, "cols", PK(p_y)], [TK(ty)])
                        act(R1[:, 8 + q, 0:N], T(ty)[:, 0:N], AF.Gelu_apprx_tanh, [TK(ty)], ["R%d" % (8 + q)])
                        tfree(ty)
                        rfree(sx)
                        psfree(p_y)
                        del qst[q]

                wstate["nrot"] = 3
                rg_A(0)
                s5_A(0)
                for c in range(8):
                    rg_B(c)
                    if c + 1 < 8:
                        rg_A(c + 1)
                    for p in (2 * c, 2 * c + 1):
                        if p + 1 < 16:
                            s5_A(p + 1)
                        s5_B(p)
                    rg_B2(c)
                    ada_step()
                wstate["nrot"] = NWBUF

                for m in range(8):
                    wa_, wak = wtile(l, "mga%d" % m)
                    wav = wa_[:, 0:2048].rearrange("p (k n) -> p k n", n=256)
                    p_ga = psalloc()
                    p_gb = psalloc()
                    for (pi, off) in ((p_ga, 0), (p_gb, 128)):
                        for k in range(8):
                            mm(pss[pi][:, 0:N], wav[:, k, off:off + 128], h[:, k, 0:N], k == 0, k == 7, [wak, "h%d" % k], [PK(pi)])
                    wb_, wbk = wtile(l, "mgb%d" % m)
                    wrg = wb_[:, 0:1024].rearrange("p (k n) -> p k n", n=128)
                    wgl = wb_[:, 1024:2048].rearrange("p (k n) -> p k n", n=256)
                    p_ba = psalloc()
                    p_la = psalloc()
                    p_lb = psalloc()
                    for k in range(8):
                        mm(pss[p_ba][:, 0:N], wrg[:, k, :], R1[:, k, 0:N], k == 0, k == 7, [wbk, "R%d" % k], [PK(p_ba)])
                    for (pi, off) in ((p_la, 0), (p_lb, 128)):
                        for k in range(4):
                            mm(pss[pi][:, 0:N], wgl[:, k, off:off + 128], R1[:, 8 + k, 0:N], k == 0, k == 3,
                               [wbk, "R%d" % (8 + k)], [PK(pi)])
                    tga = talloc()
                    tgb = talloc()
                    tgl = talloc()
                    GA, GB, GL = T(tga)[:, 0:N], T(tgb)[:, 0:N], T(tgl)[:, 0:N]
                    act(GA, pss[p_ga][:, 0:N], AF.Tanh, [PK(p_ga), "dcol"], [TK(tga)], scale=0.5, bias=dcol[:, l, 32 + m:33 + m])
                    act(GB, pss[p_gb][:, 0:N], AF.Tanh, [PK(p_gb), "dcol"], [TK(tgb)], scale=0.5, bias=dcol[:, l, 40 + m:41 + m])
                    act(GL, pss[p_lb][:, 0:N], AF.Tanh, [PK(p_lb)], [TK(tgl)], scale=0.5)
                    psfree(p_ga, p_gb, p_lb)
                    stt(GA, GA, 1.0, pss[p_ba][:, 0:N], ADD, MUL, [TK(tga), PK(p_ba)], [TK(tga)])
                    stt(GL, GL, 1.0, pss[p_la][:, 0:N], ADD, MUL, [TK(tgl), PK(p_la)], [TK(tgl)])
                    psfree(p_ba, p_la)
                    stt(GB, GB, 1.0, GL, ADD, MUL, [TK(tgb), TK(tgl)], [TK(tgb)])
                    stt(R1[:, 12 + m, 0:N], GB, 0.5, GA, MUL, ADD, [TK(tgb), TK(tga)], ["R%d" % (12 + m)])
                    tfree(tga, tgb, tgl)
                    ada_step()

                for o2 in range(4):
                    wb, wk = wtile(l, "wo%d" % o2)
                    wv = wb[:, 0:2048].rearrange("p (k n) -> p k n", n=256)
                    for hh in range(2):
                        o = 2 * o2 + hh
                        pi = psalloc()
                        for k in range(8):
                            mm(pss[pi][:, 0:N], wv[:, k, hh * 128:(hh + 1) * 128], R1[:, 12 + k, 0:N], k == 0, k == 7,
                               [wk, "R%d" % (12 + k)], [PK(pi)])
                        resid_add(l, o, pi, dmod[:, l, 1, o, :])
                        psfree(pi)

                modulate(l, 2, 24)
                fst = {}

                def ffn_A(j):
                    wb, wk = wtile(l, "up%d" % j)
                    wv = wb[:, 0:2048].rearrange("p (k n) -> p k n", n=256)
                    st_ = []
                    for half in range(2):
                        cidx = j + 24 * half
                        pi = psalloc()
                        for k in range(8):
                            mm(pss[pi][:, 0:N], wv[:, k, half * 128:(half + 1) * 128], h[:, k, 0:N], k == 0, k == 7,
                               [wk, "h%d" % k], [PK(pi)])
                        ex = talloc()
                        EK = TK(ex)
                        exv = T(ex)[:, 0:nseq * (slen + 2)].rearrange("p (s t) -> p s t", t=slen + 2)
                        cp(exv[:, :, 0:2], ffc[:, l, cidx, seq0:seq0 + nseq, :], ["ffc", EK], [EK], eng="pool")
                        act(exv[:, :, 2:2 + slen], V3(pss[pi][:, 0:N]), AF.Identity, [PK(pi), EK], [EK])
                        act(ffc[:, l, cidx, seq0:seq0 + nseq, :], V3(pss[pi][:, 0:N])[:, :, slen - 2:slen], AF.Identity,
                            [PK(pi), "ffc"], ["ffc"])
                        acc = talloc()
                        AK = TK(acc)
                        a3 = V3(T(acc)[:, 0:N])
                        act(a3, exv[:, :, 0:slen], AF.Identity, [EK, "cols"], [AK], scale=col("fcw", cidx), bias=col("fcb", cidx))
                        st_.append((pi, ex, acc))
                    fst[j] = st_

                def ffn_B(j):
                    st_ = fst.pop(j)
                    accs = []
                    for half in range(2):
                        cidx = j + 24 * half
                        pi, ex, acc = st_[half]
                        EK, AK = TK(ex), TK(acc)
                        exv = T(ex)[:, 0:nseq * (slen + 2)].rearrange("p (s t) -> p s t", t=slen + 2)
                        a3 = V3(T(acc)[:, 0:N])
                        stt(a3, exv[:, :, 1:1 + slen], col("fcw", 48 + cidx), a3, MUL, ADD, [EK, "cols", AK], [AK])
                        stt(T(acc)[:, 0:N], pss[pi][:, 0:N], col("fcw", 96 + cidx), T(acc)[:, 0:N], MUL, ADD, [PK(pi), "cols", AK], [AK])
                        psfree(pi)
                        tfree(ex)
                        accs.append(acc)
                    ua, ub = accs
                    act(T(ua)[:, 0:N], T(ua)[:, 0:N], AF.Gelu_apprx_tanh, [TK(ua)], [TK(ua)])
                    tt(R1[:, j, 0:N], T(ua)[:, 0:N], T(ub)[:, 0:N], MUL, [TK(ua), TK(ub)], ["R%d" % j], eng="pool")
                    tfree(ua, ub)

                ffn_A(0)
                for j in range(24):
                    if j + 1 < 24:
                        ffn_A(j + 1)
                    ffn_B(j)
                    if j % 3 == 2:
                        ada_step()

                for o2 in range(4):
                    pis = [psalloc(), psalloc()]
                    for i in range(3):
                        wb, wk = wtile(l, "dn%d_%d" % (o2, i))
                        wv = wb[:, 0:2048].rearrange("p (k n) -> p k n", n=256)
                        for hh in range(2):
                            for k in range(8):
                                kk = 8 * i + k
                                mm(pss[pis[hh]][:, 0:N], wv[:, k, hh * 128:(hh + 1) * 128], R1[:, kk, 0:N], kk == 0, kk == 23,
                                   [wk, "R%d" % kk], [PK(pis[hh])])
                    for hh in range(2):
                        resid_add(l, 2 * o2 + hh, pis[hh], mod[:, l, 40 + 2 * o2 + hh, :])
                    psfree(*pis)

            rs = rmsnorm_rstd()
            for c in range(8):
                t1 = talloc()
                tt(T(t1)[:, 0:N], x[:, c, 0:N], T(rs)[:, 0:N], MUL, ["x%d" % c, TK(rs)], [TK(t1)])
                act(T(t1)[:, 0:N], T(t1)[:, 0:N], AF.Identity, [TK(t1), "cols"], [TK(t1)], scale=cols[:, 2 * NCL + c:2 * NCL + c + 1])
                P.dma(yT[c * 128:(c + 1) * 128, col0:col0 + N], T(t1)[:, 0:N], reads=[TK(t1)], eng="pool")
                tfree(t1)
            tfree(rs)

        tiles = [(512 * i, 512, 1, 512, 0) for i in range(4)] + [(SEQ, NS * ST, NS, ST, 1)]
        for tile_ in tiles:
            run_tile(*tile_)

        for l in range(L):
            P.dma(o_rgh[l], rgh[:, l], reads=["rgh"], eng="pool")
            P.dma(o_rgc[l].rearrange("p c s j -> p (c s j)"), rgc[:, l].rearrange("p c s j -> p (c s j)"), reads=["rgc"], eng="pool")
            for ri in range(2):
                P.dma(o_s5[l, ri], s5s[:, l, ri], reads=["s5s"], eng="pool")
            P.dma(o_ffn[l].rearrange("p c s j -> p (c s j)"), ffc[:, l].rearrange("p c s j -> p (c s j)"), reads=["ffc"], eng="pool")
        P.finish()
    return nc


_NC = None


def kernel(**inp):
    global _NC
    inp = {k: np.asarray(v) for k, v in inp.items()}
    if _NC is None:
        _NC = build()
    cols = host_cols(inp)
    s5col, s5bp = host_s5(inp)
    wsts = [host_stream(inp, l) for l in range(L)]
    wada = host_ada(inp)
    in_maps = []
    for core in range(8):
        ss = slice(core * NS, (core + 1) * NS)
        xT = np.empty((D, NTOK), np.float32)
        xT[:, :SEQ] = inp["x_prompt"][core].T
        xT[:, SEQ:] = inp["x_sample"][ss].reshape(NS * ST, D).T
        cT = np.zeros((D, 18), np.float32)
        cT[:, 0] = inp["c_prompt"][core]
        cT[:, 1:17] = inp["c_sample"][ss].T
        st_rgh = inp["state_rg_h"][ss].reshape(NS, L, 8, 128).transpose(1, 3, 2, 0)
        st_rgc = inp["state_rg_conv"][ss].reshape(NS, L, 3, 8, 128).transpose(1, 4, 3, 0, 2)
        s5 = np.stack([inp["state_s5_re"][ss], inp["state_s5_im"][ss]], 0)
        s5 = s5.reshape(2, NS, L, 16, 2, 64).transpose(2, 0, 4, 5, 3, 1).reshape(L, 2, 128, 16, NS)
        st_ffn = inp["state_ffn_conv"][ss].reshape(NS, L, 2, 48, 128).transpose(1, 4, 3, 0, 2)
        m = {"xT": xT, "cT": cT, "cols": cols, "st_rgh": np.ascontiguousarray(st_rgh),
             "st_rgc": np.ascontiguousarray(st_rgc), "st_s5": np.ascontiguousarray(s5),
             "st_ffn": np.ascontiguousarray(st_ffn), "s5col": s5col, "s5bp": s5bp,
             "wst0": wsts[0], "wst1": wsts[1], "wada": wada}
        in_maps.append(m)
    res = run_bass_kernel_spmd(_NC, in_maps, core_ids=list(range(8)))
    R = res.results
    y_p = np.empty((8, SEQ, D), np.float32)
    y_s = np.empty((128, ST, D), np.float32)
    rg_h = np.empty((8 * 17, L, D), np.float32)
    rg_c = np.empty((8 * 17, L, 3, D), np.float32)
    s5r = np.empty((8 * 17, L, 32, 64), np.float32)
    s5i = np.empty((8 * 17, L, 32, 64), np.float32)
    ffn = np.empty((8 * 17, L, 2, 2 * DFF), np.float32)
    for core in range(8):
        r = R[core]
        yT = r["yT"]
        y_p[core] = yT[:, :SEQ].T
        y_s[core * NS:(core + 1) * NS] = yT[:, SEQ:].T.reshape(NS, ST, D)
        sl = slice(core * 17, (core + 1) * 17)
        rg_h[sl] = r["o_rgh"].transpose(3, 0, 2, 1).reshape(17, L, D)
        rg_c[sl] = r["o_rgc"].transpose(3, 0, 4, 2, 1).reshape(17, L, 3, D)
        o5 = r["o_s5"].reshape(L, 2, 2, 64, 16, 17).transpose(1, 5, 0, 4, 2, 3).reshape(2, 17, L, 32, 64)
        s5r[sl] = o5[0]
        s5i[sl] = o5[1]
        ffn[sl] = r["o_ffn"].transpose(3, 0, 4, 2, 1).reshape(17, L, 2, 2 * DFF)
    idx_p = np.arange(8) * 17
    idx_s = (np.arange(8)[:, None] * 17 + 1 + np.arange(16)[None, :]).reshape(-1)
    return (y_p, y_s, rg_h[idx_p], rg_c[idx_p], s5r[idx_p], s5i[idx_p], ffn[idx_p],
            rg_h[idx_s], rg_c[idx_s], s5r[idx_s], s5i[idx_s], ffn[idx_s])
```

```python
import contextlib
import math
import numpy as np
import concourse.bass as bass
import concourse.mybir as mybir
from concourse.bass_utils import run_bass_kernel_spmd

F32 = mybir.dt.float32
F32R = mybir.dt.float32r
AF = mybir.ActivationFunctionType
ALU = mybir.AluOpType

D = 1024
SEQ = 2048
NS = 16
ST = 8
NTOK = SEQ + NS * ST
L = 2
INW = 4608
DFF = 3072
LS = 64
ENGS = ("pe", "act", "dve", "pool", "sp")
N_DMA_SEMS = 24
SELF_SYNC = True
WB = 2048
NWBUF = 4
NTMP = 12
NTMPR = 8

CO = {}
_o = 0
for _n, _w in (("g1", 8), ("g2", 8), ("bin", 36), ("rcw", 32), ("rcb", 8), ("ba", 8), ("bx", 8), ("lam", 8),
               ("s5d", 4), ("fcw", 144), ("fcb", 48), ("bada", 48)):
    CO[_n] = _o
    _o += _w
NCL = _o
NCOL = 2 * NCL + 8


class Prog:
    def __init__(self, nc, stack):
        self.nc = nc
        self.stack = stack
        self.ops = {e: [] for e in ENGS}
        self.sem = {e: stack.enter_context(nc.semaphore("s_" + e)) for e in ENGS}
        self.cnt = {e: 0 for e in ENGS}
        self.dsem = [stack.enter_context(nc.semaphore("d%d" % i)) for i in range(3 * N_DMA_SEMS)]
        self.dcnt = [0] * (3 * N_DMA_SEMS)
        self.dnext = {"sp": 0, "pool": 0, "act": 0}
        self.seen = {e: {} for e in ENGS}
        self.tiles = {}
        self.nbuf = 0

    def sb(self, shape, dtype=F32, name=None):
        self.nbuf += 1
        return self.stack.enter_context(self.nc.sbuf_tensor("S_" + (name or ("t%d" % self.nbuf)), list(shape), dtype))

    def ps(self, shape, dtype=F32, name=None):
        self.nbuf += 1
        return self.stack.enter_context(self.nc.psum_tensor("P_" + (name or ("p%d" % self.nbuf)), list(shape), dtype))

    def _semobj(self, key):
        return self.sem[key] if isinstance(key, str) else self.dsem[key]

    def _deps(self, eng, reads, writes):
        need = {}

        def req(ev):
            if ev is None:
                return
            k, v = ev
            if k == eng and (eng == "pe" or not SELF_SYNC):
                return
            if need.get(k, 0) < v:
                need[k] = v

        for t in reads:
            st = self.tiles.get(t)
            if st:
                req(st["w"])
        for t in writes:
            st = self.tiles.get(t)
            if st:
                req(st["w"])
                for ev in st["r"]:
                    req(ev)
        waits = []
        for k, v in need.items():
            if self.seen[eng].get(k, 0) < v:
                self.seen[eng][k] = v
                waits.append((k, v))
        return waits

    def _commit(self, ev, reads, writes):
        for t in reads:
            st = self.tiles.setdefault(t, {"w": None, "r": []})
            st["r"].append(ev)
            if len(st["r"]) > 16:
                best = {}
                for k, v in st["r"]:
                    if best.get(k, 0) < v:
                        best[k] = v
                st["r"] = list(best.items())
        for t in writes:
            self.tiles[t] = {"w": ev, "r": []}

    def op(self, eng, fn, reads=(), writes=()):
        waits = self._deps(eng, reads, writes)
        self.cnt[eng] += 1
        ev = (eng, self.cnt[eng])
        sem = self.sem[eng]

        def emit(e, fn=fn, waits=waits, sem=sem):
            for k, v in waits:
                e.wait_ge(self._semobj(k), v)
            fn(e).then_inc(sem, 1)

        self.ops[eng].append(emit)
        self._commit(ev, reads, writes)

    def dma(self, out, in_, reads=(), writes=(), eng="sp"):
        k = self.dnext[eng] + {"sp": 0, "pool": N_DMA_SEMS, "act": 2 * N_DMA_SEMS}[eng]
        self.dnext[eng] = (self.dnext[eng] + 1) % N_DMA_SEMS
        waits = self._deps(eng, reads, writes)
        prev = self.dcnt[k]
        if prev and self.seen[eng].get(k, 0) < prev:
            self.seen[eng][k] = prev
            waits.append((k, prev))
        self.dcnt[k] += 16
        ev = (k, self.dcnt[k])
        dsem = self.dsem[k]

        def emit(e, waits=waits, out=out, in_=in_, dsem=dsem):
            for kk, v in waits:
                e.wait_ge(self._semobj(kk), v)
            e.dma_start(out=out, in_=in_).then_inc(dsem, 16)

        self.ops[eng].append(emit)
        self._commit(ev, reads, writes)

    def finish(self):
        fin = [(k, self.dcnt[k]) for k in range(3 * N_DMA_SEMS) if self.dcnt[k]]

        def emit_fin(e):
            for k, v in fin:
                e.wait_ge(self.dsem[k], v)

        self.ops["sp"].append(emit_fin)
        nc = self.nc
        with nc.Block() as block:
            @block.tensor
            def _(e):
                for f in self.ops["pe"]:
                    f(e)

            @block.scalar
            def _(e):
                for f in self.ops["act"]:
                    f(e)

            @block.vector
            def _(e):
                for f in self.ops["dve"]:
                    f(e)

            @block.gpsimd
            def _(e):
                for f in self.ops["pool"]:
                    f(e)

            @block.sync
            def _(e):
                for f in self.ops["sp"]:
                    f(e)


def stream_tiles():
    t = []
    for c in range(8):
        t.append(("rg%d" % c, 2048))
        t.append(("rgd%d" % c, 256))
    for q in range(4):
        t.append(("sx%d" % q, 1024))
        t.append(("s5c%d" % q, 1024))
    for m in range(8):
        t.append(("mga%d" % m, 2048))
        t.append(("mgb%d" % m, 2048))
    for o in range(4):
        t.append(("wo%d" % o, 2048))
    for j in range(24):
        t.append(("up%d" % j, 2048))
    for o in range(4):
        for i in range(3):
            t.append(("dn%d_%d" % (o, i), 2048))
    return t


STILES = stream_tiles()
SOFF = {}
_o = 0
for _n, _s in STILES:
    SOFF[_n] = (_o, _s)
    _o += _s
SLEN = _o
ADA_TILES = 24


def _pack(W, kc, cols, k0=0):
    cols = np.asarray(cols)
    blk = W[k0 * 128:(k0 + kc) * 128][:, cols].reshape(kc, 128, len(cols))
    return np.ascontiguousarray(blk.transpose(1, 0, 2)).reshape(128, kc * len(cols))


def _ar(a, n=128):
    return np.arange(a, a + n)


def host_stream(inp, l):
    w_in = inp["w_in"][l]
    out = np.zeros((128, SLEN), np.float32)

    def put(name, arr):
        o, s = SOFF[name]
        assert arr.shape == (128, s), (name, arr.shape, s)
        out[:, o:o + s] = arr

    g = np.zeros((128, 8, 256), np.float32)
    for c in range(8):
        for hh in range(2):
            h = 2 * c + hh
            g[hh * 64:(hh + 1) * 64, c, hh * 64:(hh + 1) * 64] = inp["rg_wa"][l, h]
            g[hh * 64:(hh + 1) * 64, c, 128 + hh * 64:128 + (hh + 1) * 64] = inp["rg_wx"][l, h]
    for c in range(8):
        put("rgd%d" % c, np.ascontiguousarray(g[:, c, :]))
    for c in range(8):
        put("rg%d" % c, _pack(w_in, 8, np.concatenate([_ar(c * 128), _ar(1024 + c * 128)])))
    cre = inp["s5_c_re"][l]
    cim = inp["s5_c_im"][l]
    for q in range(4):
        put("sx%d" % q, _pack(w_in, 8, _ar(2048 + q * 128)))
        cc = np.zeros((128, 2, 4, 128), np.float32)
        for pp in range(4):
            for j in range(2):
                gidx = 2 * (4 * q + pp) + j
                sl = 16 * (gidx % 8)
                cc[64 * j:64 * j + 64, 0, pp, sl:sl + 16] = cre[gidx].T
                cc[64 * j:64 * j + 64, 1, pp, sl:sl + 16] = cim[gidx].T
        put("s5c%d" % q, cc.reshape(128, 1024))
    for m in range(8):
        put("mga%d" % m, _pack(w_in, 8, np.concatenate([_ar(2560 + m * 128), _ar(3584 + m * 128)])))
        a = _pack(inp["w_rg_proj"][l], 8, _ar(m * 128))
        b = _pack(inp["w_glu"][l], 4, np.concatenate([_ar(m * 128), _ar(1024 + m * 128)]))
        put("mgb%d" % m, np.concatenate([a, b], axis=1))
    for o in range(4):
        put("wo%d" % o, _pack(inp["w_out"][l], 8, _ar(o * 256, 256)))
    for j in range(24):
        put("up%d" % j, _pack(inp["w_up"][l], 8, np.concatenate([_ar(j * 128), _ar(DFF + j * 128)])))
    for o in range(4):
        for i in range(3):
            put("dn%d_%d" % (o, i), _pack(inp["w_down"][l], 8, _ar(o * 256, 256), k0=8 * i))
    return out


def host_ada(inp):
    out = np.zeros((128, L * ADA_TILES * 2048), np.float32)
    for l in range(L):
        for t in range(ADA_TILES):
            out[:, (l * ADA_TILES + t) * 2048:(l * ADA_TILES + t + 1) * 2048] = _pack(inp["w_ada"][l], 8, _ar(t * 256, 256))
    return out


def host_cols(inp):
    c = np.zeros((128, NCOL), np.float32)

    def colv(v):
        return np.asarray(v).reshape(-1, 128).T

    for l in range(L):
        b = l * NCL
        c[:, b + CO["g1"]:b + CO["g1"] + 8] = colv(inp["g_norm1"][l])
        c[:, b + CO["g2"]:b + CO["g2"] + 8] = colv(inp["g_norm2"][l])
        c[:, b + CO["bin"]:b + CO["bin"] + 36] = colv(inp["b_in"][l])
        for j in range(4):
            c[:, b + CO["rcw"] + j * 8:b + CO["rcw"] + j * 8 + 8] = colv(inp["rg_conv_w"][l, j])
        c[:, b + CO["rcb"]:b + CO["rcb"] + 8] = colv(inp["rg_conv_b"][l])
        c[:, b + CO["ba"]:b + CO["ba"] + 8] = colv(inp["rg_ba"][l])
        c[:, b + CO["bx"]:b + CO["bx"] + 8] = colv(inp["rg_bx"][l])
        c[:, b + CO["lam"]:b + CO["lam"] + 8] = colv(inp["rg_lambda"][l])
        c[:, b + CO["s5d"]:b + CO["s5d"] + 4] = colv(inp["s5_d"][l])
        for j in range(3):
            c[:, b + CO["fcw"] + j * 48:b + CO["fcw"] + j * 48 + 48] = colv(inp["ffn_conv_w"][l, j])
        c[:, b + CO["fcb"]:b + CO["fcb"] + 48] = colv(inp["ffn_conv_b"][l])
        c[:, b + CO["bada"]:b + CO["bada"] + 48] = colv(inp["b_ada"][l])
    c[:, 2 * NCL:2 * NCL + 8] = colv(inp["g_final"])
    return c


def host_s5(inp):
    col = np.zeros((L, 128, 3, 16), np.float32)
    bp = np.zeros((L, 128, 2, 16, 128), np.float32)
    for l in range(L):
        for p in range(16):
            for j in range(2):
                g = 2 * p + j
                col[l, 64 * j:64 * j + 64, 0, p] = inp["s5_lam_re"][l, g]
                col[l, 64 * j:64 * j + 64, 1, p] = inp["s5_lam_im"][l, g]
                col[l, 64 * j:64 * j + 64, 2, p] = inp["s5_log_dt"][l, g]
                sl = 16 * (g % 8)
                bp[l, sl:sl + 16, 0, p, 64 * j:64 * j + 64] = inp["s5_b_re"][l, g].T
                bp[l, sl:sl + 16, 1, p, 64 * j:64 * j + 64] = inp["s5_b_im"][l, g].T
    return col, bp.reshape(L, 128, 2, 2048)


def build():
    nc = bass.Bass("TRN2", target_bir_lowering=False)
    nc.dge_precook = False

    def din(name, shape, dt=F32):
        return nc.dram_tensor(name, list(shape), dt, kind="ExternalInput").ap()

    def dout(name, shape, dt=F32):
        return nc.dram_tensor(name, list(shape), dt, kind="ExternalOutput").ap()

    xT = din("xT", [D, NTOK])
    cT = din("cT", [D, 18])
    cols_d = din("cols", [128, NCOL])
    st_rgh = din("st_rgh", [L, 128, 8, NS])
    st_rgc = din("st_rgc", [L, 128, 8, NS, 3])
    st_s5 = din("st_s5", [L, 2, 128, 16, NS])
    st_ffn = din("st_ffn", [L, 128, 48, NS, 2])
    s5col_d = din("s5col", [L, 128, 3, 16])
    s5bp_d = din("s5bp", [L, 128, 2, 2048])
    wst = [din("wst%d" % l, [128, SLEN], F32R) for l in range(L)]
    wada = din("wada", [128, L * ADA_TILES * 2048], F32R)
    yT = dout("yT", [D, NTOK])
    o_rgh = dout("o_rgh", [L, 128, 8, 17])
    o_rgc = dout("o_rgc", [L, 128, 8, 17, 3])
    o_s5 = dout("o_s5", [L, 2, 128, 16, 17])
    o_ffn = dout("o_ffn", [L, 128, 48, 17, 2])
    s5b_d = nc.dram_tensor("s5b_scratch", [L, 128, 4, 1024], F32R, kind="Internal").ap()
    tab_d = nc.dram_tensor("tab_scratch", [L, 16, 128, 2, 512], F32, kind="Internal").ap()

    with contextlib.ExitStack() as stack:
        P = Prog(nc, stack)
        x = P.sb([128, 8, 512], F32, "x")
        h = P.sb([128, 8, 512], F32R, "h")
        R1 = P.sb([128, 24, 512], F32R, "R1")
        wbuf = [P.sb([128, WB], F32R, "wbuf%d" % i) for i in range(NWBUF)]
        tmps = [P.sb([128, 520], F32, "tmp%d" % i) for i in range(NTMP)]
        tmpr = [P.sb([128, 520], F32R, "tmpr%d" % i) for i in range(NTMPR)]
        pss = [P.ps([128, 512], F32, "ps%d" % i) for i in range(8)]
        cols = P.sb([128, NCOL], F32, "cols")
        dcol = P.sb([128, L, 64], F32, "dcol")
        mod = P.sb([128, L, 48, 18], F32, "mod")
        dmod = P.sb([128, L, 3, 8, 18], F32, "dmod")
        ones = P.sb([128, 128], F32R, "ones")
        ident = P.sb([128, 128], F32, "ident")
        cst = P.sb([128, 8], F32, "cst")
        rgh = P.sb([128, L, 8, 17], F32, "rgh")
        rgc = P.sb([128, L, 8, 17, 3], F32, "rgc")
        s5s = P.sb([128, L, 2, 16, 17], F32, "s5s")
        ffc = P.sb([128, L, 48, 17, 2], F32, "ffc")
        tbuf = [P.sb([128, 1024], F32, "tbuf%d" % i) for i in range(2)]
        dgb = [P.sb([128, 6, 128], F32R, "dgb%d" % i) for i in range(2)]
        cC = P.sb([128, 16, 8], F32, "cC")
        cS = P.sb([128, 16, 8], F32, "cS")
        rhoM = P.sb([128, L, 16, 8], F32, "rhoM")
        identR = P.sb([128, 128], F32R, "identR")
        nidentR = P.sb([128, 128], F32R, "nidentR")
        s5p = P.sb([128, L, 8, 16], F32, "s5p")
        scT = P.sb([128, 8, 18], F32R, "scT")

        free_t = list(range(NTMP))
        free_r = list(range(NTMPR))
        free_p = list(range(8))

        def talloc():
            i = free_t.pop(0)
            return i

        def tfree(*ids):
            for i in ids:
                free_t.append(i)

        def T(i):
            return tmps[i]

        def TK(i):
            return "tmp%d" % i

        def ralloc():
            return free_r.pop(0)

        def rfree(*ids):
            for i in ids:
                free_r.append(i)

        def psalloc():
            return free_p.pop(0)

        def psfree(*ids):
            for i in ids:
                free_p.append(i)

        def PK(i):
            return "ps%d" % i


        def tt(out, in0, in1, op, reads, writes, eng="dve"):
            P.op(eng, lambda e: e.tensor_tensor(out=out, in0=in0, in1=in1, op=op), reads, writes)

        def ts(out, in0, s1, s2, op0, op1, reads, writes, eng="dve"):
            if s2 is None:
                P.op(eng, lambda e: e.tensor_scalar(out=out, in0=in0, scalar1=s1, scalar2=None, op0=op0), reads, writes)
            else:
                P.op(eng, lambda e: e.tensor_scalar(out=out, in0=in0, scalar1=s1, scalar2=s2, op0=op0, op1=op1), reads, writes)

        def stt(out, in0, sc, in1, op0, op1, reads, writes):
            P.op("dve", lambda e: e.scalar_tensor_tensor(out=out, in0=in0, scalar=sc, in1=in1, op0=op0, op1=op1), reads, writes)

        def act(out, in_, func, reads, writes, scale=1.0, bias=None):
            if bias is None:
                P.op("act", lambda e: e.activation(out=out, in_=in_, func=func, scale=scale), reads, writes)
            else:
                P.op("act", lambda e: e.activation(out=out, in_=in_, func=func, scale=scale, bias=bias), reads, writes)

        def mm(out, lhsT, rhs, start, stop, reads, writes):
            P.op("pe", lambda e: e.matmul(out, lhsT, rhs, start=start, stop=stop), reads, writes)

        def cp(out, in_, reads, writes, eng="dve"):
            P.op(eng, lambda e: e.tensor_copy(out=out, in_=in_), reads, writes)

        def scan(out, d0, d1, init, reads, writes):
            P.op("dve", lambda e: e.tensor_tensor_scan(out=out, data0=d0, data1=d1, initial=init, op0=ALU.mult, op1=ALU.add),
                 reads, writes)

        def mset(ap, v, reads, writes, eng="dve"):
            P.op(eng, lambda e: e.memset(ap, v), reads, writes)

        MUL, ADD, SUB = ALU.mult, ALU.add, ALU.subtract

        wstate = {"n": 0, "nrot": NWBUF}

        def wload(src_ap, size, reads=(), eng="sp"):
            i = wstate["n"] % wstate["nrot"]
            wstate["n"] += 1
            key = "wbuf%d" % i
            wkeys = [key] + (["wbuf3a", "wbuf3b"] if i == 3 else [])
            P.dma(wbuf[i][:, 0:size], src_ap, reads=list(reads), writes=wkeys, eng=eng)
            return wbuf[i], key

        def wtile(l, name):
            o, s = SOFF[name]
            return wload(wst[l][:, o:o + s], s)

        P.dma(cols[:], cols_d, writes=["cols"])
        ti_c = talloc()
        P.dma(T(ti_c)[:, 0:144].rearrange("p (k s) -> p k s", s=18), cT.rearrange("(k p) s -> p k s", p=128), writes=[TK(ti_c)])
        mset(ident[:], 0.0, [], ["ident"], eng="pool")
        P.op("pool", lambda e: e.affine_select(out=ident[:], in_=ident[:], pattern=[[-1, 128]],
                                               compare_op=ALU.not_equal, fill=1.0, base=0, channel_multiplier=1),
             reads=["ident"], writes=["ident"])
        ts(ones[:], ident[:], 0.0, 1.0, MUL, ADD, ["ident"], ["ones"])
        ts(identR[:], ident[:], 1.0, 0.0, MUL, ADD, ["ident"], ["identR"])
        ts(nidentR[:], ident[:], -1.0, 0.0, MUL, ADD, ["ident"], ["identR"])
        mset(cst[:, 0:1], 1e-6, [], ["cst"])
        mset(cst[:, 1:2], math.pi / 2, ["cst"], ["cst"])
        mset(cst[:, 2:3], 1.0, ["cst"], ["cst"])
        mset(cst[:, 3:4], 0.0, ["cst"], ["cst"])
        mset(rgh[:], 0.0, [], ["rgh"], eng="pool")
        mset(rgc[:], 0.0, [], ["rgc"], eng="pool")
        mset(s5s[:], 0.0, [], ["s5s"], eng="pool")
        mset(ffc[:], 0.0, [], ["ffc"], eng="pool")
        for l in range(L):
            P.dma(rgh[:, l, :, 1:17], st_rgh[l], writes=["rgh"])
            P.dma(rgc[:, l, :, 1:17, :], st_rgc[l], writes=["rgc"])
            for ri in range(2):
                P.dma(s5s[:, l, ri, :, 1:17], st_s5[l, ri], writes=["s5s"])
            for c0 in range(0, 48, 8):
                P.dma(ffc[:, l, c0:c0 + 8, 1:17, :], st_ffn[l, :, c0:c0 + 8], writes=["ffc"])

        def setup_A1(l):
            a0 = talloc()
            a1 = talloc()
            K0, K1 = TK(a0), TK(a1)
            prm = T(a0)[:, 0:48].rearrange("p (a b) -> p a b", b=16)
            P.dma(prm, s5col_d[l], writes=[K0], eng="pool")
            w_ = T(a1)

            def S(i, w_=w_):
                return w_[:, i * 16:(i + 1) * 16]
            lr, li, ldt = prm[:, 0, :], prm[:, 1, :], prm[:, 2, :]
            rho, ar, ai, zr, zi = (s5p[:, l, i, :] for i in range(5))
            act(S(0), ldt, AF.Exp, [K0], [K1])
            tt(S(1), lr, S(0), MUL, [K0, K1], [K1])
            act(rho, S(1), AF.Exp, [K1], ["s5p"])
            tt(S(2), li, S(0), MUL, [K0, K1], [K1])
            act(S(3), S(2), AF.Sin, [K1], [K1], scale=1.0 / 16)
            act(S(4), S(2), AF.Sin, [K1, "cst"], [K1], scale=1.0 / 16, bias=cst[:, 1:2])
            for _ in range(4):
                tt(S(5), S(4), S(4), MUL, [K1], [K1])
                tt(S(6), S(3), S(3), MUL, [K1], [K1])
                tt(S(7), S(4), S(3), MUL, [K1], [K1])
                tt(S(4), S(5), S(6), SUB, [K1], [K1])
                ts(S(3), S(7), 2.0, None, MUL, None, [K1], [K1])
            cp(s5p[:, l, 5, :], S(4), [K1, "s5p"], ["s5p"])
            cp(s5p[:, l, 6, :], S(3), [K1, "s5p"], ["s5p"])
            tt(ar, rho, S(4), MUL, [K1, "s5p"], ["s5p"])
            tt(ai, rho, S(3), MUL, [K1, "s5p"], ["s5p"])
            tt(S(5), lr, lr, MUL, [K0], [K1])
            tt(S(6), li, li, MUL, [K0], [K1])
            tt(S(5), S(5), S(6), ADD, [K1], [K1])
            P.op("dve", lambda e, o=S(5): e.reciprocal(out=o, in_=o), [K1], [K1])
            ts(S(6), ar, -1.0, None, ADD, None, ["s5p"], [K1])
            tt(S(7), S(6), lr, MUL, [K1, K0], [K1])
            tt(S(8), ai, li, MUL, ["s5p", K0], [K1])
            tt(S(7), S(7), S(8), ADD, [K1], [K1])
            tt(zr, S(7), S(5), MUL, [K1], ["s5p"])
            tt(S(7), ai, lr, MUL, ["s5p", K0], [K1])
            tt(S(8), S(6), li, MUL, [K1, K0], [K1])
            tt(S(7), S(7), S(8), SUB, [K1], [K1])
            tt(zi, S(7), S(5), MUL, [K1], ["s5p"])
            tfree(a0, a1)

        def setup_A2(l):
            for q in range(4):
                zr_ps = psalloc()
                zi_ps = psalloc()
                for pp in range(4):
                    p = 4 * q + pp
                    for (zi_, dst) in ((3, zr_ps), (4, zi_ps)):
                        r = ralloc()
                        ts(tmpr[r][:, 0:128], ident[:], s5p[:, l, zi_, p:p + 1], None, MUL, None, ["ident", "s5p"], ["tmpr%d" % r])
                        mm(pss[dst][:, pp * 128:(pp + 1) * 128], ones[:], tmpr[r][:, 0:128], True, True,
                           ["ones", "tmpr%d" % r], [PK(dst)])
                        rfree(r)
                br = talloc()
                bi = talloc()
                P.dma(T(br)[:, 0:512], s5bp_d[l, :, 0, q * 512:(q + 1) * 512], writes=[TK(br)])
                P.dma(T(bi)[:, 0:512], s5bp_d[l, :, 1, q * 512:(q + 1) * 512], writes=[TK(bi)])
                t1 = talloc()
                t2 = talloc()
                r = ralloc()
                r2 = ralloc()
                RK = "tmpr%d" % r
                RK2 = "tmpr%d" % r2
                A, B_, U1, U2 = T(br)[:, 0:512], T(bi)[:, 0:512], T(t1)[:, 0:512], T(t2)[:, 0:512]
                tt(U1, A, pss[zr_ps][:], MUL, [TK(br), PK(zr_ps)], [TK(t1)])
                tt(U2, B_, pss[zi_ps][:], MUL, [TK(bi), PK(zi_ps)], [TK(t2)])
                tt(tmpr[r][:, 0:512], U1, U2, SUB, [TK(t1), TK(t2)], [RK])
                tt(U1, A, pss[zi_ps][:], MUL, [TK(br), PK(zi_ps)], [TK(t1)])
                tt(U2, B_, pss[zr_ps][:], MUL, [TK(bi), PK(zr_ps)], [TK(t2)])
                tt(tmpr[r2][:, 0:512], U1, U2, ADD, [TK(t1), TK(t2)], [RK2])
                P.dma(s5b_d[l, :, q, 0:512], tmpr[r][:, 0:512], reads=[RK], writes=["s5b_d"])
                P.dma(s5b_d[l, :, q, 512:1024], tmpr[r2][:, 0:512], reads=[RK2, "s5b_d"], writes=["s5b_d"])
                rfree(r, r2)
                tfree(br, bi, t1, t2)
                psfree(zr_ps, zi_ps)

        act(scT[:].rearrange("p k s -> p (k s)"), T(ti_c)[:, 0:144], AF.Silu, [TK(ti_c)], ["scT"])
        tfree(ti_c)
        ada_loaded = {}

        def ada_issue(l, t):
            ada_loaded[(l, t)] = wload(wada[:, (l * ADA_TILES + t) * 2048:(l * ADA_TILES + t + 1) * 2048], 2048,
                                       eng=("act" if l == 0 else "sp"))

        def ada_tile(l, t):
            if (l, t) not in ada_loaded:
                ada_issue(l, t)
            wb, wk = ada_loaded.pop((l, t))
            wv = wb[:, 0:2048].rearrange("p (k n) -> p k n", n=256)
            for half in range(2):
                m = 2 * t + half
                pi = psalloc()
                for k in range(8):
                    mm(pss[pi][:, 0:18], wv[:, k, half * 128:(half + 1) * 128], scT[:, k, :], k == 0, k == 7,
                       [wk, "scT"], [PK(pi)])
                bcol = l * NCL + CO["bada"] + m
                act(mod[:, l, m, :], pss[pi][:, 0:18], AF.Identity, [PK(pi), "cols"], ["mod%d" % l], bias=cols[:, bcol:bcol + 1])
                psfree(pi)

        def setup_B(l):
            baseC = tbuf[0][:, 0:1024].rearrange("p (a b) -> p a b", b=64)
            baseS = tbuf[1][:, 0:1024].rearrange("p (a b) -> p a b", b=64)
            BK = ["tbuf0", "tbuf1"]
            cp(baseC[:, :, 0], s5p[:, l, 5, :], ["s5p"] + BK, ["tbuf0"])
            cp(baseS[:, :, 0], s5p[:, l, 6, :], ["s5p"] + BK, ["tbuf1"])
            b1 = talloc()
            b2 = talloc()
            k = 1
            while k < 64:
                ec = baseC[:, :, k - 1:k].to_broadcast([128, 16, k])
                es = baseS[:, :, k - 1:k].to_broadcast([128, 16, k])
                u1 = T(b1)[:, 0:16 * k].rearrange("p (a b) -> p a b", b=k)
                u2 = T(b2)[:, 0:16 * k].rearrange("p (a b) -> p a b", b=k)
                c0 = baseC[:, :, 0:k]
                s0 = baseS[:, :, 0:k]
                tt(u1, c0, ec, MUL, BK, [TK(b1)])
                tt(u2, s0, es, MUL, BK, [TK(b2)])
                tt(baseC[:, :, k:2 * k], u1, u2, SUB, [TK(b1), TK(b2)] + BK, ["tbuf0"])
                tt(u1, s0, ec, MUL, BK, [TK(b1)])
                tt(u2, c0, es, MUL, BK, [TK(b2)])
                tt(baseS[:, :, k:2 * k], u1, u2, ADD, [TK(b1), TK(b2)] + BK, ["tbuf1"])
                k *= 2
            CK = ["cCS"]
            mset(cC[:, :, 0], 1.0, CK, CK)
            mset(cS[:, :, 0], 0.0, CK, CK)
            cp(cC[:, :, 1], baseC[:, :, 63], BK + CK, CK)
            cp(cS[:, :, 1], baseS[:, :, 63], BK + CK, CK)
            v1 = T(b1)[:, 0:16]
            v2 = T(b2)[:, 0:16]
            for a in range(2, 8):
                tt(v1, cC[:, :, a - 1], cC[:, :, 1], MUL, CK, [TK(b1)])
                tt(v2, cS[:, :, a - 1], cS[:, :, 1], MUL, CK, [TK(b2)])
                tt(cC[:, :, a], v1, v2, SUB, [TK(b1), TK(b2)] + CK, CK)
                tt(v1, cS[:, :, a - 1], cC[:, :, 1], MUL, CK, [TK(b1)])
                tt(v2, cC[:, :, a - 1], cS[:, :, 1], MUL, CK, [TK(b2)])
                tt(cS[:, :, a], v1, v2, ADD, [TK(b1), TK(b2)] + CK, CK)
            tfree(b1, b2)
            mset(rhoM[:, l, :, 0:1], 0.0, ["rhoM"], ["rhoM"])
            cp(rhoM[:, l, :, 1:8], s5p[:, l, 0, :].unsqueeze(2).to_broadcast([128, 16, 7]), ["s5p", "rhoM"], ["rhoM"])
            for p in range(16):
                ccb = cC[:, p, :].unsqueeze(2).to_broadcast([128, 8, 64])
                scb = cS[:, p, :].unsqueeze(2).to_broadcast([128, 8, 64])
                cbb = baseC[:, p, :].unsqueeze(1).to_broadcast([128, 8, 64])
                sbb = baseS[:, p, :].unsqueeze(1).to_broadcast([128, 8, 64])
                m1, m2, m3, m4, fc, fs = (talloc() for _ in range(6))

                def W3(i):
                    return T(i)[:, 0:512].rearrange("p (a b) -> p a b", b=64)
                tt(W3(m1), ccb, cbb, MUL, BK + CK, [TK(m1)])
                tt(W3(m2), scb, sbb, MUL, BK + CK, [TK(m2)])
                tt(T(fc)[:, 0:512], T(m1)[:, 0:512], T(m2)[:, 0:512], SUB, [TK(m1), TK(m2)], [TK(fc)])
                tt(W3(m3), scb, cbb, MUL, BK + CK, [TK(m3)])
                tt(W3(m4), ccb, sbb, MUL, BK + CK, [TK(m4)], eng="pool")
                tt(T(fs)[:, 0:512], T(m3)[:, 0:512], T(m4)[:, 0:512], ADD, [TK(m3), TK(m4)], [TK(fs)], eng="pool")
                P.dma(tab_d[l, p, :, 0, :], T(fc)[:, 0:512], reads=[TK(fc)], writes=["tab_d"])
                P.dma(tab_d[l, p, :, 1, :], T(fs)[:, 0:512], reads=[TK(fs), "tab_d"], writes=["tab_d"])
                tfree(m1, m2, m3, m4, fc, fs)

        def dmod_l(l):
            for c in range(8):
                g1c = l * NCL + CO["g1"] + c
                g2c = l * NCL + CO["g2"] + c
                ts(dmod[:, l, 0, c, :], mod[:, l, 8 + c, :], 1.0, cols[:, g1c:g1c + 1], ADD, MUL, ["mod%d" % l, "cols"], ["dmod%d" % l])
                ts(dmod[:, l, 1, c, :], mod[:, l, 16 + c, :], 0.5, None, MUL, None, ["mod%d" % l], ["dmod%d" % l])
                ts(dmod[:, l, 2, c, :], mod[:, l, 32 + c, :], 1.0, cols[:, g2c:g2c + 1], ADD, MUL, ["mod%d" % l, "cols"], ["dmod%d" % l])


        for l in range(L):
            setup_A1(l)
        for l in range(L):
            b = l * NCL
            for (dst, src) in ((0, CO["ba"]), (8, CO["bx"]), (32, CO["bin"] + 20), (40, CO["bin"] + 28)):
                ts(dcol[:, l, dst:dst + 8], cols[:, b + src:b + src + 8], 0.5, None, MUL, None, ["cols"], ["dcol"])
            act(dcol[:, l, 48:56], cols[:, b + CO["lam"]:b + CO["lam"] + 8], AF.Exp, ["cols"], ["dcol"], scale=-1.0)
            act(dcol[:, l, 56:64], dcol[:, l, 48:56], AF.Ln, ["dcol", "cst"], ["dcol"], bias=cst[:, 2:3])
            ts(dcol[:, l, 16:24], dcol[:, l, 56:64], -8.0, None, MUL, None, ["dcol"], ["dcol"])
            ts(dcol[:, l, 24:32], dcol[:, l, 56:64], -4.0, None, MUL, None, ["dcol"], ["dcol"])

        for l in range(L):
            setup_A2(l)
        for t in range(3):
            ada_issue(0, t)
        for t in range(ADA_TILES):
            if t + 3 < ADA_TILES:
                ada_issue(0, t + 3)
            ada_tile(0, t)
        for l in range(L):
            setup_B(l)
        dmod_l(0)
        ada_pending = [(1, t) for t in range(ADA_TILES)]

        def ada_step():
            if ada_pending:
                ada_tile(*ada_pending.pop(0))
                if not ada_pending:
                    dmod_l(1)

        tstate = {"n": 0}

        def tabload(l, p, small):
            i = tstate["n"] % 2
            tstate["n"] += 1
            key = "tbuf%d" % i
            if small:
                P.dma(tbuf[i][:, 0:16].rearrange("p (a n) -> p a n", a=2), tab_d[l, p, :, :, 0:8], reads=["tab_d"], writes=[key])
            else:
                P.dma(tbuf[i][:, 0:1024].rearrange("p (a n) -> p a n", a=2), tab_d[l, p], reads=["tab_d"], writes=[key])
            return tbuf[i], key

        dstate = {"n": 0}

        def mkdiag(wcols):
            i = dstate["n"] % 2
            dstate["n"] += 1
            key = "dgb%d" % i
            for t, wc in enumerate(wcols):
                ts(dgb[i][:, t, :], ident[:], wc, 0.0, MUL, ADD, ["ident", "cols"], [key], eng="pool")
            return dgb[i], key

        def run_tile(col0, N, nseq, slen, seq0):
            prompt = nseq == 1

            def V3(ap2):
                return ap2.rearrange("p (s t) -> p s t", t=slen)

            def DV(ap2):
                return ap2 if prompt else V3(ap2)

            def bc_seq(ap_pn):
                return ap_pn.unsqueeze(2).to_broadcast([128, nseq, slen])

            for c in range(8):
                P.dma(x[:, c, 0:N], xT[c * 128:(c + 1) * 128, col0:col0 + N], writes=["x%d" % c], eng="pool")

            def rmsnorm_rstd():
                pi = psalloc()
                for c in range(8):
                    r = ralloc()
                    act(tmpr[r][:, 0:N], x[:, c, 0:N], AF.Square, ["x%d" % c], ["tmpr%d" % r])
                    mm(pss[pi][:, 0:N], ones[:], tmpr[r][:, 0:N], c == 0, c == 7, ["ones", "tmpr%d" % r], [PK(pi)])
                    rfree(r)
                t1 = talloc()
                rs = talloc()
                act(T(t1)[:, 0:N], pss[pi][:, 0:N], AF.Ln, [PK(pi), "cst"], [TK(t1)], scale=1.0 / D, bias=cst[:, 0:1])
                act(T(rs)[:, 0:N], T(t1)[:, 0:N], AF.Exp, [TK(t1)], [TK(rs)], scale=-0.5)
                tfree(t1)
                psfree(pi)
                return rs

            def modulate(l, ai_, shift_chunk0):
                rs = rmsnorm_rstd()
                for c in range(8):
                    t1 = talloc()
                    eng = "pool" if c in (2, 5, 7) else "dve"
                    tt(T(t1)[:, 0:N], x[:, c, 0:N], T(rs)[:, 0:N], MUL, ["x%d" % c, TK(rs)], [TK(t1)], eng=eng)
                    if prompt:
                        act(h[:, c, 0:N], T(t1)[:, 0:N], AF.Identity, [TK(t1), "dmod%d" % l, "mod%d" % l], ["h%d" % c],
                            scale=dmod[:, l, ai_, c, 0:1], bias=mod[:, l, shift_chunk0 + c, 0:1])
                    else:
                        tt(V3(T(t1)[:, 0:N]), V3(T(t1)[:, 0:N]), bc_seq(dmod[:, l, ai_, c, seq0:seq0 + nseq]), MUL,
                           [TK(t1), "dmod%d" % l], [TK(t1)])
                        tt(V3(h[:, c, 0:N]), V3(T(t1)[:, 0:N]), bc_seq(mod[:, l, shift_chunk0 + c, seq0:seq0 + nseq]), ADD,
                           [TK(t1), "mod%d" % l], ["h%d" % c])
                    tfree(t1)
                tfree(rs)

            def resid_add(l, o, pi, gate18):
                if prompt:
                    stt(x[:, o, 0:N], pss[pi][:, 0:N], gate18[:, 0:1], x[:, o, 0:N], MUL, ADD,
                        [PK(pi), "x%d" % o, "mod%d" % l, "dmod%d" % l], ["x%d" % o])
                else:
                    t1 = talloc()
                    tt(V3(T(t1)[:, 0:N]), V3(pss[pi][:, 0:N]), bc_seq(gate18[:, seq0:seq0 + nseq]), MUL,
                       [PK(pi), "mod%d" % l, "dmod%d" % l], [TK(t1)])
                    tt(x[:, o, 0:N], x[:, o, 0:N], T(t1)[:, 0:N], ADD, [TK(t1), "x%d" % o], ["x%d" % o])
                    tfree(t1)

            for l in range(L):
                cb = l * NCL

                def col(name, i, cb=cb):
                    return cols[:, cb + CO[name] + i:cb + CO[name] + i + 1]

                modulate(l, 0, 0)

                rgst = {}
                rgst2 = {}

                def rg_A(c):
                    wb, wk = wtile(l, "rg%d" % c)
                    wv = wb[:, 0:2048].rearrange("p (k n) -> p k n", n=256)
                    p_rx = psalloc()
                    p_ry = psalloc()
                    for k in range(8):
                        mm(pss[p_rx][:, 0:N], wv[:, k, 0:128], h[:, k, 0:N], k == 0, k == 7, [wk, "h%d" % k], [PK(p_rx)])
                    for k in range(8):
                        mm(pss[p_ry][:, 0:N], wv[:, k, 128:256], h[:, k, 0:N], k == 0, k == 7, [wk, "h%d" % k], [PK(p_ry)])
                    ex = ralloc()
                    EK = "tmpr%d" % ex
                    exv = tmpr[ex][:, 0:nseq * (slen + 4)].rearrange("p (s t) -> p s t", t=slen + 4)
                    act(exv[:, :, 0:3], rgc[:, l, c, seq0:seq0 + nseq, :], AF.Identity, ["rgc", EK], [EK])
                    act(exv[:, :, 3:3 + slen], V3(pss[p_rx][:, 0:N]), AF.Identity, [PK(p_rx), "cols", EK], [EK], bias=col("bin", c))
                    act(rgc[:, l, c, seq0:seq0 + nseq, :], V3(pss[p_rx][:, 0:N])[:, :, slen - 3:slen], AF.Identity,
                        [PK(p_rx), "cols", "rgc"], ["rgc"], bias=col("bin", c))
                    gy = talloc()
                    act(T(gy)[:, 0:N], pss[p_ry][:, 0:N], AF.Gelu_apprx_tanh, [PK(p_ry), "cols"], [TK(gy)], bias=col("bin", 8 + c))
                    psfree(p_rx, p_ry)
                    dg, dk = mkdiag([col("rcw", j * 8 + c) for j in range(4)])
                    rgst[c] = (ex, gy, dg, dk)

                def rg_B(c):
                    ex, gy, dg, dk = rgst.pop(c)
                    EK = "tmpr%d" % ex
                    exv = tmpr[ex][:, 0:nseq * (slen + 4)].rearrange("p (s t) -> p s t", t=slen + 4)
                    gw, gk = wtile(l, "rgd%d" % c)
                    p_c = psalloc()
                    for j in range(4):
                        mm(V3(pss[p_c][:, 0:N]), dg[:, j, :], exv[:, :, j:j + slen], j == 0, j == 3, [dk, EK], [PK(p_c)])
                    xc = ralloc()
                    XK = "tmpr%d" % xc
                    act(tmpr[xc][:, 0:N], pss[p_c][:, 0:N], AF.Identity, [PK(p_c), "cols"], [XK], bias=col("rcb", c))
                    rfree(ex)
                    psfree(p_c)
                    xcf = tmpr[xc][:, 0:N].bitcast(F32)
                    p_a = psalloc()
                    p_x = psalloc()
                    mm(pss[p_a][:, 0:N], gw[:, 0:128], tmpr[xc][:, 0:N], True, True, [gk, XK], [PK(p_a)])
                    mm(pss[p_x][:, 0:N], gw[:, 128:256], tmpr[xc][:, 0:N], True, True, [gk, XK], [PK(p_x)])
                    tr_ = talloc()
                    ti_ = talloc()
                    TR, TI, GY = T(tr_)[:, 0:N], T(ti_)[:, 0:N], T(gy)[:, 0:N]
                    act(TR, pss[p_a][:, 0:N], AF.Tanh, [PK(p_a), "dcol"], [TK(tr_)], scale=0.5, bias=dcol[:, l, c:c + 1])
                    act(TI, pss[p_x][:, 0:N], AF.Tanh, [PK(p_x), "dcol"], [TK(ti_)], scale=0.5, bias=dcol[:, l, 8 + c:9 + c])
                    psfree(p_a, p_x)
                    a_ = talloc()
                    a2 = talloc()
                    A_, A2 = T(a_)[:, 0:N], T(a2)[:, 0:N]
                    act(A_, TR, AF.Exp, [TK(tr_), "dcol"], [TK(a_)], scale=dcol[:, l, 24 + c:25 + c], bias=dcol[:, l, 24 + c:25 + c])
                    act(A2, TR, AF.Exp, [TK(tr_), "dcol"], [TK(a2)], scale=dcol[:, l, 16 + c:17 + c], bias=dcol[:, l, 16 + c:17 + c])
                    act(A2, A2, AF.Ln, [TK(a2), "cst"], [TK(a2)], scale=-1.0, bias=cst[:, 2:3])
                    act(A2, A2, AF.Exp, [TK(a2)], [TK(a2)], scale=0.5)
                    rgst2[c] = (xc, tr_, ti_, gy, a_, a2)

                def rg_B2(c):
                    xc, tr_, ti_, gy, a_, a2 = rgst2.pop(c)
                    XK = "tmpr%d" % xc
                    xcf = tmpr[xc][:, 0:N].bitcast(F32)
                    TR, TI, GY = T(tr_)[:, 0:N], T(ti_)[:, 0:N], T(gy)[:, 0:N]
                    A_, A2 = T(a_)[:, 0:N], T(a2)[:, 0:N]
                    stt(TI, TI, 1.0, xcf, ADD, MUL, [TK(ti_), XK], [TK(ti_)])
                    stt(TI, TI, 0.5, A2, MUL, MUL, [TK(ti_), TK(a2)], [TK(ti_)])
                    rfree(xc)
                    if prompt:
                        scan(TR, A_, TI, rgh[:, l, c, seq0:seq0 + 1], [TK(a_), TK(ti_), "rgh", TK(tr_)], [TK(tr_)])
                    else:
                        a3, b3 = V3(A_), V3(TI)
                        t0 = talloc()
                        t0v = T(t0)[:, 0:nseq]
                        tt(t0v, a3[:, :, 0], rgh[:, l, c, seq0:seq0 + nseq], MUL, [TK(a_), "rgh"], [TK(t0)], eng="pool")
                        tt(b3[:, :, 0], b3[:, :, 0], t0v, ADD, [TK(ti_), TK(t0)], [TK(ti_)], eng="pool")
                        mset(a3[:, :, 0], 0.0, [TK(a_), TK(t0)], [TK(a_)], eng="pool")
                        tfree(t0)
                        scan(TR, A_, TI, 0.0, [TK(a_), TK(ti_), TK(tr_)], [TK(tr_)])
                    cp(rgh[:, l, c, seq0:seq0 + nseq], V3(TR)[:, :, slen - 1], [TK(tr_), "rgh"], ["rgh"], eng="pool")
                    tt(R1[:, c, 0:N], TR, GY, MUL, [TK(tr_), TK(gy)], ["R%d" % c])
                    tfree(tr_, ti_, gy, a_, a2)

                s5st = {}
                qst = {}

                def s5_A(p):
                    q, pp = divmod(p, 4)
                    if pp == 0:
                        wb, wk = wtile(l, "sx%d" % q)
                        wv = wb[:, 0:1024].rearrange("p (k n) -> p k n", n=128)
                        p_sx = psalloc()
                        for k in range(8):
                            mm(pss[p_sx][:, 0:N], wv[:, k, :], h[:, k, 0:N], k == 0, k == 7, [wk, "h%d" % k], [PK(p_sx)])
                        sx = ralloc()
                        SXK = "tmpr%d" % sx
                        act(tmpr[sx][:, 0:N], pss[p_sx][:, 0:N], AF.Identity, [PK(p_sx), "cols"], [SXK], bias=col("bin", 16 + q))
                        psfree(p_sx)
                        bwb, bwk = wbuf[3], "wbuf3a"
                        P.dma(bwb[:, 0:1024], s5b_d[l, :, q, :], reads=["s5b_d"], writes=[bwk, "wbuf3"])
                        qst[q] = dict(sx=sx, bwb=bwb, bwk=bwk, p_y=psalloc())
                    Q = qst[q]
                    sx = Q["sx"]
                    SXK = "tmpr%d" % sx
                    bwv = Q["bwb"][:, 0:1024].rearrange("p (a b n) -> p a b n", a=2, n=128)
                    p_vr = psalloc()
                    p_vi = psalloc()
                    mm(pss[p_vr][:, 0:N], bwv[:, 0, pp, :], tmpr[sx][:, 0:N], True, True, [Q["bwk"], SXK], [PK(p_vr)])
                    mm(pss[p_vi][:, 0:N], bwv[:, 1, pp, :], tmpr[sx][:, 0:N], True, True, [Q["bwk"], SXK], [PK(p_vi)])
                    tb, tk = tabload(l, p, not prompt)
                    s5st[p] = (p_vr, p_vi, tb, tk)

                def s5_B(p):
                    q, pp = divmod(p, 4)
                    Q = qst[q]
                    p_vr, p_vi, tb, tk = s5st.pop(p)
                    if pp == 0:
                        cwb, cwk = wbuf[3][:, 1024:2048], "wbuf3b"
                        o_, s_ = SOFF["s5c%d" % q]
                        P.dma(cwb, wst[l][:, o_:o_ + s_], writes=[cwk, "wbuf3"])
                        act(cwb[:, 512:1024], cwb[:, 512:1024].bitcast(F32), AF.Identity, [cwk], [cwk], scale=-1.0)
                        Q["cwb"], Q["cwk"] = cwb, cwk
                    if prompt:
                        Cv, Sv = tb[:, 0:N], tb[:, 512:512 + N]
                        cL, sL = tb[:, N - 1:N], tb[:, 512 + N - 1:512 + N]
                    else:
                        Cv = tb[:, 0:8].unsqueeze(1).to_broadcast([128, nseq, 8])
                        Sv = tb[:, 8:16].unsqueeze(1).to_broadcast([128, nseq, 8])
                        cL, sL = tb[:, 7:8], tb[:, 15:16]
                    t1, t2, wr, wi = talloc(), talloc(), talloc(), talloc()
                    K1_, K2_, KR, KI = TK(t1), TK(t2), TK(wr), TK(wi)
                    T1, T2, WR, WI = T(t1)[:, 0:N], T(t2)[:, 0:N], T(wr)[:, 0:N], T(wi)[:, 0:N]
                    vr, vi = DV(pss[p_vr][:, 0:N]), DV(pss[p_vi][:, 0:N])
                    tt(DV(T1), vr, Cv, MUL, [PK(p_vr), tk], [K1_])
                    tt(DV(T2), vi, Sv, MUL, [PK(p_vi), tk], [K2_])
                    tt(WR, T1, T2, ADD, [K1_, K2_], [KR])
                    tt(DV(T1), vi, Cv, MUL, [PK(p_vi), tk], [K1_])
                    tt(DV(T2), vr, Sv, MUL, [PK(p_vr), tk], [K2_])
                    tt(WI, T1, T2, SUB, [K1_, K2_], [KI])
                    psfree(p_vr, p_vi)
                    if prompt:
                        rho_b = s5p[:, l, 0, p:p + 1].to_broadcast([128, N])
                        scan(T1, rho_b, WR, s5s[:, l, 0, p, seq0:seq0 + 1], [KR, "s5s", "s5p", K1_], [K1_])
                        scan(T2, rho_b, WI, s5s[:, l, 1, p, seq0:seq0 + 1], [KI, "s5s", "s5p", K2_], [K2_])
                        zr_e, zi_e = T1[:, N - 1:N], T2[:, N - 1:N]
                        nst = 1
                    else:
                        rm = talloc()
                        RM = T(rm)[:, 0:N]
                        cp(V3(RM), rhoM[:, l, p, :].unsqueeze(1).to_broadcast([128, nseq, 8]), ["rhoM"], [TK(rm)], eng="pool")
                        stt(V3(WR)[:, :, 0], s5s[:, l, 0, p, seq0:seq0 + nseq], s5p[:, l, 0, p:p + 1], V3(WR)[:, :, 0], MUL, ADD,
                            [KR, "s5s", "s5p"], [KR])
                        stt(V3(WI)[:, :, 0], s5s[:, l, 1, p, seq0:seq0 + nseq], s5p[:, l, 0, p:p + 1], V3(WI)[:, :, 0], MUL, ADD,
                            [KI, "s5s", "s5p"], [KI])
                        scan(T1, RM, WR, 0.0, [KR, TK(rm), K1_], [K1_])
                        scan(T2, RM, WI, 0.0, [KI, TK(rm), K2_], [K2_])
                        tfree(rm)
                        zr_e, zi_e = V3(T1)[:, :, 7], V3(T2)[:, :, 7]
                        nst = nseq
                    tfree(wr, wi)
                    us = [ralloc() for _ in range(4)]
                    UK = ["tmpr%d" % u for u in us]
                    U = [tmpr[u][:, 0:N] for u in us]
                    tt(DV(U[0]), DV(T1), Cv, MUL, [K1_, tk], [UK[0]])
                    stt(DV(U[1]), DV(T2), -1.0, Sv, MUL, MUL, [K2_, tk], [UK[1]])
                    tt(DV(U[2]), DV(T2), Cv, MUL, [K2_, tk], [UK[2]], eng="pool")
                    tt(DV(U[3]), DV(T1), Sv, MUL, [K1_, tk], [UK[3]], eng="pool")
                    e1 = talloc()
                    E = T(e1)
                    EK1 = TK(e1)
                    ts(E[:, 0:nst], zr_e, cL, 0.0, MUL, ADD, [K1_, tk], [EK1], eng="pool")
                    ts(E[:, 32:32 + nst], zi_e, sL, 0.0, MUL, ADD, [K2_, tk, EK1], [EK1], eng="pool")
                    tt(s5s[:, l, 0, p, seq0:seq0 + nst], E[:, 0:nst], E[:, 32:32 + nst], SUB, [EK1, "s5s"], ["s5s"], eng="pool")
                    ts(E[:, 64:64 + nst], zi_e, cL, 0.0, MUL, ADD, [K2_, tk, EK1], [EK1], eng="pool")
                    ts(E[:, 96:96 + nst], zr_e, sL, 0.0, MUL, ADD, [K1_, tk, EK1], [EK1], eng="pool")
                    tt(s5s[:, l, 1, p, seq0:seq0 + nst], E[:, 64:64 + nst], E[:, 96:96 + nst], ADD, [EK1, "s5s"], ["s5s"], eng="pool")
                    tfree(e1)
                    tfree(t1, t2)
                    cwv = Q["cwb"].rearrange("p (a b n) -> p a b n", a=2, n=128)
                    p_y = Q["p_y"]
                    mm(pss[p_y][:, 0:N], cwv[:, 0, pp, :], U[0], pp == 0, False, [Q["cwk"], UK[0]], [PK(p_y)])
                    mm(pss[p_y][:, 0:N], cwv[:, 0, pp, :], U[1], False, False, [Q["cwk"], UK[1]], [PK(p_y)])
                    mm(pss[p_y][:, 0:N], cwv[:, 1, pp, :], U[2], False, False, [Q["cwk"], UK[2]], [PK(p_y)])
                    mm(pss[p_y][:, 0:N], cwv[:, 1, pp, :], U[3], False, pp == 3, [Q["cwk"], UK[3]], [PK(p_y)])
                    rfree(*us)
                    if pp == 3:
                        sx = Q["sx"]
                        ty = talloc()
                        stt(T(ty)[:, 0:N], tmpr[sx][:, 0:N].bitcast(F32), col("s5d", q), pss[p_y][:, 0:N], MUL, ADD,
                            ["tmpr%d" % sx, "cols", PK(p_y)], [TK(ty)])
                        act(R1[:, 8 + q, 0:N], T(ty)[:, 0:N], AF.Gelu_apprx_tanh, [TK(ty)], ["R%d" % (8 + q)])
                        tfree(ty)
                        rfree(sx)
                        psfree(p_y)
                        del qst[q]

                wstate["nrot"] = 3
                rg_A(0)
                s5_A(0)
                for c in range(8):
                    rg_B(c)
                    if c + 1 < 8:
                        rg_A(c + 1)
                    for p in (2 * c, 2 * c + 1):
                        if p + 1 < 16:
                            s5_A(p + 1)
                        s5_B(p)
                    rg_B2(c)
                    ada_step()
                wstate["nrot"] = NWBUF

                for m in range(8):
                    wa_, wak = wtile(l, "mga%d" % m)
                    wav = wa_[:, 0:2048].rearrange("p (k n) -> p k n", n=256)
                    p_ga = psalloc()
                    p_gb = psalloc()
                    for (pi, off) in ((p_ga, 0), (p_gb, 128)):
                        for k in range(8):
                            mm(pss[pi][:, 0:N], wav[:, k, off:off + 128], h[:, k, 0:N], k == 0, k == 7, [wak, "h%d" % k], [PK(pi)])
                    wb_, wbk = wtile(l, "mgb%d" % m)
                    wrg = wb_[:, 0:1024].rearrange("p (k n) -> p k n", n=128)
                    wgl = wb_[:, 1024:2048].rearrange("p (k n) -> p k n", n=256)
                    p_ba = psalloc()
                    p_la = psalloc()
                    p_lb = psalloc()
                    for k in range(8):
                        mm(pss[p_ba][:, 0:N], wrg[:, k, :], R1[:, k, 0:N], k == 0, k == 7, [wbk, "R%d" % k], [PK(p_ba)])
                    for (pi, off) in ((p_la, 0), (p_lb, 128)):
                        for k in range(4):
                            mm(pss[pi][:, 0:N], wgl[:, k, off:off + 128], R1[:, 8 + k, 0:N], k == 0, k == 3,
                               [wbk, "R%d" % (8 + k)], [PK(pi)])
                    tga = talloc()
                    tgb = talloc()
                    tgl = talloc()
                    GA, GB, GL = T(tga)[:, 0:N], T(tgb)[:, 0:N], T(tgl)[:, 0:N]
                    act(GA, pss[p_ga][:, 0:N], AF.Tanh, [PK(p_ga), "dcol"], [TK(tga)], scale=0.5, bias=dcol[:, l, 32 + m:33 + m])
                    act(GB, pss[p_gb][:, 0:N], AF.Tanh, [PK(p_gb), "dcol"], [TK(tgb)], scale=0.5, bias=dcol[:, l, 40 + m:41 + m])
                    act(GL, pss[p_lb][:, 0:N], AF.Tanh, [PK(p_lb)], [TK(tgl)], scale=0.5)
                    psfree(p_ga, p_gb, p_lb)
                    stt(GA, GA, 1.0, pss[p_ba][:, 0:N], ADD, MUL, [TK(tga), PK(p_ba)], [TK(tga)])
                    stt(GL, GL, 1.0, pss[p_la][:, 0:N], ADD, MUL, [TK(tgl), PK(p_la)], [TK(tgl)])
                    psfree(p_ba, p_la)
                    stt(GB, GB, 1.0, GL, ADD, MUL, [TK(tgb), TK(tgl)], [TK(tgb)])
                    stt(R1[:, 12 + m, 0:N], GB, 0.5, GA, MUL, ADD, [TK(tgb), TK(tga)], ["R%d" % (12 + m)])
                    tfree(tga, tgb, tgl)
                    ada_step()

                for o2 in range(4):
                    wb, wk = wtile(l, "wo%d" % o2)
                    wv = wb[:, 0:2048].rearrange("p (k n) -> p k n", n=256)
                    for hh in range(2):
                        o = 2 * o2 + hh
                        pi = psalloc()
                        for k in range(8):
                            mm(pss[pi][:, 0:N], wv[:, k, hh * 128:(hh + 1) * 128], R1[:, 12 + k, 0:N], k == 0, k == 7,
                               [wk, "R%d" % (12 + k)], [PK(pi)])
                        resid_add(l, o, pi, dmod[:, l, 1, o, :])
                        psfree(pi)

                modulate(l, 2, 24)
                fst = {}

                def ffn_A(j):
                    wb, wk = wtile(l, "up%d" % j)
                    wv = wb[:, 0:2048].rearrange("p (k n) -> p k n", n=256)
                    st_ = []
                    for half in range(2):
                        cidx = j + 24 * half
                        pi = psalloc()
                        for k in range(8):
                            mm(pss[pi][:, 0:N], wv[:, k, half * 128:(half + 1) * 128], h[:, k, 0:N], k == 0, k == 7,
                               [wk, "h%d" % k], [PK(pi)])
                        ex = talloc()
                        EK = TK(ex)
                        exv = T(ex)[:, 0:nseq * (slen + 2)].rearrange("p (s t) -> p s t", t=slen + 2)
                        cp(exv[:, :, 0:2], ffc[:, l, cidx, seq0:seq0 + nseq, :], ["ffc", EK], [EK], eng="pool")
                        act(exv[:, :, 2:2 + slen], V3(pss[pi][:, 0:N]), AF.Identity, [PK(pi), EK], [EK])
                        act(ffc[:, l, cidx, seq0:seq0 + nseq, :], V3(pss[pi][:, 0:N])[:, :, slen - 2:slen], AF.Identity,
                            [PK(pi), "ffc"], ["ffc"])
                        acc = talloc()
                        AK = TK(acc)
                        a3 = V3(T(acc)[:, 0:N])
                        act(a3, exv[:, :, 0:slen], AF.Identity, [EK, "cols"], [AK], scale=col("fcw", cidx), bias=col("fcb", cidx))
                        st_.append((pi, ex, acc))
                    fst[j] = st_

                def ffn_B(j):
                    st_ = fst.pop(j)
                    accs = []
                    for half in range(2):
                        cidx = j + 24 * half
                        pi, ex, acc = st_[half]
                        EK, AK = TK(ex), TK(acc)
                        exv = T(ex)[:, 0:nseq * (slen + 2)].rearrange("p (s t) -> p s t", t=slen + 2)
                        a3 = V3(T(acc)[:, 0:N])
                        stt(a3, exv[:, :, 1:1 + slen], col("fcw", 48 + cidx), a3, MUL, ADD, [EK, "cols", AK], [AK])
                        stt(T(acc)[:, 0:N], pss[pi][:, 0:N], col("fcw", 96 + cidx), T(acc)[:, 0:N], MUL, ADD, [PK(pi), "cols", AK], [AK])
                        psfree(pi)
                        tfree(ex)
                        accs.append(acc)
                    ua, ub = accs
                    act(T(ua)[:, 0:N], T(ua)[:, 0:N], AF.Gelu_apprx_tanh, [TK(ua)], [TK(ua)])
                    tt(R1[:, j, 0:N], T(ua)[:, 0:N], T(ub)[:, 0:N], MUL, [TK(ua), TK(ub)], ["R%d" % j])
                    tfree(ua, ub)

                ffn_A(0)
                for j in range(24):
                    if j + 1 < 24:
                        ffn_A(j + 1)
                    ffn_B(j)
                    if j % 3 == 2:
                        ada_step()

                for o2 in range(4):
                    pis = [psalloc(), psalloc()]
                    for i in range(3):
                        wb, wk = wtile(l, "dn%d_%d" % (o2, i))
                        wv = wb[:, 0:2048].rearrange("p (k n) -> p k n", n=256)
                        for hh in range(2):
                            for k in range(8):
                                kk = 8 * i + k
                                mm(pss[pis[hh]][:, 0:N], wv[:, k, hh * 128:(hh + 1) * 128], R1[:, kk, 0:N], kk == 0, kk == 23,
                                   [wk, "R%d" % kk], [PK(pis[hh])])
                    for hh in range(2):
                        resid_add(l, 2 * o2 + hh, pis[hh], mod[:, l, 40 + 2 * o2 + hh, :])
                    psfree(*pis)

            rs = rmsnorm_rstd()
            for c in range(8):
                t1 = talloc()
                tt(T(t1)[:, 0:N], x[:, c, 0:N], T(rs)[:, 0:N], MUL, ["x%d" % c, TK(rs)], [TK(t1)])
                act(T(t1)[:, 0:N], T(t1)[:, 0:N], AF.Identity, [TK(t1), "cols"], [TK(t1)], scale=cols[:, 2 * NCL + c:2 * NCL + c + 1])
                P.dma(yT[c * 128:(c + 1) * 128, col0:col0 + N], T(t1)[:, 0:N], reads=[TK(t1)], eng="pool")
                tfree(t1)
            tfree(rs)

        tiles = [(512 * i, 512, 1, 512, 0) for i in range(4)] + [(SEQ, NS * ST, NS, ST, 1)]
        for tile_ in tiles:
            run_tile(*tile_)

        for l in range(L):
            P.dma(o_rgh[l], rgh[:, l], reads=["rgh"], eng="pool")
            P.dma(o_rgc[l].rearrange("p c s j -> p (c s j)"), rgc[:, l].rearrange("p c s j -> p (c s j)"), reads=["rgc"], eng="pool")
            for ri in range(2):
                P.dma(o_s5[l, ri], s5s[:, l, ri], reads=["s5s"], eng="pool")
            P.dma(o_ffn[l].rearrange("p c s j -> p (c s j)"), ffc[:, l].rearrange("p c s j -> p (c s j)"), reads=["ffc"], eng="pool")
        P.finish()
    return nc


_NC = None


def kernel(**inp):
    global _NC
    inp = {k: np.asarray(v) for k, v in inp.items()}
    if _NC is None:
        _NC = build()
    cols = host_cols(inp)
    s5col, s5bp = host_s5(inp)
    wsts = [host_stream(inp, l) for l in range(L)]
    wada = host_ada(inp)
    in_maps = []
    for core in range(8):
        ss = slice(core * NS, (core + 1) * NS)
        xT = np.empty((D, NTOK), np.float32)
        xT[:, :SEQ] = inp["x_prompt"][core].T
        xT[:, SEQ:] = inp["x_sample"][ss].reshape(NS * ST, D).T
        cT = np.zeros((D, 18), np.float32)
        cT[:, 0] = inp["c_prompt"][core]
        cT[:, 1:17] = inp["c_sample"][ss].T
        st_rgh = inp["state_rg_h"][ss].reshape(NS, L, 8, 128).transpose(1, 3, 2, 0)
        st_rgc = inp["state_rg_conv"][ss].reshape(NS, L, 3, 8, 128).transpose(1, 4, 3, 0, 2)
        s5 = np.stack([inp["state_s5_re"][ss], inp["state_s5_im"][ss]], 0)
        s5 = s5.reshape(2, NS, L, 16, 2, 64).transpose(2, 0, 4, 5, 3, 1).reshape(L, 2, 128, 16, NS)
        st_ffn = inp["state_ffn_conv"][ss].reshape(NS, L, 2, 48, 128).transpose(1, 4, 3, 0, 2)
        m = {"xT": xT, "cT": cT, "cols": cols, "st_rgh": np.ascontiguousarray(st_rgh),
             "st_rgc": np.ascontiguousarray(st_rgc), "st_s5": np.ascontiguousarray(s5),
             "st_ffn": np.ascontiguousarray(st_ffn), "s5col": s5col, "s5bp": s5bp,
             "wst0": wsts[0], "wst1": wsts[1], "wada": wada}
        in_maps.append(m)
    res = run_bass_kernel_spmd(_NC, in_maps, core_ids=list(range(8)))
    R = res.results
    y_p = np.empty((8, SEQ, D), np.float32)
    y_s = np.empty((128, ST, D), np.float32)
    rg_h = np.empty((8 * 17, L, D), np.float32)
    rg_c = np.empty((8 * 17, L, 3, D), np.float32)
    s5r = np.empty((8 * 17, L, 32, 64), np.float32)
    s5i = np.empty((8 * 17, L, 32, 64), np.float32)
    ffn = np.empty((8 * 17, L, 2, 2 * DFF), np.float32)
    for core in range(8):
        r = R[core]
        yT = r["yT"]
        y_p[core] = yT[:, :SEQ].T
        y_s[core * NS:(core + 1) * NS] = yT[:, SEQ:].T.reshape(NS, ST, D)
        sl = slice(core * 17, (core + 1) * 17)
        rg_h[sl] = r["o_rgh"].transpose(3, 0, 2, 1).reshape(17, L, D)
        rg_c[sl] = r["o_rgc"].transpose(3, 0, 4, 2, 1).reshape(17, L, 3, D)
        o5 = r["o_s5"].reshape(L, 2, 2, 64, 16, 17).transpose(1, 5, 0, 4, 2, 3).reshape(2, 17, L, 32, 64)
        s5r[sl] = o5[0]
        s5i[sl] = o5[1]
        ffn[sl] = r["o_ffn"].transpose(3, 0, 4, 2, 1).reshape(17, L, 2, 2 * DFF)
    idx_p = np.arange(8) * 17
    idx_s = (np.arange(8)[:, None] * 17 + 1 + np.arange(16)[None, :]).reshape(-1)
    return (y_p, y_s, rg_h[idx_p], rg_c[idx_p], s5r[idx_p], s5i[idx_p], ffn[idx_p],
            rg_h[idx_s], rg_c[idx_s], s5r[idx_s], s5i[idx_s], ffn[idx_s])
```

```python
import contextlib
import math
import numpy as np
import concourse.bass as bass
import concourse.mybir as mybir
from concourse.bass_utils import run_bass_kernel_spmd

F32 = mybir.dt.float32
F32R = mybir.dt.float32r
AF = mybir.ActivationFunctionType
ALU = mybir.AluOpType

D = 1024
SEQ = 2048
NS = 16
ST = 8
NTOK = SEQ + NS * ST
L = 2
INW = 4608
DFF = 3072
LS = 64
ENGS = ("pe", "act", "dve", "pool", "sp")
N_DMA_SEMS = 24
SELF_SYNC = True
WB = 2048
NWBUF = 4
NTMP = 12
NTMPR = 8

CO = {}
_o = 0
for _n, _w in (("g1", 8), ("g2", 8), ("bin", 36), ("rcw", 32), ("rcb", 8), ("ba", 8), ("bx", 8), ("lam", 8),
               ("s5d", 4), ("fcw", 144), ("fcb", 48), ("bada", 48)):
    CO[_n] = _o
    _o += _w
NCL = _o
NCOL = 2 * NCL + 8


class Prog:
    def __init__(self, nc, stack):
        self.nc = nc
        self.stack = stack
        self.ops = {e: [] for e in ENGS}
        self.sem = {e: stack.enter_context(nc.semaphore("s_" + e)) for e in ENGS}
        self.cnt = {e: 0 for e in ENGS}
        self.dsem = [stack.enter_context(nc.semaphore("d%d" % i)) for i in range(3 * N_DMA_SEMS)]
        self.dcnt = [0] * (3 * N_DMA_SEMS)
        self.dnext = {"sp": 0, "pool": 0, "act": 0}
        self.seen = {e: {} for e in ENGS}
        self.tiles = {}
        self.nbuf = 0

    def sb(self, shape, dtype=F32, name=None):
        self.nbuf += 1
        return self.stack.enter_context(self.nc.sbuf_tensor("S_" + (name or ("t%d" % self.nbuf)), list(shape), dtype))

    def ps(self, shape, dtype=F32, name=None):
        self.nbuf += 1
        return self.stack.enter_context(self.nc.psum_tensor("P_" + (name or ("p%d" % self.nbuf)), list(shape), dtype))

    def _semobj(self, key):
        return self.sem[key] if isinstance(key, str) else self.dsem[key]

    def _deps(self, eng, reads, writes):
        need = {}

        def req(ev):
            if ev is None:
                return
            k, v = ev
            if k == eng and (eng == "pe" or not SELF_SYNC):
                return
            if need.get(k, 0) < v:
                need[k] = v

        for t in reads:
            st = self.tiles.get(t)
            if st:
                req(st["w"])
        for t in writes:
            st = self.tiles.get(t)
            if st:
                req(st["w"])
                for ev in st["r"]:
                    req(ev)
        waits = []
        for k, v in need.items():
            if self.seen[eng].get(k, 0) < v:
                self.seen[eng][k] = v
                waits.append((k, v))
        return waits

    def _commit(self, ev, reads, writes):
        for t in reads:
            st = self.tiles.setdefault(t, {"w": None, "r": []})
            st["r"].append(ev)
            if len(st["r"]) > 16:
                best = {}
                for k, v in st["r"]:
                    if best.get(k, 0) < v:
                        best[k] = v
                st["r"] = list(best.items())
        for t in writes:
            self.tiles[t] = {"w": ev, "r": []}

    def op(self, eng, fn, reads=(), writes=()):
        waits = self._deps(eng, reads, writes)
        self.cnt[eng] += 1
        ev = (eng, self.cnt[eng])
        sem = self.sem[eng]

        def emit(e, fn=fn, waits=waits, sem=sem):
            for k, v in waits:
                e.wait_ge(self._semobj(k), v)
            fn(e).then_inc(sem, 1)

        self.ops[eng].append(emit)
        self._commit(ev, reads, writes)

    def dma(self, out, in_, reads=(), writes=(), eng="sp"):
        k = self.dnext[eng] + {"sp": 0, "pool": N_DMA_SEMS, "act": 2 * N_DMA_SEMS}[eng]
        self.dnext[eng] = (self.dnext[eng] + 1) % N_DMA_SEMS
        waits = self._deps(eng, reads, writes)
        prev = self.dcnt[k]
        if prev and self.seen[eng].get(k, 0) < prev:
            self.seen[eng][k] = prev
            waits.append((k, prev))
        self.dcnt[k] += 16
        ev = (k, self.dcnt[k])
        dsem = self.dsem[k]

        def emit(e, waits=waits, out=out, in_=in_, dsem=dsem):
            for kk, v in waits:
                e.wait_ge(self._semobj(kk), v)
            e.dma_start(out=out, in_=in_).then_inc(dsem, 16)

        self.ops[eng].append(emit)
        self._commit(ev, reads, writes)

    def finish(self):
        fin = [(k, self.dcnt[k]) for k in range(3 * N_DMA_SEMS) if self.dcnt[k]]

        def emit_fin(e):
            for k, v in fin:
                e.wait_ge(self.dsem[k], v)

        self.ops["sp"].append(emit_fin)
        nc = self.nc
        with nc.Block() as block:
            @block.tensor
            def _(e):
                for f in self.ops["pe"]:
                    f(e)

            @block.scalar
            def _(e):
                for f in self.ops["act"]:
                    f(e)

            @block.vector
            def _(e):
                for f in self.ops["dve"]:
                    f(e)

            @block.gpsimd
            def _(e):
                for f in self.ops["pool"]:
                    f(e)

            @block.sync
            def _(e):
                for f in self.ops["sp"]:
                    f(e)


def stream_tiles():
    t = []
    for c in range(8):
        t.append(("rg%d" % c, 2048))
        t.append(("rgd%d" % c, 256))
    for q in range(4):
        t.append(("sx%d" % q, 1024))
        t.append(("s5c%d" % q, 1024))
    for m in range(8):
        t.append(("mga%d" % m, 2048))
        t.append(("mgb%d" % m, 2048))
    for o in range(4):
        t.append(("wo%d" % o, 2048))
    for j in range(24):
        t.append(("up%d" % j, 2048))
    for o in range(4):
        for i in range(3):
            t.append(("dn%d_%d" % (o, i), 2048))
    return t


STILES = stream_tiles()
SOFF = {}
_o = 0
for _n, _s in STILES:
    SOFF[_n] = (_o, _s)
    _o += _s
SLEN = _o
ADA_TILES = 24


def _pack(W, kc, cols, k0=0):
    cols = np.asarray(cols)
    blk = W[k0 * 128:(k0 + kc) * 128][:, cols].reshape(kc, 128, len(cols))
    return np.ascontiguousarray(blk.transpose(1, 0, 2)).reshape(128, kc * len(cols))


def _ar(a, n=128):
    return np.arange(a, a + n)


def host_stream(inp, l):
    w_in = inp["w_in"][l]
    out = np.zeros((128, SLEN), np.float32)

    def put(name, arr):
        o, s = SOFF[name]
        assert arr.shape == (128, s), (name, arr.shape, s)
        out[:, o:o + s] = arr

    g = np.zeros((128, 8, 256), np.float32)
    for c in range(8):
        for hh in range(2):
            h = 2 * c + hh
            g[hh * 64:(hh + 1) * 64, c, hh * 64:(hh + 1) * 64] = inp["rg_wa"][l, h]
            g[hh * 64:(hh + 1) * 64, c, 128 + hh * 64:128 + (hh + 1) * 64] = inp["rg_wx"][l, h]
    for c in range(8):
        put("rgd%d" % c, np.ascontiguousarray(g[:, c, :]))
    for c in range(8):
        put("rg%d" % c, _pack(w_in, 8, np.concatenate([_ar(c * 128), _ar(1024 + c * 128)])))
    cre = inp["s5_c_re"][l]
    cim = inp["s5_c_im"][l]
    for q in range(4):
        put("sx%d" % q, _pack(w_in, 8, _ar(2048 + q * 128)))
        cc = np.zeros((128, 2, 4, 128), np.float32)
        for pp in range(4):
            for j in range(2):
                gidx = 2 * (4 * q + pp) + j
                sl = 16 * (gidx % 8)
                cc[64 * j:64 * j + 64, 0, pp, sl:sl + 16] = cre[gidx].T
                cc[64 * j:64 * j + 64, 1, pp, sl:sl + 16] = cim[gidx].T
        put("s5c%d" % q, cc.reshape(128, 1024))
    for m in range(8):
        put("mga%d" % m, _pack(w_in, 8, np.concatenate([_ar(2560 + m * 128), _ar(3584 + m * 128)])))
        a = _pack(inp["w_rg_proj"][l], 8, _ar(m * 128))
        b = _pack(inp["w_glu"][l], 4, np.concatenate([_ar(m * 128), _ar(1024 + m * 128)]))
        put("mgb%d" % m, np.concatenate([a, b], axis=1))
    for o in range(4):
        put("wo%d" % o, _pack(inp["w_out"][l], 8, _ar(o * 256, 256)))
    for j in range(24):
        put("up%d" % j, _pack(inp["w_up"][l], 8, np.concatenate([_ar(j * 128), _ar(DFF + j * 128)])))
    for o in range(4):
        for i in range(3):
            put("dn%d_%d" % (o, i), _pack(inp["w_down"][l], 8, _ar(o * 256, 256), k0=8 * i))
    return out


def host_ada(inp):
    out = np.zeros((128, L * ADA_TILES * 2048), np.float32)
    for l in range(L):
        for t in range(ADA_TILES):
            out[:, (l * ADA_TILES + t) * 2048:(l * ADA_TILES + t + 1) * 2048] = _pack(inp["w_ada"][l], 8, _ar(t * 256, 256))
    return out


def host_cols(inp):
    c = np.zeros((128, NCOL), np.float32)

    def colv(v):
        return np.asarray(v).reshape(-1, 128).T

    for l in range(L):
        b = l * NCL
        c[:, b + CO["g1"]:b + CO["g1"] + 8] = colv(inp["g_norm1"][l])
        c[:, b + CO["g2"]:b + CO["g2"] + 8] = colv(inp["g_norm2"][l])
        c[:, b + CO["bin"]:b + CO["bin"] + 36] = colv(inp["b_in"][l])
        for j in range(4):
            c[:, b + CO["rcw"] + j * 8:b + CO["rcw"] + j * 8 + 8] = colv(inp["rg_conv_w"][l, j])
        c[:, b + CO["rcb"]:b + CO["rcb"] + 8] = colv(inp["rg_conv_b"][l])
        c[:, b + CO["ba"]:b + CO["ba"] + 8] = colv(inp["rg_ba"][l])
        c[:, b + CO["bx"]:b + CO["bx"] + 8] = colv(inp["rg_bx"][l])
        c[:, b + CO["lam"]:b + CO["lam"] + 8] = colv(inp["rg_lambda"][l])
        c[:, b + CO["s5d"]:b + CO["s5d"] + 4] = colv(inp["s5_d"][l])
        for j in range(3):
            c[:, b + CO["fcw"] + j * 48:b + CO["fcw"] + j * 48 + 48] = colv(inp["ffn_conv_w"][l, j])
        c[:, b + CO["fcb"]:b + CO["fcb"] + 48] = colv(inp["ffn_conv_b"][l])
        c[:, b + CO["bada"]:b + CO["bada"] + 48] = colv(inp["b_ada"][l])
    c[:, 2 * NCL:2 * NCL + 8] = colv(inp["g_final"])
    return c


def host_s5(inp):
    col = np.zeros((L, 128, 3, 16), np.float32)
    bp = np.zeros((L, 128, 2, 16, 128), np.float32)
    for l in range(L):
        for p in range(16):
            for j in range(2):
                g = 2 * p + j
                col[l, 64 * j:64 * j + 64, 0, p] = inp["s5_lam_re"][l, g]
                col[l, 64 * j:64 * j + 64, 1, p] = inp["s5_lam_im"][l, g]
                col[l, 64 * j:64 * j + 64, 2, p] = inp["s5_log_dt"][l, g]
                sl = 16 * (g % 8)
                bp[l, sl:sl + 16, 0, p, 64 * j:64 * j + 64] = inp["s5_b_re"][l, g].T
                bp[l, sl:sl + 16, 1, p, 64 * j:64 * j + 64] = inp["s5_b_im"][l, g].T
    return col, bp.reshape(L, 128, 2, 2048)


def build():
    nc = bass.Bass("TRN2", target_bir_lowering=False)
    nc.dge_precook = False

    def din(name, shape, dt=F32):
        return nc.dram_tensor(name, list(shape), dt, kind="ExternalInput").ap()

    def dout(name, shape, dt=F32):
        return nc.dram_tensor(name, list(shape), dt, kind="ExternalOutput").ap()

    xT = din("xT", [D, NTOK])
    cT = din("cT", [D, 18])
    cols_d = din("cols", [128, NCOL])
    st_rgh = din("st_rgh", [L, 128, 8, NS])
    st_rgc = din("st_rgc", [L, 128, 8, NS, 3])
    st_s5 = din("st_s5", [L, 2, 128, 16, NS])
    st_ffn = din("st_ffn", [L, 128, 48, NS, 2])
    s5col_d = din("s5col", [L, 128, 3, 16])
    s5bp_d = din("s5bp", [L, 128, 2, 2048])
    wst = [din("wst%d" % l, [128, SLEN], F32R) for l in range(L)]
    wada = din("wada", [128, L * ADA_TILES * 2048], F32R)
    yT = dout("yT", [D, NTOK])
    o_rgh = dout("o_rgh", [L, 128, 8, 17])
    o_rgc = dout("o_rgc", [L, 128, 8, 17, 3])
    o_s5 = dout("o_s5", [L, 2, 128, 16, 17])
    o_ffn = dout("o_ffn", [L, 128, 48, 17, 2])
    s5b_d = nc.dram_tensor("s5b_scratch", [L, 128, 4, 1024], F32R, kind="Internal").ap()
    tab_d = nc.dram_tensor("tab_scratch", [L, 16, 128, 2, 512], F32, kind="Internal").ap()

    with contextlib.ExitStack() as stack:
        P = Prog(nc, stack)
        x = P.sb([128, 8, 512], F32, "x")
        h = P.sb([128, 8, 512], F32R, "h")
        R1 = P.sb([128, 24, 512], F32R, "R1")
        wbuf = [P.sb([128, WB], F32R, "wbuf%d" % i) for i in range(NWBUF)]
        tmps = [P.sb([128, 520], F32, "tmp%d" % i) for i in range(NTMP)]
        tmpr = [P.sb([128, 520], F32R, "tmpr%d" % i) for i in range(NTMPR)]
        pss = [P.ps([128, 512], F32, "ps%d" % i) for i in range(8)]
        cols = P.sb([128, NCOL], F32, "cols")
        dcol = P.sb([128, L, 64], F32, "dcol")
        mod = P.sb([128, L, 48, 18], F32, "mod")
        dmod = P.sb([128, L, 3, 8, 18], F32, "dmod")
        ones = P.sb([128, 128], F32R, "ones")
        ident = P.sb([128, 128], F32, "ident")
        cst = P.sb([128, 8], F32, "cst")
        rgh = P.sb([128, L, 8, 17], F32, "rgh")
        rgc = P.sb([128, L, 8, 17, 3], F32, "rgc")
        s5s = P.sb([128, L, 2, 16, 17], F32, "s5s")
        ffc = P.sb([128, L, 48, 17, 2], F32, "ffc")
        tbuf = [P.sb([128, 1024], F32, "tbuf%d" % i) for i in range(2)]
        dgb = [P.sb([128, 6, 128], F32R, "dgb%d" % i) for i in range(2)]
        cC = P.sb([128, 16, 8], F32, "cC")
        cS = P.sb([128, 16, 8], F32, "cS")
        rhoM = P.sb([128, L, 16, 8], F32, "rhoM")
        identR = P.sb([128, 128], F32R, "identR")
        nidentR = P.sb([128, 128], F32R, "nidentR")
        s5p = P.sb([128, L, 8, 16], F32, "s5p")
        scT = P.sb([128, 8, 18], F32R, "scT")

        free_t = list(range(NTMP))
        free_r = list(range(NTMPR))
        free_p = list(range(8))

        def talloc():
            i = free_t.pop(0)
            return i

        def tfree(*ids):
            for i in ids:
                free_t.append(i)

        def T(i):
            return tmps[i]

        def TK(i):
            return "tmp%d" % i

        def ralloc():
            return free_r.pop(0)

        def rfree(*ids):
            for i in ids:
                free_r.append(i)

        def psalloc():
            return free_p.pop(0)

        def psfree(*ids):
            for i in ids:
                free_p.append(i)

        def PK(i):
            return "ps%d" % i


        def tt(out, in0, in1, op, reads, writes, eng="dve"):
            P.op(eng, lambda e: e.tensor_tensor(out=out, in0=in0, in1=in1, op=op), reads, writes)

        def ts(out, in0, s1, s2, op0, op1, reads, writes, eng="dve"):
            if s2 is None:
                P.op(eng, lambda e: e.tensor_scalar(out=out, in0=in0, scalar1=s1, scalar2=None, op0=op0), reads, writes)
            else:
                P.op(eng, lambda e: e.tensor_scalar(out=out, in0=in0, scalar1=s1, scalar2=s2, op0=op0, op1=op1), reads, writes)

        def stt(out, in0, sc, in1, op0, op1, reads, writes):
            P.op("dve", lambda e: e.scalar_tensor_tensor(out=out, in0=in0, scalar=sc, in1=in1, op0=op0, op1=op1), reads, writes)

        def act(out, in_, func, reads, writes, scale=1.0, bias=None):
            if bias is None:
                P.op("act", lambda e: e.activation(out=out, in_=in_, func=func, scale=scale), reads, writes)
            else:
                P.op("act", lambda e: e.activation(out=out, in_=in_, func=func, scale=scale, bias=bias), reads, writes)

        def mm(out, lhsT, rhs, start, stop, reads, writes):
            P.op("pe", lambda e: e.matmul(out, lhsT, rhs, start=start, stop=stop), reads, writes)

        def cp(out, in_, reads, writes, eng="dve"):
            P.op(eng, lambda e: e.tensor_copy(out=out, in_=in_), reads, writes)

        def scan(out, d0, d1, init, reads, writes):
            P.op("dve", lambda e: e.tensor_tensor_scan(out=out, data0=d0, data1=d1, initial=init, op0=ALU.mult, op1=ALU.add),
                 reads, writes)

        def mset(ap, v, reads, writes, eng="dve"):
            P.op(eng, lambda e: e.memset(ap, v), reads, writes)

        MUL, ADD, SUB = ALU.mult, ALU.add, ALU.subtract

        wstate = {"n": 0, "nrot": NWBUF}

        def wload(src_ap, size, reads=(), eng="sp"):
            i = wstate["n"] % wstate["nrot"]
            wstate["n"] += 1
            key = "wbuf%d" % i
            wkeys = [key] + (["wbuf3a", "wbuf3b"] if i == 3 else [])
            P.dma(wbuf[i][:, 0:size], src_ap, reads=list(reads), writes=wkeys, eng=eng)
            return wbuf[i], key

        def wtile(l, name):
            o, s = SOFF[name]
            return wload(wst[l][:, o:o + s], s)

        P.dma(cols[:], cols_d, writes=["cols"])
        ti_c = talloc()
        P.dma(T(ti_c)[:, 0:144].rearrange("p (k s) -> p k s", s=18), cT.rearrange("(k p) s -> p k s", p=128), writes=[TK(ti_c)])
        mset(ident[:], 0.0, [], ["ident"], eng="pool")
        P.op("pool", lambda e: e.affine_select(out=ident[:], in_=ident[:], pattern=[[-1, 128]],
                                               compare_op=ALU.not_equal, fill=1.0, base=0, channel_multiplier=1),
             reads=["ident"], writes=["ident"])
        ts(ones[:], ident[:], 0.0, 1.0, MUL, ADD, ["ident"], ["ones"])
        ts(identR[:], ident[:], 1.0, 0.0, MUL, ADD, ["ident"], ["identR"])
        ts(nidentR[:], ident[:], -1.0, 0.0, MUL, ADD, ["ident"], ["identR"])
        mset(cst[:, 0:1], 1e-6, [], ["cst"])
        mset(cst[:, 1:2], math.pi / 2, ["cst"], ["cst"])
        mset(cst[:, 2:3], 1.0, ["cst"], ["cst"])
        mset(cst[:, 3:4], 0.0, ["cst"], ["cst"])
        mset(rgh[:], 0.0, [], ["rgh"], eng="pool")
        mset(rgc[:], 0.0, [], ["rgc"], eng="pool")
        mset(s5s[:], 0.0, [], ["s5s"], eng="pool")
        mset(ffc[:], 0.0, [], ["ffc"], eng="pool")
        for l in range(L):
            P.dma(rgh[:, l, :, 1:17], st_rgh[l], writes=["rgh"])
            P.dma(rgc[:, l, :, 1:17, :], st_rgc[l], writes=["rgc"])
            for ri in range(2):
                P.dma(s5s[:, l, ri, :, 1:17], st_s5[l, ri], writes=["s5s"])
            for c0 in range(0, 48, 8):
                P.dma(ffc[:, l, c0:c0 + 8, 1:17, :], st_ffn[l, :, c0:c0 + 8], writes=["ffc"])

        def setup_A1(l):
            a0 = talloc()
            a1 = talloc()
            K0, K1 = TK(a0), TK(a1)
            prm = T(a0)[:, 0:48].rearrange("p (a b) -> p a b", b=16)
            P.dma(prm, s5col_d[l], writes=[K0], eng="pool")
            w_ = T(a1)

            def S(i, w_=w_):
                return w_[:, i * 16:(i + 1) * 16]
            lr, li, ldt = prm[:, 0, :], prm[:, 1, :], prm[:, 2, :]
            rho, ar, ai, zr, zi = (s5p[:, l, i, :] for i in range(5))
            act(S(0), ldt, AF.Exp, [K0], [K1])
            tt(S(1), lr, S(0), MUL, [K0, K1], [K1])
            act(rho, S(1), AF.Exp, [K1], ["s5p"])
            tt(S(2), li, S(0), MUL, [K0, K1], [K1])
            act(S(3), S(2), AF.Sin, [K1], [K1], scale=1.0 / 16)
            act(S(4), S(2), AF.Sin, [K1, "cst"], [K1], scale=1.0 / 16, bias=cst[:, 1:2])
            for _ in range(4):
                tt(S(5), S(4), S(4), MUL, [K1], [K1])
                tt(S(6), S(3), S(3), MUL, [K1], [K1])
                tt(S(7), S(4), S(3), MUL, [K1], [K1])
                tt(S(4), S(5), S(6), SUB, [K1], [K1])
                ts(S(3), S(7), 2.0, None, MUL, None, [K1], [K1])
            cp(s5p[:, l, 5, :], S(4), [K1, "s5p"], ["s5p"])
            cp(s5p[:, l, 6, :], S(3), [K1, "s5p"], ["s5p"])
            tt(ar, rho, S(4), MUL, [K1, "s5p"], ["s5p"])
            tt(ai, rho, S(3), MUL, [K1, "s5p"], ["s5p"])
            tt(S(5), lr, lr, MUL, [K0], [K1])
            tt(S(6), li, li, MUL, [K0], [K1])
            tt(S(5), S(5), S(6), ADD, [K1], [K1])
            P.op("dve", lambda e, o=S(5): e.reciprocal(out=o, in_=o), [K1], [K1])
            ts(S(6), ar, -1.0, None, ADD, None, ["s5p"], [K1])
            tt(S(7), S(6), lr, MUL, [K1, K0], [K1])
            tt(S(8), ai, li, MUL, ["s5p", K0], [K1])
            tt(S(7), S(7), S(8), ADD, [K1], [K1])
            tt(zr, S(7), S(5), MUL, [K1], ["s5p"])
            tt(S(7), ai, lr, MUL, ["s5p", K0], [K1])
            tt(S(8), S(6), li, MUL, [K1, K0], [K1])
            tt(S(7), S(7), S(8), SUB, [K1], [K1])
            tt(zi, S(7), S(5), MUL, [K1], ["s5p"])
            tfree(a0, a1)

        def setup_A2(l):
            for q in range(4):
                zr_ps = psalloc()
                zi_ps = psalloc()
                for pp in range(4):
                    p = 4 * q + pp
                    for (zi_, dst) in ((3, zr_ps), (4, zi_ps)):
                        r = ralloc()
                        ts(tmpr[r][:, 0:128], ident[:], s5p[:, l, zi_, p:p + 1], None, MUL, None, ["ident", "s5p"], ["tmpr%d" % r])
                        mm(pss[dst][:, pp * 128:(pp + 1) * 128], ones[:], tmpr[r][:, 0:128], True, True,
                           ["ones", "tmpr%d" % r], [PK(dst)])
                        rfree(r)
                br = talloc()
                bi = talloc()
                P.dma(T(br)[:, 0:512], s5bp_d[l, :, 0, q * 512:(q + 1) * 512], writes=[TK(br)])
                P.dma(T(bi)[:, 0:512], s5bp_d[l, :, 1, q * 512:(q + 1) * 512], writes=[TK(bi)])
                t1 = talloc()
                t2 = talloc()
                r = ralloc()
                r2 = ralloc()
                RK = "tmpr%d" % r
                RK2 = "tmpr%d" % r2
                A, B_, U1, U2 = T(br)[:, 0:512], T(bi)[:, 0:512], T(t1)[:, 0:512], T(t2)[:, 0:512]
                tt(U1, A, pss[zr_ps][:], MUL, [TK(br), PK(zr_ps)], [TK(t1)])
                tt(U2, B_, pss[zi_ps][:], MUL, [TK(bi), PK(zi_ps)], [TK(t2)])
                tt(tmpr[r][:, 0:512], U1, U2, SUB, [TK(t1), TK(t2)], [RK])
                tt(U1, A, pss[zi_ps][:], MUL, [TK(br), PK(zi_ps)], [TK(t1)])
                tt(U2, B_, pss[zr_ps][:], MUL, [TK(bi), PK(zr_ps)], [TK(t2)])
                tt(tmpr[r2][:, 0:512], U1, U2, ADD, [TK(t1), TK(t2)], [RK2])
                P.dma(s5b_d[l, :, q, 0:512], tmpr[r][:, 0:512], reads=[RK], writes=["s5b_d"])
                P.dma(s5b_d[l, :, q, 512:1024], tmpr[r2][:, 0:512], reads=[RK2, "s5b_d"], writes=["s5b_d"])
                rfree(r, r2)
                tfree(br, bi, t1, t2)
                psfree(zr_ps, zi_ps)

        act(scT[:].rearrange("p k s -> p (k s)"), T(ti_c)[:, 0:144], AF.Silu, [TK(ti_c)], ["scT"])
        tfree(ti_c)
        ada_loaded = {}

        def ada_issue(l, t):
            ada_loaded[(l, t)] = wload(wada[:, (l * ADA_TILES + t) * 2048:(l * ADA_TILES + t + 1) * 2048], 2048,
                                       eng=("act" if l == 0 else "sp"))

        def ada_tile(l, t):
            if (l, t) not in ada_loaded:
                ada_issue(l, t)
            wb, wk = ada_loaded.pop((l, t))
            wv = wb[:, 0:2048].rearrange("p (k n) -> p k n", n=256)
            for half in range(2):
                m = 2 * t + half
                pi = psalloc()
                for k in range(8):
                    mm(pss[pi][:, 0:18], wv[:, k, half * 128:(half + 1) * 128], scT[:, k, :], k == 0, k == 7,
                       [wk, "scT"], [PK(pi)])
                bcol = l * NCL + CO["bada"] + m
                act(mod[:, l, m, :], pss[pi][:, 0:18], AF.Identity, [PK(pi), "cols"], ["mod%d" % l], bias=cols[:, bcol:bcol + 1])
                psfree(pi)

        def setup_B(l):
            baseC = tbuf[0][:, 0:1024].rearrange("p (a b) -> p a b", b=64)
            baseS = tbuf[1][:, 0:1024].rearrange("p (a b) -> p a b", b=64)
            BK = ["tbuf0", "tbuf1"]
            cp(baseC[:, :, 0], s5p[:, l, 5, :], ["s5p"] + BK, ["tbuf0"])
            cp(baseS[:, :, 0], s5p[:, l, 6, :], ["s5p"] + BK, ["tbuf1"])
            b1 = talloc()
            b2 = talloc()
            k = 1
            while k < 64:
                ec = baseC[:, :, k - 1:k].to_broadcast([128, 16, k])
                es = baseS[:, :, k - 1:k].to_broadcast([128, 16, k])
                u1 = T(b1)[:, 0:16 * k].rearrange("p (a b) -> p a b", b=k)
                u2 = T(b2)[:, 0:16 * k].rearrange("p (a b) -> p a b", b=k)
                c0 = baseC[:, :, 0:k]
                s0 = baseS[:, :, 0:k]
                tt(u1, c0, ec, MUL, BK, [TK(b1)])
                tt(u2, s0, es, MUL, BK, [TK(b2)])
                tt(baseC[:, :, k:2 * k], u1, u2, SUB, [TK(b1), TK(b2)] + BK, ["tbuf0"])
                tt(u1, s0, ec, MUL, BK, [TK(b1)])
                tt(u2, c0, es, MUL, BK, [TK(b2)])
                tt(baseS[:, :, k:2 * k], u1, u2, ADD, [TK(b1), TK(b2)] + BK, ["tbuf1"])
                k *= 2
            CK = ["cCS"]
            mset(cC[:, :, 0], 1.0, CK, CK)
            mset(cS[:, :, 0], 0.0, CK, CK)
            cp(cC[:, :, 1], baseC[:, :, 63], BK + CK, CK)
            cp(cS[:, :, 1], baseS[:, :, 63], BK + CK, CK)
            v1 = T(b1)[:, 0:16]
            v2 = T(b2)[:, 0:16]
            for a in range(2, 8):
                tt(v1, cC[:, :, a - 1], cC[:, :, 1], MUL, CK, [TK(b1)])
                tt(v2, cS[:, :, a - 1], cS[:, :, 1], MUL, CK, [TK(b2)])
                tt(cC[:, :, a], v1, v2, SUB, [TK(b1), TK(b2)] + CK, CK)
                tt(v1, cS[:, :, a - 1], cC[:, :, 1], MUL, CK, [TK(b1)])
                tt(v2, cC[:, :, a - 1], cS[:, :, 1], MUL, CK, [TK(b2)])
                tt(cS[:, :, a], v1, v2, ADD, [TK(b1), TK(b2)] + CK, CK)
            tfree(b1, b2)
            mset(rhoM[:, l, :, 0:1], 0.0, ["rhoM"], ["rhoM"])
            cp(rhoM[:, l, :, 1:8], s5p[:, l, 0, :].unsqueeze(2).to_broadcast([128, 16, 7]), ["s5p", "rhoM"], ["rhoM"])
            for p in range(16):
                ccb = cC[:, p, :].unsqueeze(2).to_broadcast([128, 8, 64])
                scb = cS[:, p, :].unsqueeze(2).to_broadcast([128, 8, 64])
                cbb = baseC[:, p, :].unsqueeze(1).to_broadcast([128, 8, 64])
                sbb = baseS[:, p, :].unsqueeze(1).to_broadcast([128, 8, 64])
                m1, m2, m3, m4, fc, fs = (talloc() for _ in range(6))

                def W3(i):
                    return T(i)[:, 0:512].rearrange("p (a b) -> p a b", b=64)
                tt(W3(m1), ccb, cbb, MUL, BK + CK, [TK(m1)])
                tt(W3(m2), scb, sbb, MUL, BK + CK, [TK(m2)])
                tt(T(fc)[:, 0:512], T(m1)[:, 0:512], T(m2)[:, 0:512], SUB, [TK(m1), TK(m2)], [TK(fc)])
                tt(W3(m3), scb, cbb, MUL, BK + CK, [TK(m3)])
                tt(W3(m4), ccb, sbb, MUL, BK + CK, [TK(m4)], eng="pool")
                tt(T(fs)[:, 0:512], T(m3)[:, 0:512], T(m4)[:, 0:512], ADD, [TK(m3), TK(m4)], [TK(fs)], eng="pool")
                P.dma(tab_d[l, p, :, 0, :], T(fc)[:, 0:512], reads=[TK(fc)], writes=["tab_d"])
                P.dma(tab_d[l, p, :, 1, :], T(fs)[:, 0:512], reads=[TK(fs), "tab_d"], writes=["tab_d"])
                tfree(m1, m2, m3, m4, fc, fs)

        def dmod_l(l):
            for c in range(8):
                g1c = l * NCL + CO["g1"] + c
                g2c = l * NCL + CO["g2"] + c
                ts(dmod[:, l, 0, c, :], mod[:, l, 8 + c, :], 1.0, cols[:, g1c:g1c + 1], ADD, MUL, ["mod%d" % l, "cols"], ["dmod%d" % l])
                ts(dmod[:, l, 1, c, :], mod[:, l, 16 + c, :], 0.5, None, MUL, None, ["mod%d" % l], ["dmod%d" % l])
                ts(dmod[:, l, 2, c, :], mod[:, l, 32 + c, :], 1.0, cols[:, g2c:g2c + 1], ADD, MUL, ["mod%d" % l, "cols"], ["dmod%d" % l])


        for l in range(L):
            setup_A1(l)
        for l in range(L):
            b = l * NCL
            for (dst, src) in ((0, CO["ba"]), (8, CO["bx"]), (32, CO["bin"] + 20), (40, CO["bin"] + 28)):
                ts(dcol[:, l, dst:dst + 8], cols[:, b + src:b + src + 8], 0.5, None, MUL, None, ["cols"], ["dcol"])
            act(dcol[:, l, 48:56], cols[:, b + CO["lam"]:b + CO["lam"] + 8], AF.Exp, ["cols"], ["dcol"], scale=-1.0)
            act(dcol[:, l, 56:64], dcol[:, l, 48:56], AF.Ln, ["dcol", "cst"], ["dcol"], bias=cst[:, 2:3])
            ts(dcol[:, l, 16:24], dcol[:, l, 56:64], -8.0, None, MUL, None, ["dcol"], ["dcol"])
            ts(dcol[:, l, 24:32], dcol[:, l, 56:64], -4.0, None, MUL, None, ["dcol"], ["dcol"])

        for l in range(L):
            setup_A2(l)
        for t in range(3):
            ada_issue(0, t)
        for t in range(ADA_TILES):
            if t + 3 < ADA_TILES:
                ada_issue(0, t + 3)
            ada_tile(0, t)
        for l in range(L):
            setup_B(l)
        dmod_l(0)
        ada_pending = [(1, t) for t in range(ADA_TILES)]

        def ada_step():
            if ada_pending:
                ada_tile(*ada_pending.pop(0))
                if not ada_pending:
                    dmod_l(1)

        tstate = {"n": 0}

        def tabload(l, p, small):
            i = tstate["n"] % 2
            tstate["n"] += 1
            key = "tbuf%d" % i
            if small:
                P.dma(tbuf[i][:, 0:16].rearrange("p (a n) -> p a n", a=2), tab_d[l, p, :, :, 0:8], reads=["tab_d"], writes=[key])
            else:
                P.dma(tbuf[i][:, 0:1024].rearrange("p (a n) -> p a n", a=2), tab_d[l, p], reads=["tab_d"], writes=[key])
            return tbuf[i], key

        dstate = {"n": 0}

        def mkdiag(wcols):
            i = dstate["n"] % 2
            dstate["n"] += 1
            key = "dgb%d" % i
            for t, wc in enumerate(wcols):
                ts(dgb[i][:, t, :], ident[:], wc, 0.0, MUL, ADD, ["ident", "cols"], [key], eng="pool")
            return dgb[i], key

        def run_tile(col0, N, nseq, slen, seq0):
            prompt = nseq == 1

            def V3(ap2):
                return ap2.rearrange("p (s t) -> p s t", t=slen)

            def DV(ap2):
                return ap2 if prompt else V3(ap2)

            def bc_seq(ap_pn):
                return ap_pn.unsqueeze(2).to_broadcast([128, nseq, slen])

            for c0 in (0, 4):
                P.dma(x[:, c0:c0 + 4, 0:N], xT[c0 * 128:(c0 + 4) * 128, col0:col0 + N].rearrange("(c p) n -> p c n", p=128),
                      writes=["x%d" % c for c in range(c0, c0 + 4)], eng=("pool" if col0 == 0 else "sp"))

            def rmsnorm_rstd():
                pi = psalloc()
                for c in range(8):
                    r = ralloc()
                    act(tmpr[r][:, 0:N], x[:, c, 0:N], AF.Square, ["x%d" % c], ["tmpr%d" % r])
                    mm(pss[pi][:, 0:N], ones[:], tmpr[r][:, 0:N], c == 0, c == 7, ["ones", "tmpr%d" % r], [PK(pi)])
                    rfree(r)
                t1 = talloc()
                rs = talloc()
                act(T(t1)[:, 0:N], pss[pi][:, 0:N], AF.Ln, [PK(pi), "cst"], [TK(t1)], scale=1.0 / D, bias=cst[:, 0:1])
                act(T(rs)[:, 0:N], T(t1)[:, 0:N], AF.Exp, [TK(t1)], [TK(rs)], scale=-0.5)
                tfree(t1)
                psfree(pi)
                return rs

            def modulate(l, ai_, shift_chunk0):
                rs = rmsnorm_rstd()
                for c in range(8):
                    t1 = talloc()
                    eng = "pool" if c in (2, 5, 7) else "dve"
                    tt(T(t1)[:, 0:N], x[:, c, 0:N], T(rs)[:, 0:N], MUL, ["x%d" % c, TK(rs)], [TK(t1)], eng=eng)
                    if prompt:
                        act(h[:, c, 0:N], T(t1)[:, 0:N], AF.Identity, [TK(t1), "dmod%d" % l, "mod%d" % l], ["h%d" % c],
                            scale=dmod[:, l, ai_, c, 0:1], bias=mod[:, l, shift_chunk0 + c, 0:1])
                    else:
                        tt(V3(T(t1)[:, 0:N]), V3(T(t1)[:, 0:N]), bc_seq(dmod[:, l, ai_, c, seq0:seq0 + nseq]), MUL,
                           [TK(t1), "dmod%d" % l], [TK(t1)])
                        tt(V3(h[:, c, 0:N]), V3(T(t1)[:, 0:N]), bc_seq(mod[:, l, shift_chunk0 + c, seq0:seq0 + nseq]), ADD,
                           [TK(t1), "mod%d" % l], ["h%d" % c])
                    tfree(t1)
                tfree(rs)

            def resid_add(l, o, pi, gate18):
                if prompt:
                    stt(x[:, o, 0:N], pss[pi][:, 0:N], gate18[:, 0:1], x[:, o, 0:N], MUL, ADD,
                        [PK(pi), "x%d" % o, "mod%d" % l, "dmod%d" % l], ["x%d" % o])
                else:
                    t1 = talloc()
                    tt(V3(T(t1)[:, 0:N]), V3(pss[pi][:, 0:N]), bc_seq(gate18[:, seq0:seq0 + nseq]), MUL,
                       [PK(pi), "mod%d" % l, "dmod%d" % l], [TK(t1)])
                    tt(x[:, o, 0:N], x[:, o, 0:N], T(t1)[:, 0:N], ADD, [TK(t1), "x%d" % o], ["x%d" % o])
                    tfree(t1)

            for l in range(L):
                cb = l * NCL

                def col(name, i, cb=cb):
                    return cols[:, cb + CO[name] + i:cb + CO[name] + i + 1]

                modulate(l, 0, 0)

                rgst = {}
                rgst2 = {}

                def rg_A(c):
                    wb, wk = wtile(l, "rg%d" % c)
                    wv = wb[:, 0:2048].rearrange("p (k n) -> p k n", n=256)
                    p_rx = psalloc()
                    p_ry = psalloc()
                    for k in range(8):
                        mm(pss[p_rx][:, 0:N], wv[:, k, 0:128], h[:, k, 0:N], k == 0, k == 7, [wk, "h%d" % k], [PK(p_rx)])
                    for k in range(8):
                        mm(pss[p_ry][:, 0:N], wv[:, k, 128:256], h[:, k, 0:N], k == 0, k == 7, [wk, "h%d" % k], [PK(p_ry)])
                    ex = ralloc()
                    EK = "tmpr%d" % ex
                    exv = tmpr[ex][:, 0:nseq * (slen + 4)].rearrange("p (s t) -> p s t", t=slen + 4)
                    act(exv[:, :, 0:3], rgc[:, l, c, seq0:seq0 + nseq, :], AF.Identity, ["rgc", EK], [EK])
                    act(exv[:, :, 3:3 + slen], V3(pss[p_rx][:, 0:N]), AF.Identity, [PK(p_rx), "cols", EK], [EK], bias=col("bin", c))
                    act(rgc[:, l, c, seq0:seq0 + nseq, :], V3(pss[p_rx][:, 0:N])[:, :, slen - 3:slen], AF.Identity,
                        [PK(p_rx), "cols", "rgc"], ["rgc"], bias=col("bin", c))
                    gy = talloc()
                    act(T(gy)[:, 0:N], pss[p_ry][:, 0:N], AF.Gelu_apprx_tanh, [PK(p_ry), "cols"], [TK(gy)], bias=col("bin", 8 + c))
                    psfree(p_rx, p_ry)
                    dg, dk = mkdiag([col("rcw", j * 8 + c) for j in range(4)])
                    rgst[c] = (ex, gy, dg, dk)

                def rg_B(c):
                    ex, gy, dg, dk = rgst.pop(c)
                    EK = "tmpr%d" % ex
                    exv = tmpr[ex][:, 0:nseq * (slen + 4)].rearrange("p (s t) -> p s t", t=slen + 4)
                    gw, gk = wtile(l, "rgd%d" % c)
                    p_c = psalloc()
                    for j in range(4):
                        mm(V3(pss[p_c][:, 0:N]), dg[:, j, :], exv[:, :, j:j + slen], j == 0, j == 3, [dk, EK], [PK(p_c)])
                    xc = ralloc()
                    XK = "tmpr%d" % xc
                    act(tmpr[xc][:, 0:N], pss[p_c][:, 0:N], AF.Identity, [PK(p_c), "cols"], [XK], bias=col("rcb", c))
                    rfree(ex)
                    psfree(p_c)
                    xcf = tmpr[xc][:, 0:N].bitcast(F32)
                    p_a = psalloc()
                    p_x = psalloc()
                    mm(pss[p_a][:, 0:N], gw[:, 0:128], tmpr[xc][:, 0:N], True, True, [gk, XK], [PK(p_a)])
                    mm(pss[p_x][:, 0:N], gw[:, 128:256], tmpr[xc][:, 0:N], True, True, [gk, XK], [PK(p_x)])
                    tr_ = talloc()
                    ti_ = talloc()
                    TR, TI, GY = T(tr_)[:, 0:N], T(ti_)[:, 0:N], T(gy)[:, 0:N]
                    act(TR, pss[p_a][:, 0:N], AF.Tanh, [PK(p_a), "dcol"], [TK(tr_)], scale=0.5, bias=dcol[:, l, c:c + 1])
                    act(TI, pss[p_x][:, 0:N], AF.Tanh, [PK(p_x), "dcol"], [TK(ti_)], scale=0.5, bias=dcol[:, l, 8 + c:9 + c])
                    psfree(p_a, p_x)
                    a_ = talloc()
                    a2 = talloc()
                    A_, A2 = T(a_)[:, 0:N], T(a2)[:, 0:N]
                    act(A_, TR, AF.Exp, [TK(tr_), "dcol"], [TK(a_)], scale=dcol[:, l, 24 + c:25 + c], bias=dcol[:, l, 24 + c:25 + c])
                    act(A2, TR, AF.Exp, [TK(tr_), "dcol"], [TK(a2)], scale=dcol[:, l, 16 + c:17 + c], bias=dcol[:, l, 16 + c:17 + c])
                    act(A2, A2, AF.Ln, [TK(a2), "cst"], [TK(a2)], scale=-1.0, bias=cst[:, 2:3])
                    act(A2, A2, AF.Exp, [TK(a2)], [TK(a2)], scale=0.5)
                    rgst2[c] = (xc, tr_, ti_, gy, a_, a2)

                def rg_B2(c):
                    xc, tr_, ti_, gy, a_, a2 = rgst2.pop(c)
                    XK = "tmpr%d" % xc
                    xcf = tmpr[xc][:, 0:N].bitcast(F32)
                    TR, TI, GY = T(tr_)[:, 0:N], T(ti_)[:, 0:N], T(gy)[:, 0:N]
                    A_, A2 = T(a_)[:, 0:N], T(a2)[:, 0:N]
                    stt(TI, TI, 1.0, xcf, ADD, MUL, [TK(ti_), XK], [TK(ti_)])
                    stt(TI, TI, 0.5, A2, MUL, MUL, [TK(ti_), TK(a2)], [TK(ti_)])
                    rfree(xc)
                    if prompt:
                        scan(TR, A_, TI, rgh[:, l, c, seq0:seq0 + 1], [TK(a_), TK(ti_), "rgh", TK(tr_)], [TK(tr_)])
                    else:
                        a3, b3 = V3(A_), V3(TI)
                        t0 = talloc()
                        t0v = T(t0)[:, 0:nseq]
                        tt(t0v, a3[:, :, 0], rgh[:, l, c, seq0:seq0 + nseq], MUL, [TK(a_), "rgh"], [TK(t0)], eng="pool")
                        tt(b3[:, :, 0], b3[:, :, 0], t0v, ADD, [TK(ti_), TK(t0)], [TK(ti_)], eng="pool")
                        mset(a3[:, :, 0], 0.0, [TK(a_), TK(t0)], [TK(a_)], eng="pool")
                        tfree(t0)
                        scan(TR, A_, TI, 0.0, [TK(a_), TK(ti_), TK(tr_)], [TK(tr_)])
                    cp(rgh[:, l, c, seq0:seq0 + nseq], V3(TR)[:, :, slen - 1], [TK(tr_), "rgh"], ["rgh"], eng="pool")
                    tt(R1[:, c, 0:N], TR, GY, MUL, [TK(tr_), TK(gy)], ["R%d" % c])
                    tfree(tr_, ti_, gy, a_, a2)

                s5st = {}
                qst = {}

                def s5_A(p):
                    q, pp = divmod(p, 4)
                    if pp == 0:
                        wb, wk = wtile(l, "sx%d" % q)
                        wv = wb[:, 0:1024].rearrange("p (k n) -> p k n", n=128)
                        p_sx = psalloc()
                        for k in range(8):
                            mm(pss[p_sx][:, 0:N], wv[:, k, :], h[:, k, 0:N], k == 0, k == 7, [wk, "h%d" % k], [PK(p_sx)])
                        sx = ralloc()
                        SXK = "tmpr%d" % sx
                        act(tmpr[sx][:, 0:N], pss[p_sx][:, 0:N], AF.Identity, [PK(p_sx), "cols"], [SXK], bias=col("bin", 16 + q))
                        psfree(p_sx)
                        bwb, bwk = wbuf[3], "wbuf3a"
                        P.dma(bwb[:, 0:1024], s5b_d[l, :, q, :], reads=["s5b_d"], writes=[bwk, "wbuf3"])
                        qst[q] = dict(sx=sx, bwb=bwb, bwk=bwk, p_y=psalloc())
                    Q = qst[q]
                    sx = Q["sx"]
                    SXK = "tmpr%d" % sx
                    bwv = Q["bwb"][:, 0:1024].rearrange("p (a b n) -> p a b n", a=2, n=128)
                    p_vr = psalloc()
                    p_vi = psalloc()
                    mm(pss[p_vr][:, 0:N], bwv[:, 0, pp, :], tmpr[sx][:, 0:N], True, True, [Q["bwk"], SXK], [PK(p_vr)])
                    mm(pss[p_vi][:, 0:N], bwv[:, 1, pp, :], tmpr[sx][:, 0:N], True, True, [Q["bwk"], SXK], [PK(p_vi)])
                    tb, tk = tabload(l, p, not prompt)
                    s5st[p] = (p_vr, p_vi, tb, tk)

                def s5_B(p):
                    q, pp = divmod(p, 4)
                    Q = qst[q]
                    p_vr, p_vi, tb, tk = s5st.pop(p)
                    if pp == 0:
                        cwb, cwk = wbuf[3][:, 1024:2048], "wbuf3b"
                        o_, s_ = SOFF["s5c%d" % q]
                        P.dma(cwb, wst[l][:, o_:o_ + s_], writes=[cwk, "wbuf3"])
                        act(cwb[:, 512:1024], cwb[:, 512:1024].bitcast(F32), AF.Identity, [cwk], [cwk], scale=-1.0)
                        Q["cwb"], Q["cwk"] = cwb, cwk
                    if prompt:
                        Cv, Sv = tb[:, 0:N], tb[:, 512:512 + N]
                        cL, sL = tb[:, N - 1:N], tb[:, 512 + N - 1:512 + N]
                    else:
                        Cv = tb[:, 0:8].unsqueeze(1).to_broadcast([128, nseq, 8])
                        Sv = tb[:, 8:16].unsqueeze(1).to_broadcast([128, nseq, 8])
                        cL, sL = tb[:, 7:8], tb[:, 15:16]
                    t1, t2, wr, wi = talloc(), talloc(), talloc(), talloc()
                    K1_, K2_, KR, KI = TK(t1), TK(t2), TK(wr), TK(wi)
                    T1, T2, WR, WI = T(t1)[:, 0:N], T(t2)[:, 0:N], T(wr)[:, 0:N], T(wi)[:, 0:N]
                    vr, vi = DV(pss[p_vr][:, 0:N]), DV(pss[p_vi][:, 0:N])
                    tt(DV(T1), vr, Cv, MUL, [PK(p_vr), tk], [K1_])
                    tt(DV(T2), vi, Sv, MUL, [PK(p_vi), tk], [K2_])
                    tt(WR, T1, T2, ADD, [K1_, K2_], [KR])
                    tt(DV(T1), vi, Cv, MUL, [PK(p_vi), tk], [K1_])
                    tt(DV(T2), vr, Sv, MUL, [PK(p_vr), tk], [K2_])
                    tt(WI, T1, T2, SUB, [K1_, K2_], [KI])
                    psfree(p_vr, p_vi)
                    if prompt:
                        rho_b = s5p[:, l, 0, p:p + 1].to_broadcast([128, N])
                        scan(T1, rho_b, WR, s5s[:, l, 0, p, seq0:seq0 + 1], [KR, "s5s", "s5p", K1_], [K1_])
                        scan(T2, rho_b, WI, s5s[:, l, 1, p, seq0:seq0 + 1], [KI, "s5s", "s5p", K2_], [K2_])
                        zr_e, zi_e = T1[:, N - 1:N], T2[:, N - 1:N]
                        nst = 1
                    else:
                        rm = talloc()
                        RM = T(rm)[:, 0:N]
                        cp(V3(RM), rhoM[:, l, p, :].unsqueeze(1).to_broadcast([128, nseq, 8]), ["rhoM"], [TK(rm)], eng="pool")
                        stt(V3(WR)[:, :, 0], s5s[:, l, 0, p, seq0:seq0 + nseq], s5p[:, l, 0, p:p + 1], V3(WR)[:, :, 0], MUL, ADD,
                            [KR, "s5s", "s5p"], [KR])
                        stt(V3(WI)[:, :, 0], s5s[:, l, 1, p, seq0:seq0 + nseq], s5p[:, l, 0, p:p + 1], V3(WI)[:, :, 0], MUL, ADD,
                            [KI, "s5s", "s5p"], [KI])
                        scan(T1, RM, WR, 0.0, [KR, TK(rm), K1_], [K1_])
                        scan(T2, RM, WI, 0.0, [KI, TK(rm), K2_], [K2_])
                        tfree(rm)
                        zr_e, zi_e = V3(T1)[:, :, 7], V3(T2)[:, :, 7]
                        nst = nseq
                    tfree(wr, wi)
                    us = [ralloc() for _ in range(4)]
                    UK = ["tmpr%d" % u for u in us]
                    U = [tmpr[u][:, 0:N] for u in us]
                    tt(DV(U[0]), DV(T1), Cv, MUL, [K1_, tk], [UK[0]])
                    stt(DV(U[1]), DV(T2), -1.0, Sv, MUL, MUL, [K2_, tk], [UK[1]])
                    tt(DV(U[2]), DV(T2), Cv, MUL, [K2_, tk], [UK[2]], eng="pool")
                    tt(DV(U[3]), DV(T1), Sv, MUL, [K1_, tk], [UK[3]], eng="pool")
                    e1 = talloc()
                    E = T(e1)
                    EK1 = TK(e1)
                    ts(E[:, 0:nst], zr_e, cL, 0.0, MUL, ADD, [K1_, tk], [EK1], eng="pool")
                    ts(E[:, 32:32 + nst], zi_e, sL, 0.0, MUL, ADD, [K2_, tk, EK1], [EK1], eng="pool")
                    tt(s5s[:, l, 0, p, seq0:seq0 + nst], E[:, 0:nst], E[:, 32:32 + nst], SUB, [EK1, "s5s"], ["s5s"], eng="pool")
                    ts(E[:, 64:64 + nst], zi_e, cL, 0.0, MUL, ADD, [K2_, tk, EK1], [EK1], eng="pool")
                    ts(E[:, 96:96 + nst], zr_e, sL, 0.0, MUL, ADD, [K1_, tk, EK1], [EK1], eng="pool")
                    tt(s5s[:, l, 1, p, seq0:seq0 + nst], E[:, 64:64 + nst], E[:, 96:96 + nst], ADD, [EK1, "s5s"], ["s5s"], eng="pool")
                    tfree(e1)
                    tfree(t1, t2)
                    cwv = Q["cwb"].rearrange("p (a b n) -> p a b n", a=2, n=128)
                    p_y = Q["p_y"]
                    mm(pss[p_y][:, 0:N], cwv[:, 0, pp, :], U[0], pp == 0, False, [Q["cwk"], UK[0]], [PK(p_y)])
                    mm(pss[p_y][:, 0:N], cwv[:, 0, pp, :], U[1], False, False, [Q["cwk"], UK[1]], [PK(p_y)])
                    mm(pss[p_y][:, 0:N], cwv[:, 1, pp, :], U[2], False, False, [Q["cwk"], UK[2]], [PK(p_y)])
                    mm(pss[p_y][:, 0:N], cwv[:, 1, pp, :], U[3], False, pp == 3, [Q["cwk"], UK[3]], [PK(p_y)])
                    rfree(*us)
                    if pp == 3:
                        sx = Q["sx"]
                        ty = talloc()
                        stt(T(ty)[:, 0:N], tmpr[sx][:, 0:N].bitcast(F32), col("s5d", q), pss[p_y][:, 0:N], MUL, ADD,
                            ["tmpr%d" % sx, "cols", PK(p_y)], [TK(ty)])
                        act(R1[:, 8 + q, 0:N], T(ty)[:, 0:N], AF.Gelu_apprx_tanh, [TK(ty)], ["R%d" % (8 + q)])
                        tfree(ty)
                        rfree(sx)
                        psfree(p_y)
                        del qst[q]

                wstate["nrot"] = 3
                rg_A(0)
                s5_A(0)
                for c in range(8):
                    rg_B(c)
                    if c + 1 < 8:
                        rg_A(c + 1)
                    for p in (2 * c, 2 * c + 1):
                        if p + 1 < 16:
                            s5_A(p + 1)
                        s5_B(p)
                    rg_B2(c)
                    ada_step()
                wstate["nrot"] = NWBUF

                for m in range(8):
                    wa_, wak = wtile(l, "mga%d" % m)
                    wav = wa_[:, 0:2048].rearrange("p (k n) -> p k n", n=256)
                    p_ga = psalloc()
                    p_gb = psalloc()
                    for (pi, off) in ((p_ga, 0), (p_gb, 128)):
                        for k in range(8):
                            mm(pss[pi][:, 0:N], wav[:, k, off:off + 128], h[:, k, 0:N], k == 0, k == 7, [wak, "h%d" % k], [PK(pi)])
                    wb_, wbk = wtile(l, "mgb%d" % m)
                    wrg = wb_[:, 0:1024].rearrange("p (k n) -> p k n", n=128)
                    wgl = wb_[:, 1024:2048].rearrange("p (k n) -> p k n", n=256)
                    p_ba = psalloc()
                    p_la = psalloc()
                    p_lb = psalloc()
                    for k in range(8):
                        mm(pss[p_ba][:, 0:N], wrg[:, k, :], R1[:, k, 0:N], k == 0, k == 7, [wbk, "R%d" % k], [PK(p_ba)])
                    for (pi, off) in ((p_la, 0), (p_lb, 128)):
                        for k in range(4):
                            mm(pss[pi][:, 0:N], wgl[:, k, off:off + 128], R1[:, 8 + k, 0:N], k == 0, k == 3,
                               [wbk, "R%d" % (8 + k)], [PK(pi)])
                    tga = talloc()
                    tgb = talloc()
                    tgl = talloc()
                    GA, GB, GL = T(tga)[:, 0:N], T(tgb)[:, 0:N], T(tgl)[:, 0:N]
                    act(GA, pss[p_ga][:, 0:N], AF.Tanh, [PK(p_ga), "dcol"], [TK(tga)], scale=0.5, bias=dcol[:, l, 32 + m:33 + m])
                    act(GB, pss[p_gb][:, 0:N], AF.Tanh, [PK(p_gb), "dcol"], [TK(tgb)], scale=0.5, bias=dcol[:, l, 40 + m:41 + m])
                    act(GL, pss[p_lb][:, 0:N], AF.Tanh, [PK(p_lb)], [TK(tgl)], scale=0.5)
                    psfree(p_ga, p_gb, p_lb)
                    stt(GA, GA, 1.0, pss[p_ba][:, 0:N], ADD, MUL, [TK(tga), PK(p_ba)], [TK(tga)])
                    stt(GL, GL, 1.0, pss[p_la][:, 0:N], ADD, MUL, [TK(tgl), PK(p_la)], [TK(tgl)])
                    psfree(p_ba, p_la)
                    stt(GB, GB, 1.0, GL, ADD, MUL, [TK(tgb), TK(tgl)], [TK(tgb)])
                    stt(R1[:, 12 + m, 0:N], GB, 0.5, GA, MUL, ADD, [TK(tgb), TK(tga)], ["R%d" % (12 + m)])
                    tfree(tga, tgb, tgl)
                    ada_step()

                for o2 in range(4):
                    wb, wk = wtile(l, "wo%d" % o2)
                    wv = wb[:, 0:2048].rearrange("p (k n) -> p k n", n=256)
                    for hh in range(2):
                        o = 2 * o2 + hh
                        pi = psalloc()
                        for k in range(8):
                            mm(pss[pi][:, 0:N], wv[:, k, hh * 128:(hh + 1) * 128], R1[:, 12 + k, 0:N], k == 0, k == 7,
                               [wk, "R%d" % (12 + k)], [PK(pi)])
                        resid_add(l, o, pi, dmod[:, l, 1, o, :])
                        psfree(pi)

                modulate(l, 2, 24)
                fst = {}

                def ffn_A(j):
                    wb, wk = wtile(l, "up%d" % j)
                    wv = wb[:, 0:2048].rearrange("p (k n) -> p k n", n=256)
                    st_ = []
                    for half in range(2):
                        cidx = j + 24 * half
                        pi = psalloc()
                        for k in range(8):
                            mm(pss[pi][:, 0:N], wv[:, k, half * 128:(half + 1) * 128], h[:, k, 0:N], k == 0, k == 7,
                               [wk, "h%d" % k], [PK(pi)])
                        ex = talloc()
                        EK = TK(ex)
                        exv = T(ex)[:, 0:nseq * (slen + 2)].rearrange("p (s t) -> p s t", t=slen + 2)
                        cp(exv[:, :, 0:2], ffc[:, l, cidx, seq0:seq0 + nseq, :], ["ffc", EK], [EK], eng="pool")
                        act(exv[:, :, 2:2 + slen], V3(pss[pi][:, 0:N]), AF.Identity, [PK(pi), EK], [EK])
                        act(ffc[:, l, cidx, seq0:seq0 + nseq, :], V3(pss[pi][:, 0:N])[:, :, slen - 2:slen], AF.Identity,
                            [PK(pi), "ffc"], ["ffc"])
                        acc = talloc()
                        AK = TK(acc)
                        a3 = V3(T(acc)[:, 0:N])
                        act(a3, exv[:, :, 0:slen], AF.Identity, [EK, "cols"], [AK], scale=col("fcw", cidx), bias=col("fcb", cidx))
                        st_.append((pi, ex, acc))
                    fst[j] = st_

                def ffn_B(j):
                    st_ = fst.pop(j)
                    accs = []
                    for half in range(2):
                        cidx = j + 24 * half
                        pi, ex, acc = st_[half]
                        EK, AK = TK(ex), TK(acc)
                        exv = T(ex)[:, 0:nseq * (slen + 2)].rearrange("p (s t) -> p s t", t=slen + 2)
                        a3 = V3(T(acc)[:, 0:N])
                        stt(a3, exv[:, :, 1:1 + slen], col("fcw", 48 + cidx), a3, MUL, ADD, [EK, "cols", AK], [AK])
                        stt(T(acc)[:, 0:N], pss[pi][:, 0:N], col("fcw", 96 + cidx), T(acc)[:, 0:N], MUL, ADD, [PK(pi), "cols", AK], [AK])
                        psfree(pi)
                        tfree(ex)
                        accs.append(acc)
                    ua, ub = accs
                    act(T(ua)[:, 0:N], T(ua)[:, 0:N], AF.Gelu_apprx_tanh, [TK(ua)], [TK(ua)])
                    tt(R1[:, j, 0:N], T(ua)[:, 0:N], T(ub)[:, 0:N], MUL, [TK(ua), TK(ub)], ["R%d" % j])
                    tfree(ua, ub)

                ffn_A(0)
                for j in range(24):
                    if j + 1 < 24:
                        ffn_A(j + 1)
                    ffn_B(j)
                    if j % 3 == 2:
                        ada_step()

                for o2 in range(4):
                    pis = [psalloc(), psalloc()]
                    for i in range(3):
                        wb, wk = wtile(l, "dn%d_%d" % (o2, i))
                        wv = wb[:, 0:2048].rearrange("p (k n) -> p k n", n=256)
                        for hh in range(2):
                            for k in range(8):
                                kk = 8 * i + k
                                mm(pss[pis[hh]][:, 0:N], wv[:, k, hh * 128:(hh + 1) * 128], R1[:, kk, 0:N], kk == 0, kk == 23,
                                   [wk, "R%d" % kk], [PK(pis[hh])])
                    for hh in range(2):
                        resid_add(l, 2 * o2 + hh, pis[hh], mod[:, l, 40 + 2 * o2 + hh, :])
                    psfree(*pis)

            rs = rmsnorm_rstd()
            for c in range(8):
                t1 = talloc()
                tt(T(t1)[:, 0:N], x[:, c, 0:N], T(rs)[:, 0:N], MUL, ["x%d" % c, TK(rs)], [TK(t1)])
                act(T(t1)[:, 0:N], T(t1)[:, 0:N], AF.Identity, [TK(t1), "cols"], [TK(t1)], scale=cols[:, 2 * NCL + c:2 * NCL + c + 1])
                P.dma(yT[c * 128:(c + 1) * 128, col0:col0 + N], T(t1)[:, 0:N], reads=[TK(t1)],
                      eng=("sp" if col0 == SEQ else "pool"))
                tfree(t1)
            tfree(rs)

        tiles = [(512 * i, 512, 1, 512, 0) for i in range(4)] + [(SEQ, NS * ST, NS, ST, 1)]
        for tile_ in tiles:
            run_tile(*tile_)

        for l in range(L):
            P.dma(o_rgh[l], rgh[:, l], reads=["rgh"])
            P.dma(o_rgc[l].rearrange("p c s j -> p (c s j)"), rgc[:, l].rearrange("p c s j -> p (c s j)"), reads=["rgc"])
            for ri in range(2):
                P.dma(o_s5[l, ri], s5s[:, l, ri], reads=["s5s"])
            P.dma(o_ffn[l].rearrange("p c s j -> p (c s j)"), ffc[:, l].rearrange("p c s j -> p (c s j)"), reads=["ffc"])
        P.finish()
    return nc


_NC = None


def kernel(**inp):
    global _NC
    inp = {k: np.asarray(v) for k, v in inp.items()}
    if _NC is None:
        _NC = build()
    cols = host_cols(inp)
    s5col, s5bp = host_s5(inp)
    wsts = [host_stream(inp, l) for l in range(L)]
    wada = host_ada(inp)
    in_maps = []
    for core in range(8):
        ss = slice(core * NS, (core + 1) * NS)
        xT = np.empty((D, NTOK), np.float32)
        xT[:, :SEQ] = inp["x_prompt"][core].T
        xT[:, SEQ:] = inp["x_sample"][ss].reshape(NS * ST, D).T
        cT = np.zeros((D, 18), np.float32)
        cT[:, 0] = inp["c_prompt"][core]
        cT[:, 1:17] = inp["c_sample"][ss].T
        st_rgh = inp["state_rg_h"][ss].reshape(NS, L, 8, 128).transpose(1, 3, 2, 0)
        st_rgc = inp["state_rg_conv"][ss].reshape(NS, L, 3, 8, 128).transpose(1, 4, 3, 0, 2)
        s5 = np.stack([inp["state_s5_re"][ss], inp["state_s5_im"][ss]], 0)
        s5 = s5.reshape(2, NS, L, 16, 2, 64).transpose(2, 0, 4, 5, 3, 1).reshape(L, 2, 128, 16, NS)
        st_ffn = inp["state_ffn_conv"][ss].reshape(NS, L, 2, 48, 128).transpose(1, 4, 3, 0, 2)
        m = {"xT": xT, "cT": cT, "cols": cols, "st_rgh": np.ascontiguousarray(st_rgh),
             "st_rgc": np.ascontiguousarray(st_rgc), "st_s5": np.ascontiguousarray(s5),
             "st_ffn": np.ascontiguousarray(st_ffn), "s5col": s5col, "s5bp": s5bp,
             "wst0": wsts[0], "wst1": wsts[1], "wada": wada}
        in_maps.append(m)
    res = run_bass_kernel_spmd(_NC, in_maps, core_ids=list(range(8)))
    R = res.results
    y_p = np.empty((8, SEQ, D), np.float32)
    y_s = np.empty((128, ST, D), np.float32)
    rg_h = np.empty((8 * 17, L, D), np.float32)
    rg_c = np.empty((8 * 17, L, 3, D), np.float32)
    s5r = np.empty((8 * 17, L, 32, 64), np.float32)
    s5i = np.empty((8 * 17, L, 32, 64), np.float32)
    ffn = np.empty((8 * 17, L, 2, 2 * DFF), np.float32)
    for core in range(8):
        r = R[core]
        yT = r["yT"]
        y_p[core] = yT[:, :SEQ].T
        y_s[core * NS:(core + 1) * NS] = yT[:, SEQ:].T.reshape(NS, ST, D)
        sl = slice(core * 17, (core + 1) * 17)
        rg_h[sl] = r["o_rgh"].transpose(3, 0, 2, 1).reshape(17, L, D)
        rg_c[sl] = r["o_rgc"].transpose(3, 0, 4, 2, 1).reshape(17, L, 3, D)
        o5 = r["o_s5"].reshape(L, 2, 2, 64, 16, 17).transpose(1, 5, 0, 4, 2, 3).reshape(2, 17, L, 32, 64)
        s5r[sl] = o5[0]
        s5i[sl] = o5[1]
        ffn[sl] = r["o_ffn"].transpose(3, 0, 4, 2, 1).reshape(17, L, 2, 2 * DFF)
    idx_p = np.arange(8) * 17
    idx_s = (np.arange(8)[:, None] * 17 + 1 + np.arange(16)[None, :]).reshape(-1)
    return (y_p, y_s, rg_h[idx_p], rg_c[idx_p], s5r[idx_p], s5i[idx_p], ffn[idx_p],
            rg_h[idx_s], rg_c[idx_s], s5r[idx_s], s5i[idx_s], ffn[idx_s])
```

```python
import contextlib
import math
import numpy as np
import concourse.bass as bass
import concourse.mybir as mybir
from concourse.bass_utils import run_bass_kernel_spmd

F32 = mybir.dt.float32
F32R = mybir.dt.float32r
AF = mybir.ActivationFunctionType
ALU = mybir.AluOpType

D = 1024
SEQ = 2048
NS = 16
ST = 8
NTOK = SEQ + NS * ST
L = 2
INW = 4608
DFF = 3072
LS = 64
ENGS = ("pe", "act", "dve", "pool", "sp")
N_DMA_SEMS = 24
SELF_SYNC = True
WB = 2048
NWBUF = 4
NTMP = 12
NTMPR = 8

CO = {}
_o = 0
for _n, _w in (("g1", 8), ("g2", 8), ("bin", 36), ("rcw", 32), ("rcb", 8), ("ba", 8), ("bx", 8), ("lam", 8),
               ("s5d", 4), ("fcw", 144), ("fcb", 48), ("bada", 48)):
    CO[_n] = _o
    _o += _w
NCL = _o
NCOL = 2 * NCL + 8


class Prog:
    def __init__(self, nc, stack):
        self.nc = nc
        self.stack = stack
        self.ops = {e: [] for e in ENGS}
        self.sem = {e: stack.enter_context(nc.semaphore("s_" + e)) for e in ENGS}
        self.cnt = {e: 0 for e in ENGS}
        self.dsem = [stack.enter_context(nc.semaphore("d%d" % i)) for i in range(3 * N_DMA_SEMS)]
        self.dcnt = [0] * (3 * N_DMA_SEMS)
        self.dnext = {"sp": 0, "pool": 0, "act": 0}
        self.seen = {e: {} for e in ENGS}
        self.tiles = {}
        self.nbuf = 0

    def sb(self, shape, dtype=F32, name=None):
        self.nbuf += 1
        return self.stack.enter_context(self.nc.sbuf_tensor("S_" + (name or ("t%d" % self.nbuf)), list(shape), dtype))

    def ps(self, shape, dtype=F32, name=None):
        self.nbuf += 1
        return self.stack.enter_context(self.nc.psum_tensor("P_" + (name or ("p%d" % self.nbuf)), list(shape), dtype))

    def _semobj(self, key):
        return self.sem[key] if isinstance(key, str) else self.dsem[key]

    def _deps(self, eng, reads, writes):
        need = {}

        def req(ev):
            if ev is None:
                return
            k, v = ev
            if k == eng and (eng == "pe" or not SELF_SYNC):
                return
            if need.get(k, 0) < v:
                need[k] = v

        for t in reads:
            st = self.tiles.get(t)
            if st:
                req(st["w"])
        for t in writes:
            st = self.tiles.get(t)
            if st:
                req(st["w"])
                for ev in st["r"]:
                    req(ev)
        waits = []
        for k, v in need.items():
            if self.seen[eng].get(k, 0) < v:
                self.seen[eng][k] = v
                waits.append((k, v))
        return waits

    def _commit(self, ev, reads, writes):
        for t in reads:
            st = self.tiles.setdefault(t, {"w": None, "r": []})
            st["r"].append(ev)
            if len(st["r"]) > 16:
                best = {}
                for k, v in st["r"]:
                    if best.get(k, 0) < v:
                        best[k] = v
                st["r"] = list(best.items())
        for t in writes:
            self.tiles[t] = {"w": ev, "r": []}

    def op(self, eng, fn, reads=(), writes=()):
        waits = self._deps(eng, reads, writes)
        self.cnt[eng] += 1
        ev = (eng, self.cnt[eng])
        sem = self.sem[eng]

        def emit(e, fn=fn, waits=waits, sem=sem):
            for k, v in waits:
                e.wait_ge(self._semobj(k), v)
            fn(e).then_inc(sem, 1)

        self.ops[eng].append(emit)
        self._commit(ev, reads, writes)

    def dma(self, out, in_, reads=(), writes=(), eng="sp"):
        k = self.dnext[eng] + {"sp": 0, "pool": N_DMA_SEMS, "act": 2 * N_DMA_SEMS}[eng]
        self.dnext[eng] = (self.dnext[eng] + 1) % N_DMA_SEMS
        waits = self._deps(eng, reads, writes)
        prev = self.dcnt[k]
        if prev and self.seen[eng].get(k, 0) < prev:
            self.seen[eng][k] = prev
            waits.append((k, prev))
        self.dcnt[k] += 16
        ev = (k, self.dcnt[k])
        dsem = self.dsem[k]

        def emit(e, waits=waits, out=out, in_=in_, dsem=dsem):
            for kk, v in waits:
                e.wait_ge(self._semobj(kk), v)
            e.dma_start(out=out, in_=in_).then_inc(dsem, 16)

        self.ops[eng].append(emit)
        self._commit(ev, reads, writes)

    def finish(self):
        fin = [(k, self.dcnt[k]) for k in range(3 * N_DMA_SEMS) if self.dcnt[k]]

        def emit_fin(e):
            for k, v in fin:
                e.wait_ge(self.dsem[k], v)

        self.ops["sp"].append(emit_fin)
        nc = self.nc
        with nc.Block() as block:
            @block.tensor
            def _(e):
                for f in self.ops["pe"]:
                    f(e)

            @block.scalar
            def _(e):
                for f in self.ops["act"]:
                    f(e)

            @block.vector
            def _(e):
                for f in self.ops["dve"]:
                    f(e)

            @block.gpsimd
            def _(e):
                for f in self.ops["pool"]:
                    f(e)

            @block.sync
            def _(e):
                for f in self.ops["sp"]:
                    f(e)


def stream_tiles():
    t = []
    for c in range(8):
        t.append(("rg%d" % c, 2048))
        t.append(("rgd%d" % c, 256))
    for q in range(4):
        t.append(("sx%d" % q, 1024))
        t.append(("s5c%d" % q, 1024))
    for m in range(8):
        t.append(("mga%d" % m, 2048))
        t.append(("mgb%d" % m, 2048))
    for o in range(4):
        t.append(("wo%d" % o, 2048))
    for j in range(24):
        t.append(("up%d" % j, 2048))
    for o in range(4):
        for i in range(3):
            t.append(("dn%d_%d" % (o, i), 2048))
    return t


STILES = stream_tiles()
SOFF = {}
_o = 0
for _n, _s in STILES:
    SOFF[_n] = (_o, _s)
    _o += _s
SLEN = _o
ADA_TILES = 24


def _pack(W, kc, cols, k0=0):
    cols = np.asarray(cols)
    blk = W[k0 * 128:(k0 + kc) * 128][:, cols].reshape(kc, 128, len(cols))
    return np.ascontiguousarray(blk.transpose(1, 0, 2)).reshape(128, kc * len(cols))


def _ar(a, n=128):
    return np.arange(a, a + n)


def host_stream(inp, l):
    w_in = inp["w_in"][l]
    out = np.zeros((128, SLEN), np.float32)

    def put(name, arr):
        o, s = SOFF[name]
        assert arr.shape == (128, s), (name, arr.shape, s)
        out[:, o:o + s] = arr

    g = np.zeros((128, 8, 256), np.float32)
    for c in range(8):
        for hh in range(2):
            h = 2 * c + hh
            g[hh * 64:(hh + 1) * 64, c, hh * 64:(hh + 1) * 64] = inp["rg_wa"][l, h]
            g[hh * 64:(hh + 1) * 64, c, 128 + hh * 64:128 + (hh + 1) * 64] = inp["rg_wx"][l, h]
    for c in range(8):
        put("rgd%d" % c, np.ascontiguousarray(g[:, c, :]))
    for c in range(8):
        put("rg%d" % c, _pack(w_in, 8, np.concatenate([_ar(c * 128), _ar(1024 + c * 128)])))
    cre = inp["s5_c_re"][l]
    cim = inp["s5_c_im"][l]
    for q in range(4):
        put("sx%d" % q, _pack(w_in, 8, _ar(2048 + q * 128)))
        cc = np.zeros((128, 2, 4, 128), np.float32)
        for pp in range(4):
            for j in range(2):
                gidx = 2 * (4 * q + pp) + j
                sl = 16 * (gidx % 8)
                cc[64 * j:64 * j + 64, 0, pp, sl:sl + 16] = cre[gidx].T
                cc[64 * j:64 * j + 64, 1, pp, sl:sl + 16] = cim[gidx].T
        put("s5c%d" % q, cc.reshape(128, 1024))
    for m in range(8):
        put("mga%d" % m, _pack(w_in, 8, np.concatenate([_ar(2560 + m * 128), _ar(3584 + m * 128)])))
        a = _pack(inp["w_rg_proj"][l], 8, _ar(m * 128))
        b = _pack(inp["w_glu"][l], 4, np.concatenate([_ar(m * 128), _ar(1024 + m * 128)]))
        put("mgb%d" % m, np.concatenate([a, b], axis=1))
    for o in range(4):
        put("wo%d" % o, _pack(inp["w_out"][l], 8, _ar(o * 256, 256)))
    for j in range(24):
        put("up%d" % j, _pack(inp["w_up"][l], 8, np.concatenate([_ar(j * 128), _ar(DFF + j * 128)])))
    for o in range(4):
        for i in range(3):
            put("dn%d_%d" % (o, i), _pack(inp["w_down"][l], 8, _ar(o * 256, 256), k0=8 * i))
    return out


def host_ada(inp):
    out = np.zeros((128, L * ADA_TILES * 2048), np.float32)
    for l in range(L):
        for t in range(ADA_TILES):
            out[:, (l * ADA_TILES + t) * 2048:(l * ADA_TILES + t + 1) * 2048] = _pack(inp["w_ada"][l], 8, _ar(t * 256, 256))
    return out


def host_cols(inp):
    c = np.zeros((128, NCOL), np.float32)

    def colv(v):
        return np.asarray(v).reshape(-1, 128).T

    for l in range(L):
        b = l * NCL
        c[:, b + CO["g1"]:b + CO["g1"] + 8] = colv(inp["g_norm1"][l])
        c[:, b + CO["g2"]:b + CO["g2"] + 8] = colv(inp["g_norm2"][l])
        c[:, b + CO["bin"]:b + CO["bin"] + 36] = colv(inp["b_in"][l])
        for j in range(4):
            c[:, b + CO["rcw"] + j * 8:b + CO["rcw"] + j * 8 + 8] = colv(inp["rg_conv_w"][l, j])
        c[:, b + CO["rcb"]:b + CO["rcb"] + 8] = colv(inp["rg_conv_b"][l])
        c[:, b + CO["ba"]:b + CO["ba"] + 8] = colv(inp["rg_ba"][l])
        c[:, b + CO["bx"]:b + CO["bx"] + 8] = colv(inp["rg_bx"][l])
        c[:, b + CO["lam"]:b + CO["lam"] + 8] = colv(inp["rg_lambda"][l])
        c[:, b + CO["s5d"]:b + CO["s5d"] + 4] = colv(inp["s5_d"][l])
        for j in range(3):
            c[:, b + CO["fcw"] + j * 48:b + CO["fcw"] + j * 48 + 48] = colv(inp["ffn_conv_w"][l, j])
        c[:, b + CO["fcb"]:b + CO["fcb"] + 48] = colv(inp["ffn_conv_b"][l])
        c[:, b + CO["bada"]:b + CO["bada"] + 48] = colv(inp["b_ada"][l])
    c[:, 2 * NCL:2 * NCL + 8] = colv(inp["g_final"])
    return c


def host_s5(inp):
    col = np.zeros((L, 128, 3, 16), np.float32)
    bp = np.zeros((L, 128, 2, 16, 128), np.float32)
    for l in range(L):
        for p in range(16):
            for j in range(2):
                g = 2 * p + j
                col[l, 64 * j:64 * j + 64, 0, p] = inp["s5_lam_re"][l, g]
                col[l, 64 * j:64 * j + 64, 1, p] = inp["s5_lam_im"][l, g]
                col[l, 64 * j:64 * j + 64, 2, p] = inp["s5_log_dt"][l, g]
                sl = 16 * (g % 8)
                bp[l, sl:sl + 16, 0, p, 64 * j:64 * j + 64] = inp["s5_b_re"][l, g].T
                bp[l, sl:sl + 16, 1, p, 64 * j:64 * j + 64] = inp["s5_b_im"][l, g].T
    return col, bp.reshape(L, 128, 2, 2048)


def build():
    nc = bass.Bass("TRN2", target_bir_lowering=False)
    nc.dge_precook = False

    def din(name, shape, dt=F32):
        return nc.dram_tensor(name, list(shape), dt, kind="ExternalInput").ap()

    def dout(name, shape, dt=F32):
        return nc.dram_tensor(name, list(shape), dt, kind="ExternalOutput").ap()

    xT = din("xT", [D, NTOK])
    cT = din("cT", [D, 18])
    cols_d = din("cols", [128, NCOL])
    st_rgh = din("st_rgh", [L, 128, 8, NS])
    st_rgc = din("st_rgc", [L, 128, 8, NS, 3])
    st_s5 = din("st_s5", [L, 2, 128, 16, NS])
    st_ffn = din("st_ffn", [L, 128, 48, NS, 2])
    s5col_d = din("s5col", [L, 128, 3, 16])
    s5bp_d = din("s5bp", [L, 128, 2, 2048])
    wst = [din("wst%d" % l, [128, SLEN], F32R) for l in range(L)]
    wada = din("wada", [128, L * ADA_TILES * 2048], F32R)
    yT = dout("yT", [D, NTOK])
    o_rgh = dout("o_rgh", [L, 128, 8, 17])
    o_rgc = dout("o_rgc", [L, 128, 8, 17, 3])
    o_s5 = dout("o_s5", [L, 2, 128, 16, 17])
    o_ffn = dout("o_ffn", [L, 128, 48, 17, 2])
    s5b_d = nc.dram_tensor("s5b_scratch", [L, 128, 4, 1024], F32R, kind="Internal").ap()
    tab_d = nc.dram_tensor("tab_scratch", [L, 16, 128, 2, 512], F32, kind="Internal").ap()

    with contextlib.ExitStack() as stack:
        P = Prog(nc, stack)
        x = P.sb([128, 8, 512], F32, "x")
        h = P.sb([128, 8, 512], F32R, "h")
        R1 = P.sb([128, 24, 512], F32R, "R1")
        wbuf = [P.sb([128, WB], F32R, "wbuf%d" % i) for i in range(NWBUF)]
        tmps = [P.sb([128, 520], F32, "tmp%d" % i) for i in range(NTMP)]
        tmpr = [P.sb([128, 520], F32R, "tmpr%d" % i) for i in range(NTMPR)]
        pss = [P.ps([128, 512], F32, "ps%d" % i) for i in range(8)]
        cols = P.sb([128, NCOL], F32, "cols")
        dcol = P.sb([128, L, 64], F32, "dcol")
        mod = P.sb([128, L, 48, 18], F32, "mod")
        dmod = P.sb([128, L, 3, 8, 18], F32, "dmod")
        ones = P.sb([128, 128], F32R, "ones")
        ident = P.sb([128, 128], F32, "ident")
        cst = P.sb([128, 8], F32, "cst")
        rgh = P.sb([128, L, 8, 17], F32, "rgh")
        rgc = P.sb([128, L, 8, 17, 3], F32, "rgc")
        s5s = P.sb([128, L, 2, 16, 17], F32, "s5s")
        ffc = P.sb([128, L, 48, 17, 2], F32, "ffc")
        tbuf = [P.sb([128, 1024], F32, "tbuf%d" % i) for i in range(2)]
        dgb = [P.sb([128, 6, 128], F32R, "dgb%d" % i) for i in range(2)]
        cC = P.sb([128, 16, 8], F32, "cC")
        cS = P.sb([128, 16, 8], F32, "cS")
        rhoM = P.sb([128, L, 16, 8], F32, "rhoM")
        identR = P.sb([128, 128], F32R, "identR")
        nidentR = P.sb([128, 128], F32R, "nidentR")
        s5p = P.sb([128, L, 8, 16], F32, "s5p")
        scT = P.sb([128, 8, 18], F32R, "scT")

        free_t = list(range(NTMP))
        free_r = list(range(NTMPR))
        free_p = list(range(8))

        def talloc():
            i = free_t.pop(0)
            return i

        def tfree(*ids):
            for i in ids:
                free_t.append(i)

        def T(i):
            return tmps[i]

        def TK(i):
            return "tmp%d" % i

        def ralloc():
            return free_r.pop(0)

        def rfree(*ids):
            for i in ids:
                free_r.append(i)

        def psalloc():
            return free_p.pop(0)

        def psfree(*ids):
            for i in ids:
                free_p.append(i)

        def PK(i):
            return "ps%d" % i


        def tt(out, in0, in1, op, reads, writes, eng="dve"):
            P.op(eng, lambda e: e.tensor_tensor(out=out, in0=in0, in1=in1, op=op), reads, writes)

        def ts(out, in0, s1, s2, op0, op1, reads, writes, eng="dve"):
            if s2 is None:
                P.op(eng, lambda e: e.tensor_scalar(out=out, in0=in0, scalar1=s1, scalar2=None, op0=op0), reads, writes)
            else:
                P.op(eng, lambda e: e.tensor_scalar(out=out, in0=in0, scalar1=s1, scalar2=s2, op0=op0, op1=op1), reads, writes)

        def stt(out, in0, sc, in1, op0, op1, reads, writes):
            P.op("dve", lambda e: e.scalar_tensor_tensor(out=out, in0=in0, scalar=sc, in1=in1, op0=op0, op1=op1), reads, writes)

        def act(out, in_, func, reads, writes, scale=1.0, bias=None):
            if bias is None:
                P.op("act", lambda e: e.activation(out=out, in_=in_, func=func, scale=scale), reads, writes)
            else:
                P.op("act", lambda e: e.activation(out=out, in_=in_, func=func, scale=scale, bias=bias), reads, writes)

        def mm(out, lhsT, rhs, start, stop, reads, writes):
            P.op("pe", lambda e: e.matmul(out, lhsT, rhs, start=start, stop=stop), reads, writes)

        def cp(out, in_, reads, writes, eng="dve"):
            P.op(eng, lambda e: e.tensor_copy(out=out, in_=in_), reads, writes)

        def scan(out, d0, d1, init, reads, writes):
            P.op("dve", lambda e: e.tensor_tensor_scan(out=out, data0=d0, data1=d1, initial=init, op0=ALU.mult, op1=ALU.add),
                 reads, writes)

        def mset(ap, v, reads, writes, eng="dve"):
            P.op(eng, lambda e: e.memset(ap, v), reads, writes)

        MUL, ADD, SUB = ALU.mult, ALU.add, ALU.subtract

        wstate = {"n": 0, "nrot": NWBUF}

        def wload(src_ap, size, reads=(), eng="sp"):
            i = wstate["n"] % wstate["nrot"]
            wstate["n"] += 1
            key = "wbuf%d" % i
            wkeys = [key] + (["wbuf3a", "wbuf3b"] if i == 3 else [])
            P.dma(wbuf[i][:, 0:size], src_ap, reads=list(reads), writes=wkeys, eng=eng)
            return wbuf[i], key

        def wtile(l, name):
            o, s = SOFF[name]
            return wload(wst[l][:, o:o + s], s)

        P.dma(cols[:], cols_d, writes=["cols"])
        ti_c = talloc()
        P.dma(T(ti_c)[:, 0:144].rearrange("p (k s) -> p k s", s=18), cT.rearrange("(k p) s -> p k s", p=128), writes=[TK(ti_c)])
        mset(ident[:], 0.0, [], ["ident"], eng="pool")
        P.op("pool", lambda e: e.affine_select(out=ident[:], in_=ident[:], pattern=[[-1, 128]],
                                               compare_op=ALU.not_equal, fill=1.0, base=0, channel_multiplier=1),
             reads=["ident"], writes=["ident"])
        ts(ones[:], ident[:], 0.0, 1.0, MUL, ADD, ["ident"], ["ones"])
        ts(identR[:], ident[:], 1.0, 0.0, MUL, ADD, ["ident"], ["identR"])
        ts(nidentR[:], ident[:], -1.0, 0.0, MUL, ADD, ["ident"], ["identR"])
        mset(cst[:, 0:1], 1e-6, [], ["cst"])
        mset(cst[:, 1:2], math.pi / 2, ["cst"], ["cst"])
        mset(cst[:, 2:3], 1.0, ["cst"], ["cst"])
        mset(cst[:, 3:4], 0.0, ["cst"], ["cst"])
        mset(rgh[:], 0.0, [], ["rgh"], eng="pool")
        mset(rgc[:], 0.0, [], ["rgc"], eng="pool")
        mset(s5s[:], 0.0, [], ["s5s"], eng="pool")
        mset(ffc[:], 0.0, [], ["ffc"], eng="pool")
        for l in range(L):
            P.dma(rgh[:, l, :, 1:17], st_rgh[l], writes=["rgh"])
            P.dma(rgc[:, l, :, 1:17, :], st_rgc[l], writes=["rgc"])
            for ri in range(2):
                P.dma(s5s[:, l, ri, :, 1:17], st_s5[l, ri], writes=["s5s"])
            for c0 in range(0, 48, 8):
                P.dma(ffc[:, l, c0:c0 + 8, 1:17, :], st_ffn[l, :, c0:c0 + 8], writes=["ffc"])

        def setup_A1(l):
            a0 = talloc()
            a1 = talloc()
            K0, K1 = TK(a0), TK(a1)
            prm = T(a0)[:, 0:48].rearrange("p (a b) -> p a b", b=16)
            P.dma(prm, s5col_d[l], writes=[K0], eng="pool")
            w_ = T(a1)

            def S(i, w_=w_):
                return w_[:, i * 16:(i + 1) * 16]
            lr, li, ldt = prm[:, 0, :], prm[:, 1, :], prm[:, 2, :]
            rho, ar, ai, zr, zi = (s5p[:, l, i, :] for i in range(5))
            act(S(0), ldt, AF.Exp, [K0], [K1])
            tt(S(1), lr, S(0), MUL, [K0, K1], [K1])
            act(rho, S(1), AF.Exp, [K1], ["s5p"])
            tt(S(2), li, S(0), MUL, [K0, K1], [K1])
            act(S(3), S(2), AF.Sin, [K1], [K1], scale=1.0 / 16)
            act(S(4), S(2), AF.Sin, [K1, "cst"], [K1], scale=1.0 / 16, bias=cst[:, 1:2])
            for _ in range(4):
                tt(S(5), S(4), S(4), MUL, [K1], [K1])
                tt(S(6), S(3), S(3), MUL, [K1], [K1])
                tt(S(7), S(4), S(3), MUL, [K1], [K1])
                tt(S(4), S(5), S(6), SUB, [K1], [K1])
                ts(S(3), S(7), 2.0, None, MUL, None, [K1], [K1])
            cp(s5p[:, l, 5, :], S(4), [K1, "s5p"], ["s5p"])
            cp(s5p[:, l, 6, :], S(3), [K1, "s5p"], ["s5p"])
            tt(ar, rho, S(4), MUL, [K1, "s5p"], ["s5p"])
            tt(ai, rho, S(3), MUL, [K1, "s5p"], ["s5p"])
            tt(S(5), lr, lr, MUL, [K0], [K1])
            tt(S(6), li, li, MUL, [K0], [K1])
            tt(S(5), S(5), S(6), ADD, [K1], [K1])
            P.op("dve", lambda e, o=S(5): e.reciprocal(out=o, in_=o), [K1], [K1])
            ts(S(6), ar, -1.0, None, ADD, None, ["s5p"], [K1])
            tt(S(7), S(6), lr, MUL, [K1, K0], [K1])
            tt(S(8), ai, li, MUL, ["s5p", K0], [K1])
            tt(S(7), S(7), S(8), ADD, [K1], [K1])
            tt(zr, S(7), S(5), MUL, [K1], ["s5p"])
            tt(S(7), ai, lr, MUL, ["s5p", K0], [K1])
            tt(S(8), S(6), li, MUL, [K1, K0], [K1])
            tt(S(7), S(7), S(8), SUB, [K1], [K1])
            tt(zi, S(7), S(5), MUL, [K1], ["s5p"])
            tfree(a0, a1)

        def setup_A2(l):
            for q in range(4):
                zr_ps = psalloc()
                zi_ps = psalloc()
                for pp in range(4):
                    p = 4 * q + pp
                    for (zi_, dst) in ((3, zr_ps), (4, zi_ps)):
                        r = ralloc()
                        ts(tmpr[r][:, 0:128], ident[:], s5p[:, l, zi_, p:p + 1], None, MUL, None, ["ident", "s5p"], ["tmpr%d" % r])
                        mm(pss[dst][:, pp * 128:(pp + 1) * 128], ones[:], tmpr[r][:, 0:128], True, True,
                           ["ones", "tmpr%d" % r], [PK(dst)])
                        rfree(r)
                br = talloc()
                bi = talloc()
                P.dma(T(br)[:, 0:512], s5bp_d[l, :, 0, q * 512:(q + 1) * 512], writes=[TK(br)])
                P.dma(T(bi)[:, 0:512], s5bp_d[l, :, 1, q * 512:(q + 1) * 512], writes=[TK(bi)])
                t1 = talloc()
                t2 = talloc()
                r = ralloc()
                r2 = ralloc()
                RK = "tmpr%d" % r
                RK2 = "tmpr%d" % r2
                A, B_, U1, U2 = T(br)[:, 0:512], T(bi)[:, 0:512], T(t1)[:, 0:512], T(t2)[:, 0:512]
                tt(U1, A, pss[zr_ps][:], MUL, [TK(br), PK(zr_ps)], [TK(t1)])
                tt(U2, B_, pss[zi_ps][:], MUL, [TK(bi), PK(zi_ps)], [TK(t2)])
                tt(tmpr[r][:, 0:512], U1, U2, SUB, [TK(t1), TK(t2)], [RK])
                tt(U1, A, pss[zi_ps][:], MUL, [TK(br), PK(zi_ps)], [TK(t1)])
                tt(U2, B_, pss[zr_ps][:], MUL, [TK(bi), PK(zr_ps)], [TK(t2)])
                tt(tmpr[r2][:, 0:512], U1, U2, ADD, [TK(t1), TK(t2)], [RK2])
                P.dma(s5b_d[l, :, q, 0:512], tmpr[r][:, 0:512], reads=[RK], writes=["s5b_d"])
                P.dma(s5b_d[l, :, q, 512:1024], tmpr[r2][:, 0:512], reads=[RK2, "s5b_d"], writes=["s5b_d"])
                rfree(r, r2)
                tfree(br, bi, t1, t2)
                psfree(zr_ps, zi_ps)

        act(scT[:].rearrange("p k s -> p (k s)"), T(ti_c)[:, 0:144], AF.Silu, [TK(ti_c)], ["scT"])
        tfree(ti_c)
        ada_loaded = {}

        def ada_issue(l, t):
            ada_loaded[(l, t)] = wload(wada[:, (l * ADA_TILES + t) * 2048:(l * ADA_TILES + t + 1) * 2048], 2048,
                                       eng=("act" if l == 0 else "sp"))

        def ada_tile(l, t):
            if (l, t) not in ada_loaded:
                ada_issue(l, t)
            wb, wk = ada_loaded.pop((l, t))
            wv = wb[:, 0:2048].rearrange("p (k n) -> p k n", n=256)
            for half in range(2):
                m = 2 * t + half
                pi = psalloc()
                for k in range(8):
                    mm(pss[pi][:, 0:18], wv[:, k, half * 128:(half + 1) * 128], scT[:, k, :], k == 0, k == 7,
                       [wk, "scT"], [PK(pi)])
                bcol = l * NCL + CO["bada"] + m
                act(mod[:, l, m, :], pss[pi][:, 0:18], AF.Identity, [PK(pi), "cols"], ["mod%d" % l], bias=cols[:, bcol:bcol + 1])
                psfree(pi)

        def setup_B(l):
            baseC = tbuf[0][:, 0:1024].rearrange("p (a b) -> p a b", b=64)
            baseS = tbuf[1][:, 0:1024].rearrange("p (a b) -> p a b", b=64)
            BK = ["tbuf0", "tbuf1"]
            cp(baseC[:, :, 0], s5p[:, l, 5, :], ["s5p"] + BK, ["tbuf0"])
            cp(baseS[:, :, 0], s5p[:, l, 6, :], ["s5p"] + BK, ["tbuf1"])
            b1 = talloc()
            b2 = talloc()
            k = 1
            while k < 64:
                ec = baseC[:, :, k - 1:k].to_broadcast([128, 16, k])
                es = baseS[:, :, k - 1:k].to_broadcast([128, 16, k])
                u1 = T(b1)[:, 0:16 * k].rearrange("p (a b) -> p a b", b=k)
                u2 = T(b2)[:, 0:16 * k].rearrange("p (a b) -> p a b", b=k)
                c0 = baseC[:, :, 0:k]
                s0 = baseS[:, :, 0:k]
                tt(u1, c0, ec, MUL, BK, [TK(b1)])
                tt(u2, s0, es, MUL, BK, [TK(b2)])
                tt(baseC[:, :, k:2 * k], u1, u2, SUB, [TK(b1), TK(b2)] + BK, ["tbuf0"])
                tt(u1, s0, ec, MUL, BK, [TK(b1)])
                tt(u2, c0, es, MUL, BK, [TK(b2)])
                tt(baseS[:, :, k:2 * k], u1, u2, ADD, [TK(b1), TK(b2)] + BK, ["tbuf1"])
                k *= 2
            CK = ["cCS"]
            mset(cC[:, :, 0], 1.0, CK, CK)
            mset(cS[:, :, 0], 0.0, CK, CK)
            cp(cC[:, :, 1], baseC[:, :, 63], BK + CK, CK)
            cp(cS[:, :, 1], baseS[:, :, 63], BK + CK, CK)
            v1 = T(b1)[:, 0:16]
            v2 = T(b2)[:, 0:16]
            for a in range(2, 8):
                tt(v1, cC[:, :, a - 1], cC[:, :, 1], MUL, CK, [TK(b1)])
                tt(v2, cS[:, :, a - 1], cS[:, :, 1], MUL, CK, [TK(b2)])
                tt(cC[:, :, a], v1, v2, SUB, [TK(b1), TK(b2)] + CK, CK)
                tt(v1, cS[:, :, a - 1], cC[:, :, 1], MUL, CK, [TK(b1)])
                tt(v2, cC[:, :, a - 1], cS[:, :, 1], MUL, CK, [TK(b2)])
                tt(cS[:, :, a], v1, v2, ADD, [TK(b1), TK(b2)] + CK, CK)
            tfree(b1, b2)
            mset(rhoM[:, l, :, 0:1], 0.0, ["rhoM"], ["rhoM"])
            cp(rhoM[:, l, :, 1:8], s5p[:, l, 0, :].unsqueeze(2).to_broadcast([128, 16, 7]), ["s5p", "rhoM"], ["rhoM"])
            for p in range(16):
                ccb = cC[:, p, :].unsqueeze(2).to_broadcast([128, 8, 64])
                scb = cS[:, p, :].unsqueeze(2).to_broadcast([128, 8, 64])
                cbb = baseC[:, p, :].unsqueeze(1).to_broadcast([128, 8, 64])
                sbb = baseS[:, p, :].unsqueeze(1).to_broadcast([128, 8, 64])
                m1, m2, m3, m4 = (talloc() for _ in range(4))

                def W3(i):
                    return T(i)[:, 0:512].rearrange("p (a b) -> p a b", b=64)
                tt(W3(m1), ccb, cbb, MUL, BK + CK, [TK(m1)])
                tt(W3(m2), scb, sbb, MUL, BK + CK, [TK(m2)])
                tt(W3(m3), scb, cbb, MUL, BK + CK, [TK(m3)])
                tt(W3(m4), ccb, sbb, MUL, BK + CK, [TK(m4)], eng="pool")
                tt(T(m1)[:, 0:512], T(m1)[:, 0:512], T(m2)[:, 0:512], SUB, [TK(m1), TK(m2)], [TK(m1)])
                tt(T(m3)[:, 0:512], T(m3)[:, 0:512], T(m4)[:, 0:512], ADD, [TK(m3), TK(m4)], [TK(m3)], eng="pool")
                P.dma(tab_d[l, p, :, 0, :], T(m1)[:, 0:512], reads=[TK(m1)], writes=["tab_d"])
                P.dma(tab_d[l, p, :, 1, :], T(m3)[:, 0:512], reads=[TK(m3), "tab_d"], writes=["tab_d"])
                tfree(m2, m4, m1, m3)

        def dmod_l(l):
            for c in range(8):
                g1c = l * NCL + CO["g1"] + c
                g2c = l * NCL + CO["g2"] + c
                ts(dmod[:, l, 0, c, :], mod[:, l, 8 + c, :], 1.0, cols[:, g1c:g1c + 1], ADD, MUL, ["mod%d" % l, "cols"], ["dmod%d" % l])
                ts(dmod[:, l, 1, c, :], mod[:, l, 16 + c, :], 0.5, None, MUL, None, ["mod%d" % l], ["dmod%d" % l])
                ts(dmod[:, l, 2, c, :], mod[:, l, 32 + c, :], 1.0, cols[:, g2c:g2c + 1], ADD, MUL, ["mod%d" % l, "cols"], ["dmod%d" % l])


        for l in range(L):
            setup_A1(l)
        for l in range(L):
            b = l * NCL
            for (dst, src) in ((0, CO["ba"]), (8, CO["bx"]), (32, CO["bin"] + 20), (40, CO["bin"] + 28)):
                ts(dcol[:, l, dst:dst + 8], cols[:, b + src:b + src + 8], 0.5, None, MUL, None, ["cols"], ["dcol"])
            act(dcol[:, l, 48:56], cols[:, b + CO["lam"]:b + CO["lam"] + 8], AF.Exp, ["cols"], ["dcol"], scale=-1.0)
            act(dcol[:, l, 56:64], dcol[:, l, 48:56], AF.Ln, ["dcol", "cst"], ["dcol"], bias=cst[:, 2:3])
            ts(dcol[:, l, 16:24], dcol[:, l, 56:64], -8.0, None, MUL, None, ["dcol"], ["dcol"])
            ts(dcol[:, l, 24:32], dcol[:, l, 56:64], -4.0, None, MUL, None, ["dcol"], ["dcol"])

        for l in range(L):
            setup_A2(l)
        for t in range(3):
            ada_issue(0, t)
        for t in range(ADA_TILES):
            if t + 3 < ADA_TILES:
                ada_issue(0, t + 3)
            ada_tile(0, t)
        for l in range(L):
            setup_B(l)
        dmod_l(0)
        ada_pending = [(1, t) for t in range(ADA_TILES)]

        def ada_step():
            if ada_pending:
                ada_tile(*ada_pending.pop(0))
                if not ada_pending:
                    dmod_l(1)

        tstate = {"n": 0}

        def tabload(l, p, small):
            i = tstate["n"] % 2
            tstate["n"] += 1
            key = "tbuf%d" % i
            if small:
                P.dma(tbuf[i][:, 0:16].rearrange("p (a n) -> p a n", a=2), tab_d[l, p, :, :, 0:8], reads=["tab_d"], writes=[key])
            else:
                P.dma(tbuf[i][:, 0:1024].rearrange("p (a n) -> p a n", a=2), tab_d[l, p], reads=["tab_d"], writes=[key])
            return tbuf[i], key

        dstate = {"n": 0}

        def mkdiag(wcols):
            i = dstate["n"] % 2
            dstate["n"] += 1
            key = "dgb%d" % i
            for t, wc in enumerate(wcols):
                ts(dgb[i][:, t, :], ident[:], wc, 0.0, MUL, ADD, ["ident", "cols"], [key], eng="pool")
            return dgb[i], key

        def run_tile(col0, N, nseq, slen, seq0):
            prompt = nseq == 1

            def V3(ap2):
                return ap2.rearrange("p (s t) -> p s t", t=slen)

            def DV(ap2):
                return ap2 if prompt else V3(ap2)

            def bc_seq(ap_pn):
                return ap_pn.unsqueeze(2).to_broadcast([128, nseq, slen])

            for c0 in (0, 4):
                P.dma(x[:, c0:c0 + 4, 0:N], xT[c0 * 128:(c0 + 4) * 128, col0:col0 + N].rearrange("(c p) n -> p c n", p=128),
                      writes=["x%d" % c for c in range(c0, c0 + 4)], eng=("pool" if col0 == 0 else "sp"))

            def rmsnorm_rstd():
                pi = psalloc()
                for c in range(8):
                    r = ralloc()
                    act(tmpr[r][:, 0:N], x[:, c, 0:N], AF.Square, ["x%d" % c], ["tmpr%d" % r])
                    mm(pss[pi][:, 0:N], ones[:], tmpr[r][:, 0:N], c == 0, c == 7, ["ones", "tmpr%d" % r], [PK(pi)])
                    rfree(r)
                t1 = talloc()
                rs = talloc()
                act(T(t1)[:, 0:N], pss[pi][:, 0:N], AF.Ln, [PK(pi), "cst"], [TK(t1)], scale=1.0 / D, bias=cst[:, 0:1])
                act(T(rs)[:, 0:N], T(t1)[:, 0:N], AF.Exp, [TK(t1)], [TK(rs)], scale=-0.5)
                tfree(t1)
                psfree(pi)
                return rs

            def modulate(l, ai_, shift_chunk0):
                rs = rmsnorm_rstd()
                for c in range(8):
                    t1 = talloc()
                    eng = "pool" if c in (2, 5, 7) else "dve"
                    tt(T(t1)[:, 0:N], x[:, c, 0:N], T(rs)[:, 0:N], MUL, ["x%d" % c, TK(rs)], [TK(t1)], eng=eng)
                    if prompt:
                        act(h[:, c, 0:N], T(t1)[:, 0:N], AF.Identity, [TK(t1), "dmod%d" % l, "mod%d" % l], ["h%d" % c],
                            scale=dmod[:, l, ai_, c, 0:1], bias=mod[:, l, shift_chunk0 + c, 0:1])
                    else:
                        tt(V3(T(t1)[:, 0:N]), V3(T(t1)[:, 0:N]), bc_seq(dmod[:, l, ai_, c, seq0:seq0 + nseq]), MUL,
                           [TK(t1), "dmod%d" % l], [TK(t1)])
                        tt(V3(h[:, c, 0:N]), V3(T(t1)[:, 0:N]), bc_seq(mod[:, l, shift_chunk0 + c, seq0:seq0 + nseq]), ADD,
                           [TK(t1), "mod%d" % l], ["h%d" % c])
                    tfree(t1)
                tfree(rs)

            def resid_add(l, o, pi, gate18):
                if prompt:
                    stt(x[:, o, 0:N], pss[pi][:, 0:N], gate18[:, 0:1], x[:, o, 0:N], MUL, ADD,
                        [PK(pi), "x%d" % o, "mod%d" % l, "dmod%d" % l], ["x%d" % o])
                else:
                    t1 = talloc()
                    tt(V3(T(t1)[:, 0:N]), V3(pss[pi][:, 0:N]), bc_seq(gate18[:, seq0:seq0 + nseq]), MUL,
                       [PK(pi), "mod%d" % l, "dmod%d" % l], [TK(t1)])
                    tt(x[:, o, 0:N], x[:, o, 0:N], T(t1)[:, 0:N], ADD, [TK(t1), "x%d" % o], ["x%d" % o])
                    tfree(t1)

            for l in range(L):
                cb = l * NCL

                def col(name, i, cb=cb):
                    return cols[:, cb + CO[name] + i:cb + CO[name] + i + 1]

                modulate(l, 0, 0)

                rgst = {}
                rgst2 = {}

                def rg_A(c):
                    wb, wk = wtile(l, "rg%d" % c)
                    wv = wb[:, 0:2048].rearrange("p (k n) -> p k n", n=256)
                    p_rx = psalloc()
                    p_ry = psalloc()
                    for k in range(8):
                        mm(pss[p_rx][:, 0:N], wv[:, k, 0:128], h[:, k, 0:N], k == 0, k == 7, [wk, "h%d" % k], [PK(p_rx)])
                    for k in range(8):
                        mm(pss[p_ry][:, 0:N], wv[:, k, 128:256], h[:, k, 0:N], k == 0, k == 7, [wk, "h%d" % k], [PK(p_ry)])
                    ex = ralloc()
                    EK = "tmpr%d" % ex
                    exv = tmpr[ex][:, 0:nseq * (slen + 4)].rearrange("p (s t) -> p s t", t=slen + 4)
                    act(exv[:, :, 0:3], rgc[:, l, c, seq0:seq0 + nseq, :], AF.Identity, ["rgc", EK], [EK])
                    act(exv[:, :, 3:3 + slen], V3(pss[p_rx][:, 0:N]), AF.Identity, [PK(p_rx), "cols", EK], [EK], bias=col("bin", c))
                    act(rgc[:, l, c, seq0:seq0 + nseq, :], V3(pss[p_rx][:, 0:N])[:, :, slen - 3:slen], AF.Identity,
                        [PK(p_rx), "cols", "rgc"], ["rgc"], bias=col("bin", c))
                    gy = talloc()
                    act(T(gy)[:, 0:N], pss[p_ry][:, 0:N], AF.Gelu_apprx_tanh, [PK(p_ry), "cols"], [TK(gy)], bias=col("bin", 8 + c))
                    psfree(p_rx, p_ry)
                    dg, dk = mkdiag([col("rcw", j * 8 + c) for j in range(4)])
                    rgst[c] = (ex, gy, dg, dk)

                def rg_B(c):
                    ex, gy, dg, dk = rgst.pop(c)
                    EK = "tmpr%d" % ex
                    exv = tmpr[ex][:, 0:nseq * (slen + 4)].rearrange("p (s t) -> p s t", t=slen + 4)
                    gw, gk = wtile(l, "rgd%d" % c)
                    p_c = psalloc()
                    for j in range(4):
                        mm(V3(pss[p_c][:, 0:N]), dg[:, j, :], exv[:, :, j:j + slen], j == 0, j == 3, [dk, EK], [PK(p_c)])
                    xc = ralloc()
                    XK = "tmpr%d" % xc
                    act(tmpr[xc][:, 0:N], pss[p_c][:, 0:N], AF.Identity, [PK(p_c), "cols"], [XK], bias=col("rcb", c))
                    rfree(ex)
                    psfree(p_c)
                    xcf = tmpr[xc][:, 0:N].bitcast(F32)
                    p_a = psalloc()
                    p_x = psalloc()
                    mm(pss[p_a][:, 0:N], gw[:, 0:128], tmpr[xc][:, 0:N], True, True, [gk, XK], [PK(p_a)])
                    mm(pss[p_x][:, 0:N], gw[:, 128:256], tmpr[xc][:, 0:N], True, True, [gk, XK], [PK(p_x)])
                    tr_ = talloc()
                    ti_ = talloc()
                    TR, TI, GY = T(tr_)[:, 0:N], T(ti_)[:, 0:N], T(gy)[:, 0:N]
                    act(TR, pss[p_a][:, 0:N], AF.Tanh, [PK(p_a), "dcol"], [TK(tr_)], scale=0.5, bias=dcol[:, l, c:c + 1])
                    act(TI, pss[p_x][:, 0:N], AF.Tanh, [PK(p_x), "dcol"], [TK(ti_)], scale=0.5, bias=dcol[:, l, 8 + c:9 + c])
                    psfree(p_a, p_x)
                    a_ = talloc()
                    a2 = talloc()
                    A_, A2 = T(a_)[:, 0:N], T(a2)[:, 0:N]
                    act(A_, TR, AF.Exp, [TK(tr_), "dcol"], [TK(a_)], scale=dcol[:, l, 24 + c:25 + c], bias=dcol[:, l, 24 + c:25 + c])
                    act(A2, TR, AF.Exp, [TK(tr_), "dcol"], [TK(a2)], scale=dcol[:, l, 16 + c:17 + c], bias=dcol[:, l, 16 + c:17 + c])
                    act(A2, A2, AF.Ln, [TK(a2), "cst"], [TK(a2)], scale=-1.0, bias=cst[:, 2:3])
                    act(A2, A2, AF.Exp, [TK(a2)], [TK(a2)], scale=0.5)
                    rgst2[c] = (xc, tr_, ti_, gy, a_, a2)

                def rg_B2(c):
                    xc, tr_, ti_, gy, a_, a2 = rgst2.pop(c)
                    XK = "tmpr%d" % xc
                    xcf = tmpr[xc][:, 0:N].bitcast(F32)
                    TR, TI, GY = T(tr_)[:, 0:N], T(ti_)[:, 0:N], T(gy)[:, 0:N]
                    A_, A2 = T(a_)[:, 0:N], T(a2)[:, 0:N]
                    stt(TI, TI, 1.0, xcf, ADD, MUL, [TK(ti_), XK], [TK(ti_)])
                    stt(TI, TI, 0.5, A2, MUL, MUL, [TK(ti_), TK(a2)], [TK(ti_)])
                    rfree(xc)
                    if prompt:
                        scan(TR, A_, TI, rgh[:, l, c, seq0:seq0 + 1], [TK(a_), TK(ti_), "rgh", TK(tr_)], [TK(tr_)])
                    else:
                        a3, b3 = V3(A_), V3(TI)
                        t0 = talloc()
                        t0v = T(t0)[:, 0:nseq]
                        tt(t0v, a3[:, :, 0], rgh[:, l, c, seq0:seq0 + nseq], MUL, [TK(a_), "rgh"], [TK(t0)], eng="pool")
                        tt(b3[:, :, 0], b3[:, :, 0], t0v, ADD, [TK(ti_), TK(t0)], [TK(ti_)], eng="pool")
                        mset(a3[:, :, 0], 0.0, [TK(a_), TK(t0)], [TK(a_)], eng="pool")
                        tfree(t0)
                        scan(TR, A_, TI, 0.0, [TK(a_), TK(ti_), TK(tr_)], [TK(tr_)])
                    cp(rgh[:, l, c, seq0:seq0 + nseq], V3(TR)[:, :, slen - 1], [TK(tr_), "rgh"], ["rgh"], eng="pool")
                    tt(R1[:, c, 0:N], TR, GY, MUL, [TK(tr_), TK(gy)], ["R%d" % c])
                    tfree(tr_, ti_, gy, a_, a2)

                s5st = {}
                qst = {}

                def s5_A(p):
                    q, pp = divmod(p, 4)
                    if pp == 0:
                        wb, wk = wtile(l, "sx%d" % q)
                        wv = wb[:, 0:1024].rearrange("p (k n) -> p k n", n=128)
                        p_sx = psalloc()
                        for k in range(8):
                            mm(pss[p_sx][:, 0:N], wv[:, k, :], h[:, k, 0:N], k == 0, k == 7, [wk, "h%d" % k], [PK(p_sx)])
                        sx = ralloc()
                        SXK = "tmpr%d" % sx
                        act(tmpr[sx][:, 0:N], pss[p_sx][:, 0:N], AF.Identity, [PK(p_sx), "cols"], [SXK], bias=col("bin", 16 + q))
                        psfree(p_sx)
                        bwb, bwk = wbuf[3], "wbuf3a"
                        P.dma(bwb[:, 0:1024], s5b_d[l, :, q, :], reads=["s5b_d"], writes=[bwk, "wbuf3"])
                        qst[q] = dict(sx=sx, bwb=bwb, bwk=bwk, p_y=psalloc())
                    Q = qst[q]
                    sx = Q["sx"]
                    SXK = "tmpr%d" % sx
                    bwv = Q["bwb"][:, 0:1024].rearrange("p (a b n) -> p a b n", a=2, n=128)
                    p_vr = psalloc()
                    p_vi = psalloc()
                    mm(pss[p_vr][:, 0:N], bwv[:, 0, pp, :], tmpr[sx][:, 0:N], True, True, [Q["bwk"], SXK], [PK(p_vr)])
                    mm(pss[p_vi][:, 0:N], bwv[:, 1, pp, :], tmpr[sx][:, 0:N], True, True, [Q["bwk"], SXK], [PK(p_vi)])
                    tb, tk = tabload(l, p, not prompt)
                    s5st[p] = (p_vr, p_vi, tb, tk)

                def s5_B(p):
                    q, pp = divmod(p, 4)
                    Q = qst[q]
                    p_vr, p_vi, tb, tk = s5st.pop(p)
                    if pp == 0:
                        cwb, cwk = wbuf[3][:, 1024:2048], "wbuf3b"
                        o_, s_ = SOFF["s5c%d" % q]
                        P.dma(cwb, wst[l][:, o_:o_ + s_], writes=[cwk, "wbuf3"])
                        act(cwb[:, 512:1024], cwb[:, 512:1024].bitcast(F32), AF.Identity, [cwk], [cwk], scale=-1.0)
                        Q["cwb"], Q["cwk"] = cwb, cwk
                    if prompt:
                        Cv, Sv = tb[:, 0:N], tb[:, 512:512 + N]
                        cL, sL = tb[:, N - 1:N], tb[:, 512 + N - 1:512 + N]
                    else:
                        Cv = tb[:, 0:8].unsqueeze(1).to_broadcast([128, nseq, 8])
                        Sv = tb[:, 8:16].unsqueeze(1).to_broadcast([128, nseq, 8])
                        cL, sL = tb[:, 7:8], tb[:, 15:16]
                    t1, t2, wr, wi = talloc(), talloc(), talloc(), talloc()
                    K1_, K2_, KR, KI = TK(t1), TK(t2), TK(wr), TK(wi)
                    T1, T2, WR, WI = T(t1)[:, 0:N], T(t2)[:, 0:N], T(wr)[:, 0:N], T(wi)[:, 0:N]
                    vr, vi = DV(pss[p_vr][:, 0:N]), DV(pss[p_vi][:, 0:N])
                    tt(DV(T1), vr, Cv, MUL, [PK(p_vr), tk], [K1_])
                    tt(DV(T2), vi, Sv, MUL, [PK(p_vi), tk], [K2_])
                    tt(WR, T1, T2, ADD, [K1_, K2_], [KR])
                    tt(DV(T1), vi, Cv, MUL, [PK(p_vi), tk], [K1_])
                    tt(DV(T2), vr, Sv, MUL, [PK(p_vr), tk], [K2_])
                    tt(WI, T1, T2, SUB, [K1_, K2_], [KI])
                    psfree(p_vr, p_vi)
                    if prompt:
                        rho_b = s5p[:, l, 0, p:p + 1].to_broadcast([128, N])
                        scan(T1, rho_b, WR, s5s[:, l, 0, p, seq0:seq0 + 1], [KR, "s5s", "s5p", K1_], [K1_])
                        scan(T2, rho_b, WI, s5s[:, l, 1, p, seq0:seq0 + 1], [KI, "s5s", "s5p", K2_], [K2_])
                        zr_e, zi_e = T1[:, N - 1:N], T2[:, N - 1:N]
                        nst = 1
                    else:
                        rm = talloc()
                        RM = T(rm)[:, 0:N]
                        cp(V3(RM), rhoM[:, l, p, :].unsqueeze(1).to_broadcast([128, nseq, 8]), ["rhoM"], [TK(rm)], eng="pool")
                        stt(V3(WR)[:, :, 0], s5s[:, l, 0, p, seq0:seq0 + nseq], s5p[:, l, 0, p:p + 1], V3(WR)[:, :, 0], MUL, ADD,
                            [KR, "s5s", "s5p"], [KR])
                        stt(V3(WI)[:, :, 0], s5s[:, l, 1, p, seq0:seq0 + nseq], s5p[:, l, 0, p:p + 1], V3(WI)[:, :, 0], MUL, ADD,
                            [KI, "s5s", "s5p"], [KI])
                        scan(T1, RM, WR, 0.0, [KR, TK(rm), K1_], [K1_])
                        scan(T2, RM, WI, 0.0, [KI, TK(rm), K2_], [K2_])
                        tfree(rm)
                        zr_e, zi_e = V3(T1)[:, :, 7], V3(T2)[:, :, 7]
                        nst = nseq
                    tfree(wr, wi)
                    us = [ralloc() for _ in range(4)]
                    UK = ["tmpr%d" % u for u in us]
                    U = [tmpr[u][:, 0:N] for u in us]
                    tt(DV(U[0]), DV(T1), Cv, MUL, [K1_, tk], [UK[0]])
                    stt(DV(U[1]), DV(T2), -1.0, Sv, MUL, MUL, [K2_, tk], [UK[1]])
                    tt(DV(U[2]), DV(T2), Cv, MUL, [K2_, tk], [UK[2]], eng="pool")
                    tt(DV(U[3]), DV(T1), Sv, MUL, [K1_, tk], [UK[3]], eng="pool")
                    e1 = talloc()
                    E = T(e1)
                    EK1 = TK(e1)
                    ts(E[:, 0:nst], zr_e, cL, 0.0, MUL, ADD, [K1_, tk], [EK1], eng="pool")
                    ts(E[:, 32:32 + nst], zi_e, sL, 0.0, MUL, ADD, [K2_, tk, EK1], [EK1], eng="pool")
                    tt(s5s[:, l, 0, p, seq0:seq0 + nst], E[:, 0:nst], E[:, 32:32 + nst], SUB, [EK1, "s5s"], ["s5s"], eng="pool")
                    ts(E[:, 64:64 + nst], zi_e, cL, 0.0, MUL, ADD, [K2_, tk, EK1], [EK1], eng="pool")
                    ts(E[:, 96:96 + nst], zr_e, sL, 0.0, MUL, ADD, [K1_, tk, EK1], [EK1], eng="pool")
                    tt(s5s[:, l, 1, p, seq0:seq0 + nst], E[:, 64:64 + nst], E[:, 96:96 + nst], ADD, [EK1, "s5s"], ["s5s"], eng="pool")
                    tfree(e1)
                    tfree(t1, t2)
                    cwv = Q["cwb"].rearrange("p (a b n) -> p a b n", a=2, n=128)
                    p_y = Q["p_y"]
                    mm(pss[p_y][:, 0:N], cwv[:, 0, pp, :], U[0], pp == 0, False, [Q["cwk"], UK[0]], [PK(p_y)])
                    mm(pss[p_y][:, 0:N], cwv[:, 0, pp, :], U[1], False, False, [Q["cwk"], UK[1]], [PK(p_y)])
                    mm(pss[p_y][:, 0:N], cwv[:, 1, pp, :], U[2], False, False, [Q["cwk"], UK[2]], [PK(p_y)])
                    mm(pss[p_y][:, 0:N], cwv[:, 1, pp, :], U[3], False, pp == 3, [Q["cwk"], UK[3]], [PK(p_y)])
                    rfree(*us)
                    if pp == 3:
                        sx = Q["sx"]
                        ty = talloc()
                        stt(T(ty)[:, 0:N], tmpr[sx][:, 0:N].bitcast(F32), col("s5d", q), pss[p_y][:, 0:N], MUL, ADD,
                            ["tmpr%d" % sx, "cols", PK(p_y)], [TK(ty)])
                        act(R1[:, 8 + q, 0:N], T(ty)[:, 0:N], AF.Gelu_apprx_tanh, [TK(ty)], ["R%d" % (8 + q)])
                        tfree(ty)
                        rfree(sx)
                        psfree(p_y)
                        del qst[q]

                wstate["nrot"] = 3
                rg_A(0)
                s5_A(0)
                for c in range(8):
                    rg_B(c)
                    if c + 1 < 8:
                        rg_A(c + 1)
                    for p in (2 * c, 2 * c + 1):
                        if p + 1 < 16:
                            s5_A(p + 1)
                        s5_B(p)
                    rg_B2(c)
                    ada_step()
                wstate["nrot"] = NWBUF

                for m in range(8):
                    wa_, wak = wtile(l, "mga%d" % m)
                    wav = wa_[:, 0:2048].rearrange("p (k n) -> p k n", n=256)
                    p_ga = psalloc()
                    p_gb = psalloc()
                    for (pi, off) in ((p_ga, 0), (p_gb, 128)):
                        for k in range(8):
                            mm(pss[pi][:, 0:N], wav[:, k, off:off + 128], h[:, k, 0:N], k == 0, k == 7, [wak, "h%d" % k], [PK(pi)])
                    wb_, wbk = wtile(l, "mgb%d" % m)
                    wrg = wb_[:, 0:1024].rearrange("p (k n) -> p k n", n=128)
                    wgl = wb_[:, 1024:2048].rearrange("p (k n) -> p k n", n=256)
                    p_ba = psalloc()
                    p_la = psalloc()
                    p_lb = psalloc()
                    for k in range(8):
                        mm(pss[p_ba][:, 0:N], wrg[:, k, :], R1[:, k, 0:N], k == 0, k == 7, [wbk, "R%d" % k], [PK(p_ba)])
                    for (pi, off) in ((p_la, 0), (p_lb, 128)):
                        for k in range(4):
                            mm(pss[pi][:, 0:N], wgl[:, k, off:off + 128], R1[:, 8 + k, 0:N], k == 0, k == 3,
                               [wbk, "R%d" % (8 + k)], [PK(pi)])
                    tga = talloc()
                    tgb = talloc()
                    tgl = talloc()
                    GA, GB, GL = T(tga)[:, 0:N], T(tgb)[:, 0:N], T(tgl)[:, 0:N]
                    act(GA, pss[p_ga][:, 0:N], AF.Tanh, [PK(p_ga), "dcol"], [TK(tga)], scale=0.5, bias=dcol[:, l, 32 + m:33 + m])
                    act(GB, pss[p_gb][:, 0:N], AF.Tanh, [PK(p_gb), "dcol"], [TK(tgb)], scale=0.5, bias=dcol[:, l, 40 + m:41 + m])
                    act(GL, pss[p_lb][:, 0:N], AF.Tanh, [PK(p_lb)], [TK(tgl)], scale=0.5)
                    psfree(p_ga, p_gb, p_lb)
                    stt(GA, GA, 1.0, pss[p_ba][:, 0:N], ADD, MUL, [TK(tga), PK(p_ba)], [TK(tga)])
                    stt(GL, GL, 1.0, pss[p_la][:, 0:N], ADD, MUL, [TK(tgl), PK(p_la)], [TK(tgl)])
                    psfree(p_ba, p_la)
                    stt(GB, GB, 1.0, GL, ADD, MUL, [TK(tgb), TK(tgl)], [TK(tgb)])
                    stt(R1[:, 12 + m, 0:N], GB, 0.5, GA, MUL, ADD, [TK(tgb), TK(tga)], ["R%d" % (12 + m)])
                    tfree(tga, tgb, tgl)
                    ada_step()

                for o2 in range(4):
                    wb, wk = wtile(l, "wo%d" % o2)
                    wv = wb[:, 0:2048].rearrange("p (k n) -> p k n", n=256)
                    for hh in range(2):
                        o = 2 * o2 + hh
                        pi = psalloc()
                        for k in range(8):
                            mm(pss[pi][:, 0:N], wv[:, k, hh * 128:(hh + 1) * 128], R1[:, 12 + k, 0:N], k == 0, k == 7,
                               [wk, "R%d" % (12 + k)], [PK(pi)])
                        resid_add(l, o, pi, dmod[:, l, 1, o, :])
                        psfree(pi)

                modulate(l, 2, 24)
                fst = {}

                def ffn_A(j):
                    wb, wk = wtile(l, "up%d" % j)
                    wv = wb[:, 0:2048].rearrange("p (k n) -> p k n", n=256)
                    st_ = []
                    for half in range(2):
                        cidx = j + 24 * half
                        pi = psalloc()
                        for k in range(8):
                            mm(pss[pi][:, 0:N], wv[:, k, half * 128:(half + 1) * 128], h[:, k, 0:N], k == 0, k == 7,
                               [wk, "h%d" % k], [PK(pi)])
                        ex = talloc()
                        EK = TK(ex)
                        exv = T(ex)[:, 0:nseq * (slen + 2)].rearrange("p (s t) -> p s t", t=slen + 2)
                        cp(exv[:, :, 0:2], ffc[:, l, cidx, seq0:seq0 + nseq, :], ["ffc", EK], [EK], eng="pool")
                        act(exv[:, :, 2:2 + slen], V3(pss[pi][:, 0:N]), AF.Identity, [PK(pi), EK], [EK])
                        act(ffc[:, l, cidx, seq0:seq0 + nseq, :], V3(pss[pi][:, 0:N])[:, :, slen - 2:slen], AF.Identity,
                            [PK(pi), "ffc"], ["ffc"])
                        acc = talloc()
                        AK = TK(acc)
                        a3 = V3(T(acc)[:, 0:N])
                        act(a3, exv[:, :, 0:slen], AF.Identity, [EK, "cols"], [AK], scale=col("fcw", cidx), bias=col("fcb", cidx))
                        st_.append((pi, ex, acc))
                    fst[j] = st_

                def ffn_B(j):
                    st_ = fst.pop(j)
                    accs = []
                    for half in range(2):
                        cidx = j + 24 * half
                        pi, ex, acc = st_[half]
                        EK, AK = TK(ex), TK(acc)
                        exv = T(ex)[:, 0:nseq * (slen + 2)].rearrange("p (s t) -> p s t", t=slen + 2)
                        a3 = V3(T(acc)[:, 0:N])
                        stt(a3, exv[:, :, 1:1 + slen], col("fcw", 48 + cidx), a3, MUL, ADD, [EK, "cols", AK], [AK])
                        stt(T(acc)[:, 0:N], pss[pi][:, 0:N], col("fcw", 96 + cidx), T(acc)[:, 0:N], MUL, ADD, [PK(pi), "cols", AK], [AK])
                        psfree(pi)
                        tfree(ex)
                        accs.append(acc)
                    ua, ub = accs
                    act(T(ua)[:, 0:N], T(ua)[:, 0:N], AF.Gelu_apprx_tanh, [TK(ua)], [TK(ua)])
                    tt(R1[:, j, 0:N], T(ua)[:, 0:N], T(ub)[:, 0:N], MUL, [TK(ua), TK(ub)], ["R%d" % j])
                    tfree(ua, ub)

                ffn_A(0)
                for j in range(24):
                    if j + 1 < 24:
                        ffn_A(j + 1)
                    ffn_B(j)
                    if j % 3 == 2:
                        ada_step()

                for o2 in range(4):
                    pis = [psalloc(), psalloc()]
                    for i in range(3):
                        wb, wk = wtile(l, "dn%d_%d" % (o2, i))
                        wv = wb[:, 0:2048].rearrange("p (k n) -> p k n", n=256)
                        for hh in range(2):
                            for k in range(8):
                                kk = 8 * i + k
                                mm(pss[pis[hh]][:, 0:N], wv[:, k, hh * 128:(hh + 1) * 128], R1[:, kk, 0:N], kk == 0, kk == 23,
                                   [wk, "R%d" % kk], [PK(pis[hh])])
                    for hh in range(2):
                        resid_add(l, 2 * o2 + hh, pis[hh], mod[:, l, 40 + 2 * o2 + hh, :])
                    psfree(*pis)

            rs = rmsnorm_rstd()
            for c in range(8):
                t1 = talloc()
                tt(T(t1)[:, 0:N], x[:, c, 0:N], T(rs)[:, 0:N], MUL, ["x%d" % c, TK(rs)], [TK(t1)])
                act(T(t1)[:, 0:N], T(t1)[:, 0:N], AF.Identity, [TK(t1), "cols"], [TK(t1)], scale=cols[:, 2 * NCL + c:2 * NCL + c + 1])
                P.dma(yT[c * 128:(c + 1) * 128, col0:col0 + N], T(t1)[:, 0:N], reads=[TK(t1)],
                      eng=("sp" if col0 == SEQ else "pool"))
                tfree(t1)
            tfree(rs)

        tiles = [(512 * i, 512, 1, 512, 0) for i in range(4)] + [(SEQ, NS * ST, NS, ST, 1)]
        for tile_ in tiles:
            run_tile(*tile_)

        for l in range(L):
            P.dma(o_rgh[l], rgh[:, l], reads=["rgh"])
            P.dma(o_rgc[l].rearrange("p c s j -> p (c s j)"), rgc[:, l].rearrange("p c s j -> p (c s j)"), reads=["rgc"])
            for ri in range(2):
                P.dma(o_s5[l, ri], s5s[:, l, ri], reads=["s5s"])
            P.dma(o_ffn[l].rearrange("p c s j -> p (c s j)"), ffc[:, l].rearrange("p c s j -> p (c s j)"), reads=["ffc"])
        P.finish()
    return nc


_NC = None


def kernel(**inp):
    global _NC
    inp = {k: np.asarray(v) for k, v in inp.items()}
    if _NC is None:
        _NC = build()
    cols = host_cols(inp)
    s5col, s5bp = host_s5(inp)
    wsts = [host_stream(inp, l) for l in range(L)]
    wada = host_ada(inp)
    in_maps = []
    for core in range(8):
        ss = slice(core * NS, (core + 1) * NS)
        xT = np.empty((D, NTOK), np.float32)
        xT[:, :SEQ] = inp["x_prompt"][core].T
        xT[:, SEQ:] = inp["x_sample"][ss].reshape(NS * ST, D).T
        cT = np.zeros((D, 18), np.float32)
        cT[:, 0] = inp["c_prompt"][core]
        cT[:, 1:17] = inp["c_sample"][ss].T
        st_rgh = inp["state_rg_h"][ss].reshape(NS, L, 8, 128).transpose(1, 3, 2, 0)
        st_rgc = inp["state_rg_conv"][ss].reshape(NS, L, 3, 8, 128).transpose(1, 4, 3, 0, 2)
        s5 = np.stack([inp["state_s5_re"][ss], inp["state_s5_im"][ss]], 0)
        s5 = s5.reshape(2, NS, L, 16, 2, 64).transpose(2, 0, 4, 5, 3, 1).reshape(L, 2, 128, 16, NS)
        st_ffn = inp["state_ffn_conv"][ss].reshape(NS, L, 2, 48, 128).transpose(1, 4, 3, 0, 2)
        m = {"xT": xT, "cT": cT, "cols": cols, "st_rgh": np.ascontiguousarray(st_rgh),
             "st_rgc": np.ascontiguousarray(st_rgc), "st_s5": np.ascontiguousarray(s5),
             "st_ffn": np.ascontiguousarray(st_ffn), "s5col": s5col, "s5bp": s5bp,
             "wst0": wsts[0], "wst1": wsts[1], "wada": wada}
        in_maps.append(m)
    res = run_bass_kernel_spmd(_NC, in_maps, core_ids=list(range(8)))
    R = res.results
    y_p = np.empty((8, SEQ, D), np.float32)
    y_s = np.empty((128, ST, D), np.float32)
    rg_h = np.empty((8 * 17, L, D), np.float32)
    rg_c = np.empty((8 * 17, L, 3, D), np.float32)
    s5r = np.empty((8 * 17, L, 32, 64), np.float32)
    s5i = np.empty((8 * 17, L, 32, 64), np.float32)
    ffn = np.empty((8 * 17, L, 2, 2 * DFF), np.float32)
    for core in range(8):
        r = R[core]
        yT = r["yT"]
        y_p[core] = yT[:, :SEQ].T
        y_s[core * NS:(core + 1) * NS] = yT[:, SEQ:].T.reshape(NS, ST, D)
        sl = slice(core * 17, (core + 1) * 17)
        rg_h[sl] = r["o_rgh"].transpose(3, 0, 2, 1).reshape(17, L, D)
        rg_c[sl] = r["o_rgc"].transpose(3, 0, 4, 2, 1).reshape(17, L, 3, D)
        o5 = r["o_s5"].reshape(L, 2, 2, 64, 16, 17).transpose(1, 5, 0, 4, 2, 3).reshape(2, 17, L, 32, 64)
        s5r[sl] = o5[0]
        s5i[sl] = o5[1]
        ffn[sl] = r["o_ffn"].transpose(3, 0, 4, 2, 1).reshape(17, L, 2, 2 * DFF)
    idx_p = np.arange(8) * 17
    idx_s = (np.arange(8)[:, None] * 17 + 1 + np.arange(16)[None, :]).reshape(-1)
    return (y_p, y_s, rg_h[idx_p], rg_c[idx_p], s5r[idx_p], s5i[idx_p], ffn[idx_p],
            rg_h[idx_s], rg_c[idx_s], s5r[idx_s], s5i[idx_s], ffn[idx_s])
```
